# Optimizing a Trainium2 kernel written in Bass

```python
import numpy as np
import jax, jax.numpy as jnp
from jax import lax

D_MODEL = 1024
BATCH = 8
SEQ = 2048
DEPTH = 4
DEC_BATCH = 128
DEC_SEQ = 1
PAST_LEN = 2048
PAGE_SIZE = 128

D_FF = 4 * D_MODEL
D_A = D_MODEL // 2
CONV_WIDTH = 31
D_B = D_MODEL // 2
B_GROUPS = 4
B_GROUP_DIM = D_B // B_GROUPS
CHUNK = 128
EVEN_IN = 2 * D_A + 2 * D_B
EVEN_OUT = D_A + D_B
N_HEADS = 16
HEAD_DIM = 64
N_KV = 4
GROUP = N_HEADS // N_KV
CMP_BLOCK = 32
CMP_STRIDE = 16
CMP_HIDDEN = 2 * HEAD_DIM
SEL_BLOCK = 64
N_SEL = 8
WINDOW = 512
Q_BLOCK = 64
ODD_IN = N_HEADS * HEAD_DIM + 6 * N_KV * HEAD_DIM + 3 * N_HEADS
ODD_OUT = N_HEADS * HEAD_DIM
ROPE_THETA = 10000.0
EPS = 1e-6
NEG = -1e30
FORCED = 1e4

kernel_name = "hybrid_conv_sgu_nsa_decoder_step"


def rmsnorm(x, g):
    xf = x.astype(jnp.float32)
    y = xf * lax.rsqrt(jnp.mean(xf * xf, -1, keepdims=True) + EPS)
    return (y * g.astype(jnp.float32)).astype(x.dtype)


def layernorm(x, g, b):
    xf = x.astype(jnp.float32)
    xc = xf - jnp.mean(xf, -1, keepdims=True)
    y = xc * lax.rsqrt(jnp.mean(xc * xc, -1, keepdims=True) + EPS)
    return (y * g.astype(jnp.float32) + b.astype(jnp.float32)).astype(x.dtype)


def rope(x, pos):
    half = HEAD_DIM // 2
    inv = jnp.power(jnp.float32(ROPE_THETA), -jnp.arange(half, dtype=jnp.float32) * (2.0 / HEAD_DIM))
    ang = pos.astype(jnp.float32)[:, None] * inv[None, :]
    cos = jnp.cos(ang)[None, :, None, :]
    sin = jnp.sin(ang)[None, :, None, :]
    xf = x.astype(jnp.float32)
    x1, x2 = xf[..., :half], xf[..., half:]
    return jnp.concatenate([x1 * cos - x2 * sin, x2 * cos + x1 * sin], -1).astype(x.dtype)


def masked_softmax(s, mask):
    s = jnp.where(mask, s.astype(jnp.float32), NEG)
    m = jnp.max(s, -1, keepdims=True)
    p = jnp.where(mask, jnp.exp(s - m), 0.0)
    return p / jnp.maximum(jnp.sum(p, -1, keepdims=True), 1e-30)


def ada_params(c, w, b):
    m = jax.nn.silu(c) @ w + b
    return [t[:, None, :] for t in jnp.split(m, 6, axis=-1)]


def modulate(x, g, shift, scale):
    return rmsnorm(x, g) * (1 + scale) + shift


def sq_relu_mlp(h, w1, w2):
    return jnp.square(jax.nn.relu(h @ w1)) @ w2


def causal_depthwise(a, buf, w, b):
    ap = jnp.concatenate([buf.astype(a.dtype), a], axis=1)
    out = lax.conv_general_dilated(ap, w[:, None, :], window_strides=(1,), padding='VALID',
                                   dimension_numbers=('NWC', 'WIO', 'NWC'),
                                   feature_group_count=a.shape[-1])
    return out + b, ap[:, -(CONV_WIDTH - 1):]


def chunk_spatial_gate(v, w_s, b_s):
    Bn, L, _ = v.shape
    nc = -(-L // CHUNK)
    vp = jnp.pad(v, ((0, 0), (0, nc * CHUNK - L), (0, 0))).reshape(Bn, nc, CHUNK, B_GROUPS, B_GROUP_DIM)
    causal = jnp.tril(jnp.ones((CHUNK, CHUNK), dtype=bool))
    wm = jnp.where(causal[None], w_s, jnp.zeros((), w_s.dtype))
    out = jnp.einsum('gts,bcsgd->bctgd', wm, vp) + b_s.T[None, None, :, :, None]
    return out.reshape(Bn, nc * CHUNK, D_B)[:, :L]


def even_mixer(h, conv_buf, w_in, w_out, conv_w, conv_b, conv_ln_g, conv_ln_b,
               sgu_ln_g, sgu_ln_b, sgu_w, sgu_b):
    z = h @ w_in
    a_lin, a_gate, b_u, b_v = jnp.split(z, [D_A, 2 * D_A, 2 * D_A + D_B], axis=-1)
    a = a_lin * jax.nn.sigmoid(a_gate)
    a_conv, new_buf = causal_depthwise(a, conv_buf, conv_w, conv_b)
    a_out = jax.nn.silu(layernorm(a_conv, conv_ln_g, conv_ln_b))
    u = jax.nn.gelu(b_u)
    v = layernorm(jax.nn.gelu(b_v), sgu_ln_g, sgu_ln_b)
    b_out = u * chunk_spatial_gate(v, sgu_w, sgu_b)
    y = jnp.concatenate([a_out, b_out], -1) @ w_out
    L = h.shape[1]
    cur = ((L - 1) // CHUNK) * CHUNK
    return y, new_buf, v[:, cur:]


def nsa_project(h, w_in, pos):
    z = h @ w_in
    kv = N_KV * HEAD_DIM
    q0 = N_HEADS * HEAD_DIM
    splits = [q0 + i * kv for i in range(7)]
    q, ck, cv, sk, sv, wk, wv, g = jnp.split(z, splits, axis=-1)
    Bn, L, _ = h.shape
    kvr = lambda t: t.reshape(Bn, L, N_KV, HEAD_DIM)
    q = q.reshape(Bn, L, N_HEADS, HEAD_DIM)
    q_rot = rope(q, pos)
    gates = jax.nn.sigmoid(g.reshape(Bn, L, N_HEADS, 3))
    return q, q_rot, kvr(ck), kvr(cv), rope(kvr(sk), pos), kvr(sv), rope(kvr(wk), pos), kvr(wv), gates


def compress(k, pe, w1, w2):
    Bn, L = k.shape[:2]
    r = CMP_BLOCK // CMP_STRIDE
    n_cmp = (L - CMP_BLOCK) // CMP_STRIDE + 1
    n_chunk = n_cmp + r - 1
    ch = k[:, :n_chunk * CMP_STRIDE].reshape(Bn, n_chunk, CMP_STRIDE, N_KV, HEAD_DIM)
    w1r = w1.reshape(r, CMP_STRIDE, HEAD_DIM, CMP_HIDDEN)
    hid = jnp.einsum('ld,lde->e', pe, w1)
    for m in range(r):
        hid = hid + jnp.einsum('bcjkd,jde->bcke', ch[:, m:m + n_cmp], w1r[m])
    return jax.nn.gelu(hid) @ w2


def cmp_last_pos(n_cmp):
    return jnp.arange(n_cmp) * CMP_STRIDE + (CMP_BLOCK - 1)


def to_sel_blocks(k):
    Bn, L = k.shape[:2]
    ns = -(-L // SEL_BLOCK)
    kp = jnp.pad(k, ((0, 0), (0, ns * SEL_BLOCK - L), (0, 0), (0, 0)))
    return kp.reshape(Bn, ns, SEL_BLOCK, N_KV, HEAD_DIM).transpose(0, 3, 1, 2, 4)


def overlap_matrix(n_cmp, n_sel):
    i = np.arange(n_cmp)[:, None] * CMP_STRIDE
    j = np.arange(n_sel)[None, :] * SEL_BLOCK
    return ((i < j + SEL_BLOCK) & (i + CMP_BLOCK > j)).astype(np.float32)


def nsa_attend(q, q_rot, q_pos, gates, k_cmp, v_cmp, cmp_last, sel_k, sel_v, k_win, v_win, win_pos):
    Bn, T = q.shape[:2]
    scale = HEAD_DIM ** -0.5
    qg = q.reshape(Bn, T, N_KV, GROUP, HEAD_DIM)
    qr = q_rot.reshape(Bn, T, N_KV, GROUP, HEAD_DIM)
    s = jnp.einsum('btkgd,bnkd->btkgn', qg, k_cmp) * scale
    cmask = (cmp_last[None, :] <= q_pos[:, None])[None, :, None, None, :]
    p_cmp = masked_softmax(s, cmask)
    o_cmp = jnp.einsum('btkgn,bnkd->btkgd', p_cmp.astype(v_cmp.dtype), v_cmp)
    n_cmp, n_sel = k_cmp.shape[1], sel_k.shape[2]
    ov = jnp.asarray(overlap_matrix(n_cmp, n_sel))
    imp = jnp.einsum('btkn,nj->btkj', p_cmp.sum(3), ov)
    blk = jnp.arange(n_sel)[None, :]
    cur = (q_pos // SEL_BLOCK)[:, None]
    valid = blk * SEL_BLOCK <= q_pos[:, None]
    forced = (blk == 0) | (blk == cur) | (blk == cur - 1)
    score = jnp.where(valid[None, :, None, :], jnp.where(forced[None, :, None, :], FORCED, imp), NEG)
    n_top = min(N_SEL, n_sel)
    _, top_idx = lax.top_k(score, n_top)
    bi = jnp.arange(Bn)[:, None, None, None]
    ki = jnp.arange(N_KV)[None, None, :, None]
    ks = sel_k[bi, ki, top_idx].reshape(Bn, T, N_KV, n_top * SEL_BLOCK, HEAD_DIM)
    vs = sel_v[bi, ki, top_idx].reshape(Bn, T, N_KV, n_top * SEL_BLOCK, HEAD_DIM)
    kpos = (top_idx[..., None] * SEL_BLOCK + jnp.arange(SEL_BLOCK)).reshape(Bn, T, N_KV, n_top * SEL_BLOCK)
    smask = (kpos <= q_pos[None, :, None, None])[:, :, :, None, :]
    s = jnp.einsum('btkgd,btksd->btkgs', qr, ks) * scale
    p = masked_softmax(s, smask)
    o_sel = jnp.einsum('btkgs,btksd->btkgd', p.astype(vs.dtype), vs)
    d = q_pos[:, None] - win_pos[None, :]
    wmask = ((d >= 0) & (d <= WINDOW) & (win_pos[None, :] >= 0))[None, :, None, None, :]
    s = jnp.einsum('btkgd,bwkd->btkgw', qr, k_win) * scale
    p = masked_softmax(s, wmask)
    o_win = jnp.einsum('btkgw,bwkd->btkgd', p.astype(v_win.dtype), v_win)
    g = gates.reshape(Bn, T, N_KV, GROUP, 3).astype(o_cmp.dtype)
    o = g[..., 0:1] * o_cmp + g[..., 1:2] * o_sel + g[..., 2:3] * o_win
    return o.reshape(Bn, T, N_HEADS * HEAD_DIM)


def nsa_prompt(h, w_in, w_out, pe_k, w1_k, w2_k, pe_v, w1_v, w2_v):
    Bn, L, _ = h.shape
    pos = jnp.arange(L)
    q, q_rot, ck, cv, sk, sv, wk, wv, gates = nsa_project(h, w_in, pos)
    k_cmp = compress(ck, pe_k, w1_k, w2_k)
    v_cmp = compress(cv, pe_v, w1_v, w2_v)
    cmp_last = cmp_last_pos(k_cmp.shape[1])
    sel_k, sel_v = to_sel_blocks(sk), to_sel_blocks(sv)
    wk_p = jnp.pad(wk, ((0, 0), (WINDOW, 0), (0, 0), (0, 0)))
    wv_p = jnp.pad(wv, ((0, 0), (WINDOW, 0), (0, 0), (0, 0)))
    nq = L // Q_BLOCK

    def blocks(t):
        return t.reshape(Bn, nq, Q_BLOCK, *t.shape[2:]).swapaxes(0, 1)

    def body(xs):
        qb, qrb, gb, start = xs
        qpos = start + jnp.arange(Q_BLOCK)
        kw = lax.dynamic_slice_in_dim(wk_p, start, WINDOW + Q_BLOCK, axis=1)
        vw = lax.dynamic_slice_in_dim(wv_p, start, WINDOW + Q_BLOCK, axis=1)
        wpos = start - WINDOW + jnp.arange(WINDOW + Q_BLOCK)
        return nsa_attend(qb, qrb, qpos, gb, k_cmp, v_cmp, cmp_last, sel_k, sel_v, kw, vw, wpos)

    o = lax.map(body, (blocks(q), blocks(q_rot), blocks(gates), jnp.arange(nq) * Q_BLOCK))
    o = o.swapaxes(0, 1).reshape(Bn, L, N_HEADS * HEAD_DIM)
    nw = min(WINDOW, L)
    return o @ w_out, (ck, cv, sk, sv, wk[:, L - nw:], wv[:, L - nw:])


def nsa_sample(h, cmp_k_pool, cmp_v_pool, sel_k_pool, sel_v_pool, win_k, win_v, page_table, layer,
               w_in, w_out, pe_k, w1_k, w2_k, pe_v, w1_v, w2_v):
    Bn, T, _ = h.shape
    past = page_table.shape[1] * PAGE_SIZE
    pos = past + jnp.arange(T)
    q, q_rot, ck, cv, sk, sv, wk, wv, gates = nsa_project(h, w_in, pos)

    def paged(pool, new):
        rows = pool[layer, page_table].reshape(Bn, past, N_KV, HEAD_DIM)
        return jnp.concatenate([rows.astype(new.dtype), new], axis=1)

    k_cmp = compress(paged(cmp_k_pool, ck), pe_k, w1_k, w2_k)
    v_cmp = compress(paged(cmp_v_pool, cv), pe_v, w1_v, w2_v)
    cmp_last = cmp_last_pos(k_cmp.shape[1])
    sel_k = to_sel_blocks(paged(sel_k_pool, sk))
    sel_v = to_sel_blocks(paged(sel_v_pool, sv))
    nbuf = win_k.shape[1]
    kw = jnp.concatenate([win_k.astype(wk.dtype), wk], axis=1)
    vw = jnp.concatenate([win_v.astype(wv.dtype), wv], axis=1)
    wpos = past - nbuf + jnp.arange(nbuf + T)
    o = nsa_attend(q, q_rot, pos, gates, k_cmp, v_cmp, cmp_last, sel_k, sel_v, kw, vw, wpos)
    nw = min(WINDOW, past + T)
    return o @ w_out, (ck, cv, sk, sv, kw[:, -nw:], vw[:, -nw:])


def setup_inputs(seed: int = 0) -> dict:
    key = jax.random.key(seed)
    ks = iter(jax.random.split(key, 48))
    n_even = (DEPTH + 1) // 2
    n_odd = DEPTH // 2
    n_pages = PAST_LEN // PAGE_SIZE
    n_used = DEC_BATCH * n_pages
    n_pool = n_used + n_used // 4
    win_buf = min(WINDOW, PAST_LEN)

    def nrm(shape, s):
        return jax.random.normal(next(ks), shape, jnp.float32) * s

    page_table = jax.random.permutation(next(ks), n_pool)[:n_used].reshape(DEC_BATCH, n_pages).astype(jnp.int32)
    pool_shape = (n_odd, n_pool, PAGE_SIZE, N_KV, HEAD_DIM)
    win_shape = (n_odd, DEC_BATCH, win_buf, N_KV, HEAD_DIM)
    return dict(
        x_prompt=nrm((BATCH, SEQ, D_MODEL), 1.0),
        x_sample=nrm((DEC_BATCH, DEC_SEQ, D_MODEL), 1.0),
        state_conv=nrm((n_even, DEC_BATCH, CONV_WIDTH - 1, D_A), 0.5),
        cache_cmp_k=nrm(pool_shape, 1.0),
        cache_cmp_v=nrm(pool_shape, 1.0),
        cache_sel_k=nrm(pool_shape, 1.0),
        cache_sel_v=nrm(pool_shape, 1.0),
        state_win_k=nrm(win_shape, 1.0),
        state_win_v=nrm(win_shape, 1.0),
        page_table=page_table,
        c_prompt=nrm((BATCH, D_MODEL), 1.0),
        c_sample=nrm((DEC_BATCH, D_MODEL), 1.0),
        ada_w=nrm((DEPTH, D_MODEL, 6 * D_MODEL), 0.5 * D_MODEL ** -0.5),
        ada_b=nrm((DEPTH, 6 * D_MODEL), 0.02),
        norm_mix_g=1.0 + nrm((DEPTH, D_MODEL), 0.02),
        norm_ffn_g=1.0 + nrm((DEPTH, D_MODEL), 0.02),
        ffn_w1=nrm((DEPTH, D_MODEL, D_FF), D_MODEL ** -0.5),
        ffn_w2=nrm((DEPTH, D_FF, D_MODEL), D_FF ** -0.5),
        even_w_in=nrm((n_even, D_MODEL, EVEN_IN), D_MODEL ** -0.5),
        even_w_out=nrm((n_even, EVEN_OUT, D_MODEL), EVEN_OUT ** -0.5),
        conv_w=nrm((n_even, CONV_WIDTH, D_A), CONV_WIDTH ** -0.5),
        conv_b=nrm((n_even, D_A), 0.02),
        conv_ln_g=1.0 + nrm((n_even, D_A), 0.02),
        conv_ln_b=nrm((n_even, D_A), 0.02),
        sgu_ln_g=1.0 + nrm((n_even, D_B), 0.02),
        sgu_ln_b=nrm((n_even, D_B), 0.02),
        sgu_w=nrm((n_even, B_GROUPS, CHUNK, CHUNK), CHUNK ** -0.5),
        sgu_b=1.0 + nrm((n_even, B_GROUPS, CHUNK), 0.02),
        odd_w_in=nrm((n_odd, D_MODEL, ODD_IN), D_MODEL ** -0.5),
        odd_w_out=nrm((n_odd, ODD_OUT, D_MODEL), ODD_OUT ** -0.5),
        cmp_pe_k=nrm((n_odd, CMP_BLOCK, HEAD_DIM), 0.1),
        cmp_w1_k=nrm((n_odd, CMP_BLOCK, HEAD_DIM, CMP_HIDDEN), (CMP_BLOCK * HEAD_DIM) ** -0.5),
        cmp_w2_k=nrm((n_odd, CMP_HIDDEN, HEAD_DIM), CMP_HIDDEN ** -0.5),
        cmp_pe_v=nrm((n_odd, CMP_BLOCK, HEAD_DIM), 0.1),
        cmp_w1_v=nrm((n_odd, CMP_BLOCK, HEAD_DIM, CMP_HIDDEN), (CMP_BLOCK * HEAD_DIM) ** -0.5),
        cmp_w2_v=nrm((n_odd, CMP_HIDDEN, HEAD_DIM), CMP_HIDDEN ** -0.5),
        final_norm_g=1.0 + nrm((D_MODEL,), 0.02),
    )


def reference(x_prompt, x_sample, state_conv, cache_cmp_k, cache_cmp_v, cache_sel_k, cache_sel_v,
              state_win_k, state_win_v, page_table, c_prompt, c_sample, ada_w, ada_b, norm_mix_g,
              norm_ffn_g, ffn_w1, ffn_w2, even_w_in, even_w_out, conv_w, conv_b, conv_ln_g, conv_ln_b,
              sgu_ln_g, sgu_ln_b, sgu_w, sgu_b, odd_w_in, odd_w_out, cmp_pe_k, cmp_w1_k, cmp_w2_k,
              cmp_pe_v, cmp_w1_v, cmp_w2_v, final_norm_g):
    xp, xs = x_prompt, x_sample
    conv_p, conv_s, chv_p, chv_s = [], [], [], []
    nsa_p = [[] for _ in range(6)]
    nsa_s = [[] for _ in range(6)]
    for l in range(DEPTH):
        mp = ada_params(c_prompt, ada_w[l], ada_b[l])
        ms = ada_params(c_sample, ada_w[l], ada_b[l])
        hp = modulate(xp, norm_mix_g[l], mp[0], mp[1])
        hs = modulate(xs, norm_mix_g[l], ms[0], ms[1])
        if l % 2 == 0:
            e = l // 2
            ew = (even_w_in[e], even_w_out[e], conv_w[e], conv_b[e], conv_ln_g[e], conv_ln_b[e],
                  sgu_ln_g[e], sgu_ln_b[e], sgu_w[e], sgu_b[e])
            zero_buf = jnp.zeros((hp.shape[0], CONV_WIDTH - 1, D_A), hp.dtype)
            yp, bp, vp = even_mixer(hp, zero_buf, *ew)
            ys, bs, vs = even_mixer(hs, state_conv[e], *ew)
            conv_p.append(bp); conv_s.append(bs); chv_p.append(vp); chv_s.append(vs)
        else:
            o = l // 2
            ow = (odd_w_in[o], odd_w_out[o], cmp_pe_k[o], cmp_w1_k[o], cmp_w2_k[o],
                  cmp_pe_v[o], cmp_w1_v[o], cmp_w2_v[o])
            yp, newp = nsa_prompt(hp, *ow)
            ys, news = nsa_sample(hs, cache_cmp_k, cache_cmp_v, cache_sel_k, cache_sel_v,
                                  state_win_k[o], state_win_v[o], page_table, o, *ow)
            for i in range(6):
                nsa_p[i].append(newp[i]); nsa_s[i].append(news[i])
        xp = xp + mp[2] * yp
        xs = xs + ms[2] * ys
        xp = xp + mp[5] * sq_relu_mlp(modulate(xp, norm_ffn_g[l], mp[3], mp[4]), ffn_w1[l], ffn_w2[l])
        xs = xs + ms[5] * sq_relu_mlp(modulate(xs, norm_ffn_g[l], ms[3], ms[4]), ffn_w1[l], ffn_w2[l])
    y_prompt = rmsnorm(xp, final_norm_g)
    y_sample = rmsnorm(xs, final_norm_g)
    st = lambda lst: jnp.stack(lst, axis=0)
    return (y_prompt, y_sample,
            st(conv_p), st(conv_s), st(chv_p), st(chv_s),
            st(nsa_p[0]), st(nsa_p[1]), st(nsa_p[2]), st(nsa_p[3]), st(nsa_p[4]), st(nsa_p[5]),
            st(nsa_s[0]), st(nsa_s[1]), st(nsa_s[2]), st(nsa_s[3]), st(nsa_s[4]), st(nsa_s[5]))
```

```python
import os
import numpy as np
from contextlib import ExitStack
import concourse.bass as bass
import concourse.mybir as mybir
from concourse.bass_utils import run_bass_kernel_spmd

F32 = mybir.dt.float32
BF16 = mybir.dt.bfloat16
I32 = mybir.dt.int32
AF = mybir.ActivationFunctionType
ALU = mybir.AluOpType
AX = mybir.AxisListType

NCORES = 8
D = 1024
SEQ = 2048
NS = 16
DEPTH = 4
EPS = 1e-6
TT = 512
NT = SEQ // TT
HC = SEQ + NS


class Sched:
    SAME_ENGINE_SYNC = True

    def __init__(self, nc, es, n_dma_slots=40):
        self.nc = nc
        self.engs = {'pe': nc.tensor, 'act': nc.scalar, 'dve': nc.vector, 'pool': nc.gpsimd, 'sp': nc.sync}
        self.sem = {}
        self.cnt = {}
        for k in self.engs:
            self.sem[k] = es.enter_context(nc.semaphore("s_" + k))
            self.cnt[k] = 0
        self.ndma = n_dma_slots
        for i in range(n_dma_slots):
            k = ('dma', i)
            self.sem[k] = es.enter_context(nc.semaphore("s_dma%d" % i))
            self.cnt[k] = 0
        self.dma_next = 0
        self.waited = {}
        self.res = {}
        self.n_wait = 0
        self.serial_compute = False
        self.epoch = 0
        self.fence_deps = {}
        self.key_epoch = {}
        self.n_ins = {k: 0 for k in self.engs}

    def _deps(self, reads, writes, e=None):
        deps = {}

        def add(kc):
            if kc is None:
                return
            k, c = kc
            if deps.get(k, 0) < c:
                deps[k] = c
        for r in reads:
            st = self.res.get(r)
            if st:
                add(st[0])
                if not isinstance(r, str) and r[0] == 'ps':
                    for k, c in st[1].items():
                        if k != e:
                            add((k, c))
        for w in writes:
            st = self.res.get(w)
            if st:
                add(st[0])
                for k, c in st[1].items():
                    add((k, c))
            name = w if isinstance(w, str) else w[0]
            if name not in self.PERSIST and self.key_epoch.get(w) != self.epoch:
                self.key_epoch[w] = self.epoch
                for k, c in self.fence_deps.items():
                    add((k, c))
        return deps

    PERSIST = {'x', 'xs', 'h', 'w', 'mod', 'gs', 'pl', 'pfin', 'scT', 'ident32', 'identb', 'trib', 'onesb',
               'eps', 's32_', 's16_', 'rstd_t', 'ps'}

    def fence(self):
        self.epoch += 1
        self.fence_deps = {k: c for k, c in self.cnt.items() if c > 0}

    def _emit_waits(self, e, deps):
        eng = self.engs[e]
        for k, c in deps.items():
            if k == e and (not self.SAME_ENGINE_SYNC or e == 'pe'):
                continue
            if self.waited.get((e, k), 0) >= c:
                continue
            eng.wait_ge(self.sem[k], c)
            self.n_wait += 1
            self.waited[(e, k)] = c

    def _record(self, k, c, reads, writes):
        for r in reads:
            st = self.res.setdefault(r, [None, {}])
            if st[1].get(k, 0) < c:
                st[1][k] = c
        for w in writes:
            self.res[w] = [(k, c), {}]

    SERIAL = set(os.environ.get("K_SERIAL", "").split(",")) - {""}

    def _all(self, deps):
        for k, c in self.cnt.items():
            if c > 0 and deps.get(k, 0) < c:
                deps[k] = c
        return deps

    def op(self, e, fn, reads=(), writes=(), inc=True):
        deps = self._deps(reads, writes, e)
        if e in self.SERIAL or "all" in self.SERIAL:
            deps = self._all(deps)
        if self.serial_compute == 2:
            deps = self._all(deps)
        elif self.serial_compute:
            for k in ('pe', 'act', 'dve'):
                c = self.cnt[k]
                if c > 0 and deps.get(k, 0) < c:
                    deps[k] = c
        self._emit_waits(e, deps)
        ins = fn()
        self.n_ins[e] += 1
        c = self.cnt[e] + 1
        if inc:
            ins.then_inc(self.sem[e], 1)
            self.cnt[e] = c
        self._record(e, c, reads, writes)
        return ins

    def dma(self, out, in_, reads=(), writes=(), q='sp', **kw):
        slot = self.dma_next
        self.dma_next = (self.dma_next + 1) % self.ndma
        k = ('dma', slot)
        deps = self._deps(reads, writes)
        if self.cnt[k] > 0:
            deps[k] = max(deps.get(k, 0), self.cnt[k])
        if "dma" in self.SERIAL or "all" in self.SERIAL or ("dma" + q) in self.SERIAL or self.serial_compute == 2:
            deps = self._all(deps)
        self._emit_waits(q, deps)
        ins = self.engs[q].dma_start(out=out, in_=in_, **kw)
        self.n_ins[q] += 1
        c = self.cnt[k] + 16
        ins.then_inc(self.sem[k], 16)
        self.cnt[k] = c
        self._record(k, c, reads, writes)
        return ins

    def dma_ind(self, out, in_, idx_ap, reads=(), writes=()):
        import concourse.bass as _b
        slot = self.dma_next
        self.dma_next = (self.dma_next + 1) % self.ndma
        k = ('dma', slot)
        deps = self._deps(reads, writes)
        if self.cnt[k] > 0:
            deps[k] = max(deps.get(k, 0), self.cnt[k])
        if self.serial_compute == 2 or "all" in self.SERIAL:
            deps = self._all(deps)
        self._emit_waits('pool', deps)
        ins = self.engs['pool'].indirect_dma_start(out=out, out_offset=None, in_=in_,
                                                   in_offset=_b.IndirectOffsetOnAxis(ap=idx_ap, axis=0))
        self.n_ins['pool'] += 1
        c = self.cnt[k] + 16
        ins.then_inc(self.sem[k], 16)
        self.cnt[k] = c
        self._record(k, c, reads, writes)
        return ins

    def finish(self):
        eng = self.engs['sp']
        for k, c in self.cnt.items():
            if c > 0 and k != 'sp' and self.waited.get(('sp', k), 0) < c:
                eng.wait_ge(self.sem[k], c)


class Ring:
    uid = 0

    def __init__(self, nc, es, name, n, shape, dt):
        Ring.uid += 1
        self.t = [es.enter_context(nc.sbuf_tensor("rg%d_%s%d" % (Ring.uid, name, i), shape, dt)) for i in range(n)]
        self.name = name
        self.i = 0

    def get(self):
        i = self.i
        self.i = (i + 1) % len(self.t)
        return self.t[i], (self.name, i)


def build_program(n_layers=DEPTH, do_odd=True):
    nc = bass.Bass("TRN2", target_bir_lowering=False)

    def din(name, shape, dt=F32):
        return nc.dram_tensor(name, list(shape), dt, kind="ExternalInput").ap()

    def dout(name, shape, dt=F32):
        return nc.dram_tensor(name, list(shape), dt, kind="ExternalOutput").ap()

    x_prompt = din("x_prompt", [SEQ, D])
    x_sample = din("x_sample", [NS, D])
    c_all = din("c_all", [1 + NS, D])
    state_conv = din("state_conv", [2, NS, 30, 512])
    ada_w = din("ada_w", [DEPTH, D, 6 * D])
    ffn_w1 = din("ffn_w1", [DEPTH, D, 4 * D])
    ffn_w2 = din("ffn_w2", [DEPTH, 4 * D, D])
    even_w_in = din("even_w_in", [2, D, 2048])
    even_w_out = din("even_w_out", [2, D, D])
    p_layer = din("p_layer", [DEPTH, 128, 64])
    p_even = din("p_even", [2, 128, 136])
    p_final = din("p_final", [128, 8])
    conv_w = din("conv_w", [2, 31, 512])
    conv_b = din("conv_b", [2, 512])
    conv_ln_g = din("conv_ln_g", [2, 512])
    conv_ln_b = din("conv_ln_b", [2, 512])
    sgu_ln_g = din("sgu_ln_g", [2, 512])
    sgu_ln_b = din("sgu_ln_b", [2, 512])
    sgu_wT = din("sgu_wT", [2, 128, 4, 128])
    sgu_b = din("sgu_b", [2, 4, 128])
    cst = din("cst", [128, 256])
    odd_w_in = din("odd_w_in", [2, D, 2608])
    odd_w_out = din("odd_w_out", [2, D, D])
    cmp_pe = [din("cmp_pe_k", [2, 32, 64]), din("cmp_pe_v", [2, 32, 64])]
    cmp_w1 = [din("cmp_w1_k", [2, 32, 64, 128]), din("cmp_w1_v", [2, 32, 64, 128])]
    cmp_w2 = [din("cmp_w2_k", [2, 128, 64]), din("cmp_w2_v", [2, 128, 64])]
    rope_cos = din("rope_cos", [SEQ + 1, 64])
    rope_sin = din("rope_sin", [SEQ + 1, 64])
    selA = din("selA", [SEQ, 32])
    selB = din("selB", [SEQ, 32])
    cmpb_d = din("cmpb", [127, SEQ])
    bandb_d = din("bandb", [128, 4, 256])
    Emat_d = din("Emat", [32, 16, 128])
    ov_d = din("ov", [127, 32])
    ov33_d = din("ov33", [127, 33])
    selA_s = din("selA_s", [1, 33])
    selB_s = din("selB_s", [1, 33])
    nb16_d = din("nb16", [16, 16])
    poff_d = din("poff", [128, 2])
    page_table = din("page_table", [NS, 16], I32)
    NPOOL = int(os.environ.get("K_POOLPAGES", 2560))
    caches = [din(n, [2 * NPOOL * 128, 256]) for n in ("cache_cmp_k", "cache_cmp_v", "cache_sel_k", "cache_sel_v")]
    state_win = [din(n, [2, NS, 512, 256]) for n in ("state_win_k", "state_win_v")]

    y_prompt = dout("y_prompt", [SEQ, D])
    y_sample = dout("y_sample", [NS, D])
    conv_p = dout("conv_p", [2, 30, 512])
    conv_s = dout("conv_s", [2, NS, 30, 512])
    chunkv_p = dout("chunkv_p", [2, 128, 512])
    chunkv_s = dout("chunkv_s", [2, NS, 512])
    nsa_p = [dout(n, [2, SEQ, 256]) for n in ("cmp_k_p", "cmp_v_p", "sel_k_p", "sel_v_p")]
    win_p = [dout(n, [2, 512, 256]) for n in ("win_k_p", "win_v_p")]
    nsa_s = [dout(n, [2, NS, 256]) for n in ("cmp_k_s", "cmp_v_s", "sel_k_s", "sel_v_s")]
    win_s = [dout(n, [2, NS, 512, 256]) for n in ("win_k_s", "win_v_s")]

    DBG = bool(os.environ.get("K_DBG"))
    with ExitStack() as es:
        S = Sched(nc, es)

        from contextlib import contextmanager

        @contextmanager
        def phase():
            S.fence()
            with ExitStack() as st_:
                yield st_
            S.fence()

        uid = [0]

        def sb(name, shape, dt, stack=es):
            uid[0] += 1
            return stack.enter_context(nc.sbuf_tensor("sb%d_%s" % (uid[0], name), list(shape), dt))

        xT = sb("xT", [128, 8, SEQ], F32)
        xsT = sb("xsT", [128, 8, NS], F32)
        hT = sb("hT", [128, 8, HC], BF16)
        NSLOT = 4
        wring = [sb("wr%d" % i, [128, 4096], BF16) for i in range(NSLOT)]
        mod = [sb("mod%d" % i, [128, 48, 17], F32) for i in range(2)]
        gsb = [sb("gs%d" % i, [128, 8, 17], F32) for i in range(2)]
        pl = sb("pl", [128, DEPTH, 64], F32)
        pfin = sb("pfin", [128, 8], F32)
        scT = sb("scT", [128, 8, 17], BF16)
        ident32 = sb("ident32", [128, 128], F32)
        identb = sb("identb", [128, 128], BF16)
        trib = sb("trib", [128, 128], BF16)
        onesb = sb("onesb", [128, 128], BF16)
        eps_t = sb("eps_t", [128, 1], F32)
        s32 = Ring(nc, es, "s32_", 6, [128, 512], F32)
        rstd_t = sb("rstd_t", [128, 512], F32)
        s16 = Ring(nc, es, "s16_", 3, [128, 512], BF16)
        ps_t = [es.enter_context(nc.psum_tensor("ps%d" % i, [128, 512], F32)) for i in range(8)]
        ps_i = [0]

        def PS():
            i = ps_i[0]
            ps_i[0] = (i + 1) % 8
            return ps_t[i], ('ps', i)

        V, A, G, T = nc.vector, nc.scalar, nc.gpsimd, nc.tensor

        def mm(out, lhsT, rhs, start, stop, reads, writes, inc=None, **kw):
            return S.op('pe', lambda: T.matmul(out, lhsT=lhsT, rhs=rhs, start=start, stop=stop, **kw),
                        reads=reads, writes=writes, inc=(stop if inc is None else inc))

        def tr(out, in_, ident, reads, writes, inc=True):
            return S.op('pe', lambda: T.transpose(out, in_, ident), reads=reads, writes=writes, inc=inc)

        wplan = []
        wstate = {'issued': 0, 'used': 0}

        def w_parts(entry):
            return entry if isinstance(entry, list) else [(0, 128, entry)]

        def w_view(i, p0=0, p1=128):
            parts = w_parts(wplan[i])
            a, b = parts[0][2].shape[1], parts[0][2].shape[2]
            slot = i % NSLOT
            return wring[slot][p0:p1, 0:a * b].rearrange("p (a b) -> p a b", a=a)

        def w_issue_upto(n):
            while wstate['issued'] < min(n, len(wplan)):
                i = wstate['issued']
                slot = i % NSLOT
                for pi, (p0, p1, src) in enumerate(w_parts(wplan[i])):
                    S.dma(w_view(i, p0, p1), src, writes=[('w', slot, pi), ('w', slot, 'r')], q='pool')
                wstate['issued'] += 1

        def w_next():
            i = wstate['used']
            wstate['used'] += 1
            w_issue_upto(i + NSLOT - 1)
            slot = i % NSLOT
            keys = [('w', slot, 'r')] + [('w', slot, pi) for pi in range(len(w_parts(wplan[i])))]
            return w_view(i), keys

        def wview(W2d):
            return W2d.rearrange("(kc p) n -> p kc n", p=128)

        def plan_ada(l):
            v = wview(ada_w[l])
            return [v[:, :, b * 512:(b + 1) * 512] for b in range(12)]

        def plan_even(e):
            vi = wview(even_w_in[e])
            vo = wview(even_w_out[e])
            out = []
            for tt in range(NT + 1):
                out += [vi[:, :, b * 512:(b + 1) * 512] for b in range(4)]
                out += [vo[:, 0:4, :], vo[:, 4:8, :]]
            return out

        def plan_mlp(l):
            v1 = wview(ffn_w1[l])
            v2 = wview(ffn_w2[l])
            out = []
            nada = 0
            for j in range(8):
                out += [v1[:, :, j * 512:(j + 1) * 512], v2[:, 4 * j:4 * j + 4, :]]
                if l + 1 < n_layers:
                    tgt = (12 * (j + 1)) // 8
                    pa = plan_ada(l + 1)
                    out += pa[nada:tgt]
                    nada = tgt
            return out

        TQ = 256
        NQ = SEQ // TQ
        OST = int(os.environ.get('K_OST', 6))
        SST = int(os.environ.get('K_SST', 5))
        P1 = int(os.environ.get('K_P1', 4))
        SBLK = [(0, 512), (512, 512), (1024, 512), (1536, 512), (2048, 512), (2560, 48)]

        def plan_odd(o):
            vi = wview(odd_w_in[o])
            out = []
            for tt in range(NT if OST >= 1 else 0):
                out += [vi[:, :, 1024 + b * 512:1024 + (b + 1) * 512] for b in range(3 if OST != 1 else int(os.environ.get('K_NBLK', 3)))]
            for qt in range(NQ if OST >= 3 else 0):
                out += [vi[:, :, 0:512], vi[:, :, 512:1024], vi[:, :, 2560:2608]]
                for b in range(2 if OST >= 5 else 0):
                    parts = []
                    for half in range(2):
                        r0 = (8 * b + 4 * half) * 64
                        parts.append((half * 64, half * 64 + 64, odd_w_out[o, r0:r0 + 256, :].rearrange("(m d) n -> d m n", d=64)))
                    out.append(parts)
            if OST >= 6:
                for (c0, n) in SBLK:
                    out.append(vi[:, :, c0:c0 + n])
                for b in range(2 if SST >= 5 else 0):
                    parts = []
                    for half in range(2):
                        r0 = (8 * b + 4 * half) * 64
                        parts.append((half * 64, half * 64 + 64, odd_w_out[o, r0:r0 + 256, :].rearrange("(m d) n -> d m n", d=64)))
                    out.append(parts)
            return out

        wplan += plan_ada(0)
        for l in range(n_layers):
            if l % 2 == 0:
                wplan += plan_even(l // 2)
            elif do_odd:
                wplan += plan_odd(l // 2)
            wplan += plan_mlp(l)

        S.dma(ident32[:], cst[:, 0:128], writes=['ident32'])
        S.dma(identb[:], cst[:, 0:128], writes=['identb'], q='pool')
        S.dma(trib[:], cst[:, 128:256], writes=['trib'], q='pool')
        S.dma(pl[:], p_layer.rearrange("l p c -> p l c"), writes=['pl'])
        S.dma(pfin[:], p_final, writes=['pfin'])
        S.op('dve', lambda: V.memset(onesb[:], 1.0), writes=['onesb'])
        S.op('dve', lambda: V.memset(eps_t[:], EPS), writes=['eps'])
        w_issue_upto(NSLOT - 1)

        with phase() as ph:
            tok = [sb("tok%d" % i, [128, D], F32, ph) for i in range(2)]
            for tt in range(SEQ // 128):
                tk = tok[tt % 2]
                key = ('tok', tt % 2)
                S.dma(tk[:], x_prompt[tt * 128:(tt + 1) * 128, :], writes=[key])
                for half in range(2):
                    ps, pk = PS()
                    for q in range(4):
                        c = half * 4 + q
                        tr(ps[:, q * 128:(q + 1) * 128], tk[:, c * 128:(c + 1) * 128], ident32[:],
                           reads=[key, 'ident32'], writes=[pk], inc=(q == 3))
                    dst = xT[:, half * 4:half * 4 + 4, tt * 128:(tt + 1) * 128]
                    src = ps[:].rearrange("p (a b) -> p a b", a=4)
                    if half == 0:
                        S.op('act', lambda: A.copy(out=dst, in_=src), reads=[pk], writes=[('x', tt // 4, half)])
                    else:
                        S.op('dve', lambda: V.tensor_copy(out=dst, in_=src), reads=[pk], writes=[('x', tt // 4, half)])
            tks = sb("toks", [NS, D], F32, ph)
            S.dma(tks[:], x_sample, writes=['toks'])
            ps, pk = PS()
            for c in range(8):
                tr(ps[:, c * NS:(c + 1) * NS], tks[:, c * 128:(c + 1) * 128], ident32[0:NS, 0:NS],
                   reads=['toks', 'ident32'], writes=[pk], inc=(c == 7))
            S.op('dve', lambda: V.tensor_copy(out=xsT[:], in_=ps[:, 0:8 * NS].rearrange("p (a b) -> p a b", a=8)),
                 reads=[pk], writes=['xs'])
            ctk = sb("ctk", [17, D], F32, ph)
            S.dma(ctk[:], c_all, writes=['ctk'])
            S.op('act', lambda: A.activation(out=ctk[:], in_=ctk[:], func=AF.Silu), reads=['ctk'], writes=['ctk'])
            ps, pk = PS()
            for c in range(8):
                tr(ps[:, c * 17:(c + 1) * 17], ctk[:, c * 128:(c + 1) * 128], ident32[0:17, 0:17],
                   reads=['ctk', 'ident32'], writes=[pk], inc=(c == 7))
            S.op('dve', lambda: V.tensor_copy(out=scT[:], in_=ps[:, 0:8 * 17].rearrange("p (a b) -> p a b", a=8)),
                 reads=[pk], writes=['scT'])
            xkeys = [[('x', t, 0), ('x', t, 1)] for t in range(NT)]

            def ada_block(l, b):
                wt, wk = w_next()
                m = mod[l % 2]
                ps, pk = PS()
                for oc in range(4):
                    for k in range(8):
                        mm(ps[:, oc * 17:(oc + 1) * 17], wt[:, k, oc * 128:(oc + 1) * 128], scT[:, k, :],
                           k == 0, k == 7, reads=wk + ['scT'], writes=[pk])
                bias = pl[:, l, 4 * b:4 * b + 4].unsqueeze(2).to_broadcast([128, 4, 17])
                S.op('dve', lambda: V.tensor_tensor(out=m[:, 4 * b:4 * b + 4, :],
                                                     in0=ps[:, 0:68].rearrange("p (a b) -> p a b", a=4),
                                                     in1=bias, op=ALU.add),
                     reads=[pk, 'pl'], writes=[('mod', l % 2)])

            def ada_finish(l):
                m = mod[l % 2]
                for which in range(2):
                    sc = m[:, (1 + 3 * which) * 8:(2 + 3 * which) * 8, :]
                    g = pl[:, l, 48 + 8 * which:56 + 8 * which].unsqueeze(2).to_broadcast([128, 8, 17])
                    S.op('dve', lambda: V.scalar_tensor_tensor(out=gsb[which][:], in0=sc, scalar=1.0, in1=g,
                                                                op0=ALU.add, op1=ALU.mult),
                         reads=[('mod', l % 2), 'pl'], writes=[('gs', which)])

            for b in range(12):
                ada_block(0, b)

        def norm_tile(l, which, tt, dst, dkeys):
            m = mod[l % 2]
            gs = gsb[which]
            sh = (3 * which) * 8
            cols = slice(tt * TT, (tt + 1) * TT)
            ps, pk = PS()
            for c in range(8):
                sq, sqk = s16.get()
                S.op('act', lambda: A.activation(out=sq[:], in_=xT[:, c, cols], func=AF.Square),
                     reads=xkeys[tt], writes=[sqk])
                mm(ps[:], onesb[:], sq[:], c == 0, c == 7, reads=[sqk, 'onesb'], writes=[pk], inc=True)
            rs, rsk = rstd_t, 'rstd_t'
            S.op('act', lambda: A.activation(out=rs[:], in_=ps[:], func=AF.Sqrt, bias=eps_t[:, 0:1], scale=1.0 / D),
                 reads=[pk, 'eps'], writes=[rsk])
            S.op('dve', lambda: V.reciprocal(out=rs[:], in_=rs[:]), reads=[rsk], writes=[rsk])
            for c in range(8):
                t, tk = s32.get()
                S.op('dve', lambda: V.tensor_tensor(out=t[:], in0=xT[:, c, cols], in1=rs[:], op=ALU.mult),
                     reads=xkeys[tt] + [rsk], writes=[tk])
                S.op('act', lambda: A.activation(out=dst[:, c, :], in_=t[:], func=AF.Identity,
                                                  scale=gs[:, c, 0:1], bias=m[:, sh + c, 0:1]),
                     reads=[tk, ('gs', which), ('mod', l % 2)], writes=[dkeys[c]])

        def norm_samples(l, which):
            m = mod[l % 2]
            gs = gsb[which]
            sh = (3 * which) * 8
            ps, pk = PS()
            sq, sqk = s16.get()
            S.op('act', lambda: A.activation(out=sq[:, 0:8 * NS], in_=xsT[:].rearrange("p a b -> p (a b)"), func=AF.Square),
                 reads=['xs'], writes=[sqk])
            for c in range(8):
                mm(ps[:, 0:NS], onesb[:], sq[:, c * NS:(c + 1) * NS], c == 0, c == 7, reads=[sqk, 'onesb'], writes=[pk])
            rs, rsk = s32.get()
            S.op('act', lambda: A.activation(out=rs[:, 0:NS], in_=ps[:, 0:NS], func=AF.Sqrt, bias=eps_t[:, 0:1], scale=1.0 / D),
                 reads=[pk, 'eps'], writes=[rsk])
            S.op('dve', lambda: V.reciprocal(out=rs[:, 0:NS], in_=rs[:, 0:NS]), reads=[rsk], writes=[rsk])
            t, tk = s32.get()
            tv = t[:, 0:8 * NS].rearrange("p (a b) -> p a b", a=8)
            S.op('dve', lambda: V.tensor_tensor(out=tv, in0=xsT[:], in1=rs[:, 0:NS].unsqueeze(1).to_broadcast([128, 8, NS]), op=ALU.mult),
                 reads=['xs', rsk], writes=[tk])
            S.op('dve', lambda: V.tensor_tensor(out=tv, in0=tv, in1=gs[:, :, 1:17], op=ALU.mult),
                 reads=[tk, ('gs', which)], writes=[tk])
            S.op('dve', lambda: V.tensor_tensor(out=hT[:, :, SEQ:HC], in0=tv, in1=m[:, sh:sh + 8, 1:17], op=ALU.add),
                 reads=[tk, ('mod', l % 2)], writes=[('h', 's')])

        def norm_pass(l, which):
            for tt in range(NT):
                norm_tile(l, which, tt, hT[:, :, tt * TT:(tt + 1) * TT], [('h', tt, c) for c in range(8)])
            norm_samples(l, which)

        hkeys = [[('h', t, c) for c in range(8)] for t in range(NT)]

        def mlp(l):
            m = mod[l % 2]
            g2 = 5 * 8
            nada = 0
            with phase() as ph:
                hid = [sb("hid%d" % i, [128, 4, TT], BF16, ph) for i in range(2)]
                hi = 0
                for j in range(8):
                    w1, w1k = w_next()
                    w2, w2k = w_next()
                    for tt in range(NT + 1):
                        samp = tt == NT
                        n = NS if samp else TT
                        cols = slice(SEQ, HC) if samp else slice(tt * TT, (tt + 1) * TT)
                        hk = [('h', 's')] if samp else hkeys[tt]
                        hd = hid[hi % 2]
                        hdk = ('hid', hi % 2)
                        hi += 1
                        for jj in range(4):
                            ps, pk = PS()
                            for k in range(8):
                                mm(ps[:, 0:n], w1[:, k, jj * 128:(jj + 1) * 128], hT[:, k, cols], k == 0, k == 7,
                                   reads=w1k + hk, writes=[pk])
                            r, rk = s32.get()
                            S.op('act', lambda: A.activation(out=r[:, 0:n], in_=ps[:, 0:n], func=AF.Relu), reads=[pk], writes=[rk])
                            S.op('dve', lambda: V.tensor_tensor(out=hd[:, jj, 0:n], in0=r[:, 0:n], in1=r[:, 0:n], op=ALU.mult),
                                 reads=[rk], writes=[hdk])
                        if samp:
                            ps, pk = PS()
                            for i in range(8):
                                for jj in range(4):
                                    mm(ps[:, i * NS:(i + 1) * NS], w2[:, jj, i * 128:(i + 1) * 128], hd[:, jj, 0:NS], jj == 0, jj == 3,
                                       reads=w2k + [hdk], writes=[pk])
                            t, tk = s32.get()
                            tv = t[:, 0:8 * NS].rearrange("p (a b) -> p a b", a=8)
                            S.op('dve', lambda: V.tensor_tensor(out=tv, in0=ps[:, 0:8 * NS].rearrange("p (a b) -> p a b", a=8),
                                                                 in1=m[:, g2:g2 + 8, 1:17], op=ALU.mult),
                                 reads=[pk, ('mod', l % 2)], writes=[tk])
                            S.op('dve', lambda: V.tensor_tensor(out=xsT[:], in0=xsT[:], in1=tv, op=ALU.add),
                                 reads=[tk, 'xs'], writes=['xs'])
                        else:
                            for i in range(8):
                                ps, pk = PS()
                                for jj in range(4):
                                    mm(ps[:], w2[:, jj, i * 128:(i + 1) * 128], hd[:, jj, :], jj == 0, jj == 3,
                                       reads=w2k + [hdk], writes=[pk])
                                S.op('dve', lambda: V.scalar_tensor_tensor(out=xT[:, i, cols], in0=ps[:], scalar=m[:, g2 + i, 0:1],
                                                                            in1=xT[:, i, cols], op0=ALU.mult, op1=ALU.add),
                                     reads=[pk, ('mod', l % 2)] + xkeys[tt], writes=[xkeys[tt][i // 4]])
                    if l + 1 < n_layers:
                        tgt = (12 * (j + 1)) // 8
                        while nada < tgt:
                            ada_block(l + 1, nada)
                            nada += 1

        def even_layer(l):
            e = l // 2
            m = mod[l % 2]
            g1 = 2 * 8
            with phase() as ph:
                pe = sb("pe", [128, 136], F32, ph)
                S.dma(pe[:], p_even[e], writes=['pe'])
                gbc = sb("gbc", [128, 512], F32, ph)
                bbc = sb("bbc", [128, 512], F32, ph)
                S.dma(gbc[:], sgu_ln_g[e:e + 1, :].to_broadcast([128, 512]), writes=['gbc'])
                S.dma(bbc[:], sgu_ln_b[e:e + 1, :].to_broadcast([128, 512]), writes=['bbc'])
                wmT = sb("wmT", [128, 4, 128], BF16, ph)
                S.dma(wmT[:], sgu_wT[e], writes=['wmT'], q='pool')
                S.op('dve', lambda: V.tensor_tensor(out=wmT[:], in0=wmT[:], in1=trib[:].unsqueeze(1).to_broadcast([128, 4, 128]), op=ALU.mult),
                     reads=['wmT', 'trib'], writes=['wmT'])
                bsb = sb("bsb", [1, 4, 128], BF16, ph)
                S.dma(bsb[:], sgu_b[e:e + 1], writes=['bsb'], q='pool')
                stat = sb("stat", [128, 8], F32, ph)

                with phase() as pa:
                    cg = sb("cg", [NS, 512], F32, pa)
                    cb = sb("cb", [NS, 512], F32, pa)
                    w30 = sb("w30", [NS, 512], F32, pa)
                    cpre = sb("cpre", [NS, 512], F32, pa)
                    sw0 = sb("sw0", [NS, 4], F32, pa)
                    sb0 = sb("sb0", [NS, 4], F32, pa)
                    S.dma(cg[:], conv_ln_g[e:e + 1, :].to_broadcast([NS, 512]), writes=['cg'])
                    S.dma(cb[:], conv_ln_b[e:e + 1, :].to_broadcast([NS, 512]), writes=['cb'])
                    S.dma(w30[:], conv_w[e, 30:31, :].to_broadcast([NS, 512]), writes=['w30'])
                    S.dma(cpre[:], conv_b[e:e + 1, :].to_broadcast([NS, 512]), writes=['cpre'])
                    S.dma(sw0[:], sgu_wT[e, 0:1, :, 0].to_broadcast([NS, 4]), writes=['sw0'], allow_slow_non_contiguous=True)
                    S.dma(sb0[:], sgu_b[e, :, 0].unsqueeze(0).to_broadcast([NS, 4]), writes=['sb0'], allow_slow_non_contiguous=True)
                    zl = sb("zl", [NS, 512], F32, pa)
                    a_s = sb("a_s", [NS, 512], F32, pa)
                    cv = sb("cv", [NS, 512], F32, pa)
                    us = sb("us", [NS, 512], F32, pa)
                    vs = sb("vs", [NS, 512], F32, pa)
                    abs_ = sb("abs", [NS, 1024], F32, pa)
                    abT = sb("abT", [128, 8, NS], BF16, pa)
                    st = sb("st", [NS, 30, 64], F32, pa)
                    wb_ = sb("wbc", [NS, 30, 64], F32, pa)
                    red = sb("red", [NS, 64], F32, pa)
                    for oc in range(8):
                        cs = slice(oc * 64, (oc + 1) * 64)
                        S.dma(st[:], state_conv[e, :, :, cs], writes=['st'])
                        S.dma(wb_[:], conv_w[e:e + 1, 0:30, cs].to_broadcast([NS, 30, 64]), writes=['wbc'])
                        S.op('dve', lambda: V.tensor_tensor(out=st[:], in0=st[:], in1=wb_[:], op=ALU.mult), reads=['st', 'wbc'], writes=['st'])
                        S.op('dve', lambda: V.tensor_reduce(out=red[:], in_=st[:].rearrange("p k c -> p c k"), axis=AX.X, op=ALU.add),
                             reads=['st'], writes=['red'])
                        S.op('dve', lambda: V.tensor_tensor(out=cpre[:, cs], in0=cpre[:, cs], in1=red[:], op=ALU.add),
                             reads=['red', 'cpre'], writes=['cpre'])
                    S.dma(conv_s[e, :, 0:29, :], state_conv[e, :, 1:30, :])

                    def zproj():
                        Wb, Wbk = w_next()
                        ps, pk = PS()
                        for k in range(8):
                            mm(ps[0:NS, :], hT[:, k, SEQ:HC], Wb[:, k, :], k == 0, k == 7, reads=Wbk + [('h', 's')], writes=[pk])
                        return ps, pk
                    ps, pk = zproj()
                    S.op('act', lambda: A.copy(out=zl[:], in_=ps[0:NS, :]), reads=[pk], writes=['zl'])
                    ps, pk = zproj()
                    S.op('act', lambda: A.activation(out=a_s[:], in_=ps[0:NS, :], func=AF.Sigmoid), reads=[pk], writes=['a_s'])
                    S.op('dve', lambda: V.tensor_tensor(out=a_s[:], in0=zl[:], in1=a_s[:], op=ALU.mult), reads=['zl', 'a_s'], writes=['a_s'])
                    S.dma(conv_s[e, :, 29, :], a_s[:], reads=['a_s'])
                    S.op('dve', lambda: V.tensor_tensor(out=cv[:], in0=a_s[:], in1=w30[:], op=ALU.mult), reads=['a_s', 'w30'], writes=['cv'])
                    S.op('dve', lambda: V.tensor_tensor(out=cv[:], in0=cv[:], in1=cpre[:], op=ALU.add), reads=['cv', 'cpre'], writes=['cv'])

                    def ln_tok(t, tk, gsrc, bsrc, gkey, bkey):
                        S.op('dve', lambda: V.bn_stats(out=stat[0:NS, 0:6], in_=t[:]), reads=[tk], writes=['stat'])
                        S.op('dve', lambda: V.bn_aggr(out=stat[0:NS, 6:8], in_=stat[0:NS, 0:6]), reads=['stat'], writes=['stat'])
                        S.op('act', lambda: A.activation(out=stat[0:NS, 7:8], in_=stat[0:NS, 7:8], func=AF.Sqrt, bias=eps_t[0:NS, 0:1], scale=1.0),
                             reads=['stat', 'eps'], writes=['stat'])
                        S.op('dve', lambda: V.reciprocal(out=stat[0:NS, 7:8], in_=stat[0:NS, 7:8]), reads=['stat'], writes=['stat'])
                        S.op('dve', lambda: V.tensor_scalar(out=t[:], in0=t[:], scalar1=stat[0:NS, 6:7], scalar2=stat[0:NS, 7:8],
                                                             op0=ALU.subtract, op1=ALU.mult), reads=[tk, 'stat'], writes=[tk])
                        S.op('dve', lambda: V.tensor_tensor(out=t[:], in0=t[:], in1=gsrc, op=ALU.mult), reads=[tk, gkey], writes=[tk])
                        S.op('dve', lambda: V.tensor_tensor(out=t[:], in0=t[:], in1=bsrc, op=ALU.add), reads=[tk, bkey], writes=[tk])

                    ln_tok(cv, 'cv', cg[:], cb[:], 'cg', 'cb')
                    S.op('act', lambda: A.activation(out=abs_[:, 0:512], in_=cv[:], func=AF.Silu), reads=['cv'], writes=['abs'])
                    ps, pk = zproj()
                    S.op('act', lambda: A.activation(out=us[:], in_=ps[0:NS, :], func=AF.Gelu_apprx_tanh), reads=[pk], writes=['us'])
                    ps, pk = zproj()
                    S.op('act', lambda: A.activation(out=vs[:], in_=ps[0:NS, :], func=AF.Gelu_apprx_tanh), reads=[pk], writes=['vs'])
                    ln_tok(vs, 'vs', gbc[0:NS, :], bbc[0:NS, :], 'gbc', 'bbc')
                    S.dma(chunkv_s[e], vs[:], reads=['vs'])
                    vs3 = vs[:].rearrange("p (g d) -> p g d", g=4)
                    S.op('dve', lambda: V.tensor_tensor(out=vs3, in0=vs3, in1=sw0[:].unsqueeze(2).to_broadcast([NS, 4, 128]), op=ALU.mult),
                         reads=['vs', 'sw0'], writes=['vs'])
                    S.op('dve', lambda: V.tensor_tensor(out=vs3, in0=vs3, in1=sb0[:].unsqueeze(2).to_broadcast([NS, 4, 128]), op=ALU.add),
                         reads=['vs', 'sb0'], writes=['vs'])
                    S.op('dve', lambda: V.tensor_tensor(out=abs_[:, 512:1024], in0=us[:], in1=vs[:], op=ALU.mult),
                         reads=['us', 'vs', 'abs'], writes=['abs'])
                    ps, pk = PS()
                    for c in range(8):
                        tr(ps[:, c * NS:(c + 1) * NS], abs_[:, c * 128:(c + 1) * 128], ident32[0:NS, 0:NS],
                           reads=['abs', 'ident32'], writes=[pk], inc=(c == 7))
                    S.op('dve', lambda: V.tensor_copy(out=abT[:], in_=ps[:, 0:8 * NS].rearrange("p (a b) -> p a b", a=8)), reads=[pk], writes=['abT'])
                    Wo1, Wo1k = w_next()
                    Wo2, Wo2k = w_next()
                    ps, pk = PS()
                    for i in range(8):
                        for k in range(8):
                            Wo, Wok = (Wo1, Wo1k) if k < 4 else (Wo2, Wo2k)
                            mm(ps[:, i * NS:(i + 1) * NS], Wo[:, k % 4, i * 128:(i + 1) * 128], abT[:, k, :], k == 0, k == 7,
                               reads=Wok + ['abT'], writes=[pk])
                    t, tk = s32.get()
                    tv = t[:, 0:8 * NS].rearrange("p (a b) -> p a b", a=8)
                    S.op('dve', lambda: V.tensor_tensor(out=tv, in0=ps[:, 0:8 * NS].rearrange("p (a b) -> p a b", a=8),
                                                         in1=m[:, g1:g1 + 8, 1:17], op=ALU.mult), reads=[pk, ('mod', l % 2)], writes=[tk])
                    S.op('dve', lambda: V.tensor_tensor(out=xsT[:], in0=xsT[:], in1=tv, op=ALU.add), reads=[tk, 'xs'], writes=['xs'])

                diag = [sb("diag%d" % i, [128, 31, 128], BF16, ph) for i in range(2)]
                apad1 = sb("apad", [128, 4, 30 + TT], BF16, ph)
                apad = [apad1, apad1]
                mean = sb("mean_t", [128, 512], F32, ph)
                msq = sb("msq_t", [128, 512], F32, ph)
                mk, msk = 'mean_t', 'msq_t'
                u_t = sb("u_t", [128, 4, TT], BF16, ph)
                bo_t = sb("bo_t", [128, 4, TT], BF16, ph)
                ao_t = sb("ao_t", [128, 4, TT], BF16, ph)
                acv = sb("acv", [128, 4, TT], BF16, ph)
                vb = sb("vb", [128, 512], BF16, ph)
                a32 = sb("a32", [128, 4, 32], F32, ph)
                S.op('dve', lambda: V.memset(apad1[:, :, 0:30], 0.0), writes=['apad'])
                ndiag = [0]

                for tt in range(NT):
                    t0 = tt * TT
                    cols = slice(t0, t0 + TT)
                    hk = hkeys[tt]
                    ap_t = apad[tt % 2]
                    apk = 'apad'
                    Wa, Wak = w_next()
                    Wg, Wgk = w_next()
                    for oc in range(4):
                        psl, plk = PS()
                        for k in range(8):
                            mm(psl[:], Wa[:, k, oc * 128:(oc + 1) * 128], hT[:, k, cols], k == 0, k == 7, reads=Wak + hk, writes=[plk])
                        psg, pgk = PS()
                        for k in range(8):
                            mm(psg[:], Wg[:, k, oc * 128:(oc + 1) * 128], hT[:, k, cols], k == 0, k == 7, reads=Wgk + hk, writes=[pgk])
                        sg, sgk = s32.get()
                        S.op('act', lambda: A.activation(out=sg[:], in_=psg[:], func=AF.Sigmoid), reads=[pgk], writes=[sgk])
                        S.op('dve', lambda: V.tensor_tensor(out=ap_t[:, oc, 30:30 + TT], in0=psl[:], in1=sg[:], op=ALU.mult),
                             reads=[plk, sgk], writes=[apk])
                        if tt == NT - 1:
                            S.op('dve', lambda: V.tensor_tensor(out=a32[:, oc, 0:32], in0=psl[:, TT - 32:TT], in1=sg[:, TT - 32:TT], op=ALU.mult),
                                 reads=[plk, sgk], writes=['a32'])
                    Wu, Wuk = w_next()
                    for oc in range(4):
                        ps, pk = PS()
                        for k in range(8):
                            mm(ps[:], Wu[:, k, oc * 128:(oc + 1) * 128], hT[:, k, cols], k == 0, k == 7, reads=Wuk + hk, writes=[pk])
                        S.op('act', lambda: A.activation(out=u_t[:, oc, :], in_=ps[:], func=AF.Gelu_apprx_tanh), reads=[pk], writes=['u_t'])
                    Wv, Wvk = w_next()
                    for sub in range(4):
                        c0 = t0 + sub * 128
                        ps, pk = PS()
                        for k in range(8):
                            mm(ps[:], hT[:, k, c0:c0 + 128], Wv[:, k, :], k == 0, k == 7, reads=Wvk + hk, writes=[pk])
                        gv, gvk = s32.get()
                        S.op('act', lambda: A.activation(out=gv[:], in_=ps[:], func=AF.Gelu_apprx_tanh), reads=[pk], writes=[gvk])
                        S.op('dve', lambda: V.bn_stats(out=stat[:, 0:6], in_=gv[:]), reads=[gvk], writes=['stat'])
                        S.op('dve', lambda: V.bn_aggr(out=stat[:, 6:8], in_=stat[:, 0:6]), reads=['stat'], writes=['stat'])
                        S.op('act', lambda: A.activation(out=stat[:, 7:8], in_=stat[:, 7:8], func=AF.Sqrt, bias=eps_t[:, 0:1], scale=1.0),
                             reads=['stat', 'eps'], writes=['stat'])
                        S.op('dve', lambda: V.reciprocal(out=stat[:, 7:8], in_=stat[:, 7:8]), reads=['stat'], writes=['stat'])
                        S.op('dve', lambda: V.tensor_scalar(out=gv[:], in0=gv[:], scalar1=stat[:, 6:7], scalar2=stat[:, 7:8],
                                                             op0=ALU.subtract, op1=ALU.mult), reads=[gvk, 'stat'], writes=[gvk])
                        S.op('dve', lambda: V.tensor_tensor(out=gv[:], in0=gv[:], in1=gbc[:], op=ALU.mult), reads=[gvk, 'gbc'], writes=[gvk])
                        last = (tt == NT - 1 and sub == 3)
                        if last:
                            S.op('dve', lambda: V.tensor_tensor(out=gv[:], in0=gv[:], in1=bbc[:], op=ALU.add), reads=[gvk, 'bbc'], writes=[gvk])
                            S.dma(chunkv_p[e], gv[:], reads=[gvk])
                            S.op('act', lambda: A.copy(out=vb[:], in_=gv[:]), reads=[gvk], writes=['vb'])
                        else:
                            S.op('dve', lambda: V.tensor_tensor(out=vb[:], in0=gv[:], in1=bbc[:], op=ALU.add), reads=[gvk, 'bbc'], writes=['vb'])
                        ps, pk = PS()
                        for g in range(4):
                            mm(ps[:, g * 128:(g + 1) * 128], vb[:, g * 128:(g + 1) * 128], wmT[:, g, :], True, False,
                               reads=['vb', 'wmT'], writes=[pk])
                            mm(ps[:, g * 128:(g + 1) * 128], onesb[0:1, :], bsb[0:1, g, :], False, True,
                               reads=['onesb', 'bsb'], writes=[pk])
                        S.op('dve', lambda: V.tensor_tensor(out=bo_t[:, :, sub * 128:(sub + 1) * 128],
                                                             in0=u_t[:, :, sub * 128:(sub + 1) * 128],
                                                             in1=ps[:].rearrange("p (a b) -> p a b", a=4), op=ALU.mult),
                             reads=['u_t', pk], writes=['bo_t'])
                    pss, pssk = PS()
                    psq, psqk = PS()
                    for oc in range(4):
                        dg = diag[ndiag[0] % 2]
                        dgk = ('diag', ndiag[0] % 2)
                        ndiag[0] += 1
                        for k in range(31):
                            wcol = pe[:, 12 + 4 * k + oc:13 + 4 * k + oc]
                            if k % 2:
                                S.op('dve', lambda: V.tensor_scalar(out=dg[:, k, :], in0=identb[:], scalar1=wcol, scalar2=None, op0=ALU.mult),
                                     reads=['identb', 'pe'], writes=[dgk])
                            else:
                                S.op('act', lambda: A.activation(out=dg[:, k, :], in_=identb[:], func=AF.Identity, scale=wcol),
                                     reads=['identb', 'pe'], writes=[dgk])
                        ps, pk = PS()
                        for k in range(31):
                            mm(ps[:], dg[:, k, :], ap_t[:, oc, k:k + TT], k == 0, k == 30, reads=[dgk, apk], writes=[pk])
                        S.op('act', lambda: A.activation(out=acv[:, oc, :], in_=ps[:], func=AF.Identity, bias=pe[:, oc:oc + 1], scale=1.0),
                             reads=[pk, 'pe'], writes=[('acv', oc)])
                        sq, sqk = s16.get()
                        S.op('act', lambda: A.activation(out=sq[:], in_=ps[:], func=AF.Square, bias=pe[:, oc:oc + 1], scale=1.0),
                             reads=[pk, 'pe'], writes=[sqk])
                        mm(pss[:], onesb[:], acv[:, oc, :], oc == 0, oc == 3, reads=[('acv', oc), 'onesb'], writes=[pssk], inc=True)
                        mm(psq[:], onesb[:], sq[:], oc == 0, oc == 3, reads=[sqk, 'onesb'], writes=[psqk], inc=True)
                    S.op('act', lambda: A.activation(out=mean[:], in_=pss[:], func=AF.Identity, scale=1.0 / 512), reads=[pssk], writes=[mk])
                    if tt + 1 < NT:
                        hal, halk = s16.get()
                        S.op('dve', lambda: V.tensor_copy(out=hal[:, 0:120].rearrange("p (a b) -> p a b", a=4), in_=ap_t[:, :, TT:TT + 30]),
                             reads=[apk], writes=[halk])
                        S.op('dve', lambda: V.tensor_copy(out=ap_t[:, :, 0:30], in_=hal[:, 0:120].rearrange("p (a b) -> p a b", a=4)),
                             reads=[halk], writes=[apk])
                    S.op('dve', lambda: V.tensor_tensor(out=msq[:], in0=mean[:], in1=mean[:], op=ALU.mult), reads=[mk], writes=[msk])
                    S.op('dve', lambda: V.scalar_tensor_tensor(out=msq[:], in0=psq[:], scalar=1.0 / 512, in1=msq[:], op0=ALU.mult, op1=ALU.subtract),
                         reads=[psqk, msk], writes=[msk])
                    S.op('act', lambda: A.activation(out=msq[:], in_=msq[:], func=AF.Sqrt, bias=eps_t[:, 0:1], scale=1.0), reads=[msk, 'eps'], writes=[msk])
                    S.op('dve', lambda: V.reciprocal(out=msq[:], in_=msq[:]), reads=[msk], writes=[msk])
                    for oc in range(4):
                        xc, xck = s32.get()
                        S.op('dve', lambda: V.tensor_tensor(out=xc[:], in0=acv[:, oc, :], in1=mean[:], op=ALU.subtract), reads=[('acv', oc), mk], writes=[xck])
                        S.op('dve', lambda: V.tensor_tensor(out=xc[:], in0=xc[:], in1=msq[:], op=ALU.mult), reads=[xck, msk], writes=[xck])
                        S.op('act', lambda: A.activation(out=ao_t[:, oc, :], in_=xc[:], func=AF.Silu, scale=pe[:, 4 + oc:5 + oc], bias=pe[:, 8 + oc:9 + oc]),
                             reads=[xck, 'pe'], writes=['ao_t'])
                    if tt == NT - 1:
                        ps, pk = PS()
                        for oc in range(4):
                            tr(ps[0:32, oc * 128:(oc + 1) * 128], a32[:, oc, :], ident32[:], reads=['a32', 'ident32'], writes=[pk], inc=(oc == 3))
                        o32, o32k = s32.get()
                        S.op('act', lambda: A.copy(out=o32[0:32, :], in_=ps[0:32, :]), reads=[pk], writes=[o32k])
                        S.dma(conv_p[e], o32[2:32, :], reads=[o32k])
                    Wo1, Wo1k = w_next()
                    Wo2, Wo2k = w_next()
                    for i in range(8):
                        ps, pk = PS()
                        for k in range(8):
                            if k < 4:
                                mm(ps[:], Wo1[:, k, i * 128:(i + 1) * 128], ao_t[:, k, :], k == 0, False, reads=Wo1k + ['ao_t'], writes=[pk])
                            else:
                                mm(ps[:], Wo2[:, k - 4, i * 128:(i + 1) * 128], bo_t[:, k - 4, :], False, k == 7, reads=Wo2k + ['bo_t'], writes=[pk])
                        S.op('dve', lambda: V.scalar_tensor_tensor(out=xT[:, i, cols], in0=ps[:], scalar=m[:, g1 + i, 0:1],
                                                                    in1=xT[:, i, cols], op0=ALU.mult, op1=ALU.add),
                             reads=[pk, ('mod', l % 2)] + xkeys[tt], writes=[xkeys[tt][i // 4]])


        def odd_layer(l):
            o = l // 2
            m = mod[l % 2]
            g1 = 2 * 8
            SC = 0.125
            hflat = hT[:].rearrange("p a b -> p (a b)")
            skT = hflat[:, 0:4096].rearrange("p (c t) -> p c t", c=2)
            wkT = hflat[:, 4096:8192].rearrange("p (c t) -> p c t", c=2)
            svb = hflat[:, 8192:12288].rearrange("p (t f) -> p t f", t=16)
            wvb = hflat[:, 12288:16384].rearrange("p (t f) -> p t f", t=16)
            allh = [k for t in range(NT) for k in hkeys[t]] + [('h', 's')]
            arena = [(n, t) for n in ('skT', 'wkT', 'svb', 'wvb') for t in range(16)]
            with phase() as ph:
                hs_t = sb("hs_t", [128, 8, NS], BF16, ph)
                norm_samples(l, 0)
                S.op('dve', lambda: V.tensor_copy(out=hs_t[:], in_=hT[:, :, SEQ:HC]), reads=[('h', 's')], writes=['hs_t'])
                S.op('dve', lambda: V.memset(hflat[:, 16384:16392], 0.0), reads=['hs_t'], writes=allh + arena)
                kcT = sb("kcT", [128, 2, 128], BF16, ph)
                vc = sb("vc", [128, 4, 64], BF16, ph)
                htile = sb("htile", [128, 8, TT], BF16, ph)
                htk = [('ht', c) for c in range(8)]
                cs2 = sb("cs2", [128, 4, 64], F32, ph)
                sn2 = sb("sn2", [128, 4, 64], F32, ph)
                with phase() as pa:
                    ckcvT = sb("ckcvT", [128, 4, SEQ], BF16, pa)
                    for tt in range(NT if OST >= 1 else 0):
                        t0 = tt * TT
                        norm_tile(l, 0, tt, htile, htk)
                        S.dma(cs2[:], rope_cos[t0:t0 + TT, :].rearrange("(s p) d -> p s d", p=128), writes=['cs2'])
                        S.dma(sn2[:], rope_sin[t0:t0 + TT, :].rearrange("(s p) d -> p s d", p=128), writes=['sn2'])
                        for blk in range(3 if OST != 1 else int(os.environ.get('K_NBLK', 3))):
                            W, Wk = w_next()
                            for sub in range(4):
                                T128 = tt * 4 + sub
                                r0 = T128 * 128
                                ps, pk = PS()
                                for k in range(8):
                                    mm(ps[:], htile[:, k, sub * 128:(sub + 1) * 128], W[:, k, :], k == 0, k == 7, reads=Wk + htk, writes=[pk])
                                zt, ztk = s32.get()
                                if blk == 0:
                                    S.op('act', lambda: A.copy(out=zt[:], in_=ps[:]), reads=[pk], writes=[ztk])
                                    S.dma(nsa_p[0][o, r0:r0 + 128, :], zt[:, 0:256], reads=[ztk])
                                    S.dma(nsa_p[1][o, r0:r0 + 128, :], zt[:, 256:512], reads=[ztk])
                                    ps2, pk2 = PS()
                                    for q in range(4):
                                        tr(ps2[:, q * 128:(q + 1) * 128], zt[:, q * 128:(q + 1) * 128], ident32[:], reads=[ztk, 'ident32'], writes=[pk2], inc=(q == 3))
                                    S.op('dve', lambda: V.tensor_copy(out=ckcvT[:, :, r0:r0 + 128], in_=ps2[:].rearrange("p (a b) -> p a b", a=4)),
                                         reads=[pk2], writes=[('ckcvT', T128)])
                                else:
                                    t12, t12k = s32.get()
                                    p3 = ps[:, 0:256].rearrange("p (h d) -> p h d", h=4)
                                    t1 = t12[:, 0:256].rearrange("p (h d) -> p h d", h=4)
                                    t2 = t12[:, 256:512].rearrange("p (h d) -> p h d", h=4)
                                    cb_ = cs2[:, sub, :].unsqueeze(1).to_broadcast([128, 4, 64])
                                    S.op('dve', lambda: V.tensor_tensor(out=t1, in0=p3, in1=cb_, op=ALU.mult), reads=[pk, 'cs2'], writes=[t12k])
                                    S.op('dve', lambda: V.tensor_tensor(out=t2[:, :, 0:32], in0=p3[:, :, 32:64],
                                                                         in1=sn2[:, sub, 0:32].unsqueeze(1).to_broadcast([128, 4, 32]), op=ALU.mult),
                                         reads=[pk, 'sn2'], writes=[t12k])
                                    S.op('dve', lambda: V.tensor_tensor(out=t2[:, :, 32:64], in0=p3[:, :, 0:32],
                                                                         in1=sn2[:, sub, 32:64].unsqueeze(1).to_broadcast([128, 4, 32]), op=ALU.mult),
                                         reads=[pk, 'sn2'], writes=[t12k])
                                    S.op('dve', lambda: V.tensor_tensor(out=zt[:, 0:256], in0=t12[:, 0:256], in1=t12[:, 256:512], op=ALU.add),
                                         reads=[t12k], writes=[ztk])
                                    S.op('act', lambda: A.copy(out=zt[:, 256:512], in_=ps[:, 256:512]), reads=[pk], writes=[ztk])
                                    if blk == 1:
                                        S.dma(nsa_p[2][o, r0:r0 + 128, :], zt[:, 0:256], reads=[ztk])
                                        S.dma(nsa_p[3][o, r0:r0 + 128, :], zt[:, 256:512], reads=[ztk])
                                    elif T128 >= 12:
                                        S.dma(win_p[0][o, r0 - 1536:r0 - 1408, :], zt[:, 0:256], reads=[ztk])
                                        S.dma(win_p[1][o, r0 - 1536:r0 - 1408, :], zt[:, 256:512], reads=[ztk])
                                    VB, vbn = (svb, 'svb') if blk == 1 else (wvb, 'wvb')
                                    KT, ktn = (skT, 'skT') if blk == 1 else (wkT, 'wkT')
                                    S.op('dve', lambda: V.tensor_copy(out=VB[:, T128, :], in_=zt[:, 256:512]), reads=[ztk], writes=[(vbn, T128)])
                                    ps2, pk2 = PS()
                                    for q in range(2):
                                        tr(ps2[:, q * 128:(q + 1) * 128], zt[:, q * 128:(q + 1) * 128], ident32[:], reads=[ztk, 'ident32'], writes=[pk2], inc=(q == 1))
                                    S.op('act', lambda: A.copy(out=KT[:, :, r0:r0 + 128], in_=ps2[:, 0:256].rearrange("p (a b) -> p a b", a=2)),
                                         reads=[pk2], writes=[(ktn, T128)])
                    with phase() as pc:
                      if OST >= 2:
                            w1t = sb("w1t", [128, 32, 128], BF16, pc)
                            w2t = sb("w2t", [128, 64], BF16, pc)
                            peT = sb("peT", [128, 32], BF16, pc)
                            hb = sb("hb", [128, 1], F32, pc)
                            ckk = [('ckcvT', t) for t in range(16)]
                            for X in range(2):
                                for half in range(2):
                                    S.dma(w1t[half * 64:(half + 1) * 64, :, :], cmp_w1[X][o].rearrange("l d e -> d l e"), writes=[('w1t', half)], q='pool')
                                    S.dma(peT[half * 64:(half + 1) * 64, :], cmp_pe[X][o].rearrange("l d -> d l"), writes=[('peT', half)], q='pool',
                                          allow_slow_non_contiguous=True)
                                S.dma(w2t[:], cmp_w2[X][o], writes=['w2t'], q='pool')
                                ps, pk = PS()
                                for li in range(32):
                                    mm(ps[:, 0:1], w1t[0:64, li, :], peT[0:64, li:li + 1], li == 0, li == 31, reads=[('w1t', 0), ('peT', 0)], writes=[pk])
                                S.op('dve', lambda: V.tensor_copy(out=hb[:], in_=ps[:, 0:1]), reads=[pk], writes=['hb'])
                                for kv in range(4):
                                    half = kv % 2
                                    hs = slice(half * 64, half * 64 + 64)
                                    X3 = ckcvT[hs, 2 * X + kv // 2, :].rearrange("p (c s) -> p c s", s=16)
                                    ps, pk = PS()
                                    for li in range(32):
                                        mm(ps[:, 0:127], w1t[hs, li, :], X3[:, li // 16:li // 16 + 127, li % 16], li == 0, li == 31,
                                           reads=[('w1t', half)] + ckk, writes=[pk])
                                    hg, hgk = s16.get()
                                    S.op('act', lambda: A.activation(out=hg[:, 0:127], in_=ps[:, 0:127], func=AF.Gelu_apprx_tanh, bias=hb[:, 0:1], scale=1.0),
                                         reads=[pk, 'hb'], writes=[hgk])
                                    ps2, pk2 = PS()
                                    if X == 0:
                                        mm(ps2[hs, 0:127], w2t[:, 0:64], hg[:, 0:127], True, True, reads=['w2t', hgk], writes=[pk2])
                                        S.op('dve', lambda: V.tensor_copy(out=kcT[hs, kv // 2, 0:127], in_=ps2[hs, 0:127]), reads=[pk2], writes=['kcT'])
                                    else:
                                        mm(ps2[0:127, 0:64], hg[:, 0:127], w2t[:, 0:64], True, True, reads=['w2t', hgk], writes=[pk2])
                                        S.op('dve', lambda: V.tensor_copy(out=vc[0:127, kv, :], in_=ps2[0:127, 0:64]), reads=[pk2], writes=['vc'])
                with phase() as pb_:
                    qT = sb("qT", [128, 8, TQ], BF16, pb_)
                    qrT = sb("qrT", [128, 8, TQ], BF16, pb_)
                    oT = htile[:, :, 0:TQ]
                    qtk = sb("qtk", [128, 1024], F32, pb_)
                    qrk = sb("qrk", [128, 1024], F32, pb_)
                    gT = sb("gT", [48, TQ], BF16, pb_)
                    gtk = sb("gtk", [128, 48], F32, pb_)
                    bandb = sb("bandb", [128, 4, TQ], BF16, pb_)
                    Emat = sb("Emat", [32, 16, 128], BF16, pb_)
                    ov32 = sb("ov32", [128, 32], F32, pb_)
                    cmpb = sb("cmpb", [128, TQ], BF16, pb_)
                    selbT = sb("selbT", [32, 4, TQ], BF16, pb_)
                    tA = sb("tA", [128, 2, 32], F32, pb_)
                    tB = sb("tB", [128, 2, 32], F32, pb_)
                    pgA = [sb("pgA%d" % i, [128, TQ], F32, pb_) for i in range(2)]
                    scr = sb("scr", [128, 32], F32, pb_)
                    selt = sb("selt", [128, 32], F32, pb_)
                    selb = sb("selb", [128, 32], F32, pb_)
                    m8 = sb("m8", [128, 8], F32, pb_)
                    acc = sb("acc", [128, TQ], F32, pb_)
                    pring = Ring(nc, pb_, "pr_", 6, [128, TQ], BF16)
                    S.dma(bandb[:], bandb_d, writes=['bandb'], q='pool')
                    S.dma(Emat[:], Emat_d, writes=['Emat'], q='pool')
                    S.dma(ov32[0:127, :], ov_d, writes=['ov32'])
                    band_idx = {-4: 0, -3: 1, 0: 2, 1: 3}

                    sets = {'acc': [0, 1, 2, 3], 'st': [4, 5, 6], 'misc': [7]}
                    seti = {k: 0 for k in sets}

                    def PSS(name):
                        lst = sets[name]
                        i = lst[seti[name] % len(lst)]
                        seti[name] += 1
                        return ps_t[i], ('ps', i), None

                    def softmax_block(head_rows, qsrc, m_, kv, KT, VB, vrows, kts, kind, qt, po, pok, psm, psmk, M_sum):
                        hs = head_rows
                        n = len(kts)
                        for i, kt in enumerate(kts):
                            ps_s, psk, _ = PSS('st')
                            if kind == 'cmp':
                                kr = 127
                                mm(ps_s[0:kr, 0:TQ], kcT[hs, m_ // 4, 0:127], qsrc[hs, m_, :], True, False, reads=['kcT', 'qT'], writes=[psk])
                                mm(ps_s[0:kr, 0:TQ], identb[0:127, 0:127], cmpb[0:127, :], False, True, reads=['identb', 'cmpb'], writes=[psk])
                            else:
                                kr = 128
                                d = kt - 2 * qt
                                extra = []
                                if kind == 'sel':
                                    extra.append((Emat[0:32, kt, :], selbT[0:32, kv, :], ['Emat', ('selbT', kv)]))
                                if d in band_idx:
                                    extra.append((identb[:], bandb[:, band_idx[d], :], ['identb', 'bandb']))
                                ktn = 'skT' if kind == 'sel' else 'wkT'
                                mm(ps_s[:, 0:TQ], KT[hs, m_ // 4, kt * 128:(kt + 1) * 128], qsrc[hs, m_, :], True, len(extra) == 0,
                                   reads=[(ktn, kt), 'qrT'], writes=[psk])
                                for ei, (l_, r_, rd) in enumerate(extra):
                                    mm(ps_s[:, 0:TQ], l_, r_, False, ei == len(extra) - 1, reads=rd, writes=[psk])
                            P, Pk = pring.get()
                            S.op('act', lambda: A.activation(out=P[0:kr, :], in_=ps_s[0:kr, 0:TQ], func=AF.Exp, scale=SC), reads=[psk], writes=[Pk])
                            if kind == 'cmp':
                                lv = vc[0:127, kv, :]
                                vrd = ['vc']
                            else:
                                lv = VB[:, kt, kv * 64:(kv + 1) * 64]
                                vrd = [('svb' if kind == 'sel' else 'wvb', kt)]
                            mm(po[hs, 0:TQ], lv, P[0:kr, :], i == 0, i == n - 1, reads=vrd + [Pk], writes=[pok])
                            if M_sum == 128:
                                mm(psm[:, 0:TQ], onesb[0:kr, :], P[0:kr, :], i == 0, i == n - 1, reads=['onesb', Pk], writes=[psmk])
                            else:
                                mm(psm[hs, 0:TQ], onesb[0:kr, 0:64], P[0:kr, :], i == 0, i == n - 1, reads=['onesb', Pk], writes=[psmk])

                    def gate_bc(m_, br):
                        pg_, pgk, _ = PSS('misc')
                        for half in range(2):
                            h_ = (8 * (m_ // 4) + 4 * half + (m_ % 4))
                            row = h_ * 3 + br
                            mm(pg_[half * 64:(half + 1) * 64, 0:TQ], identb[0:48, row:row + 1].to_broadcast([48, 64]), gT[0:48, :], True, True,
                               reads=['identb', 'gT'], writes=[pgk], inc=(half == 1))
                        return pg_, pgk

                    for qt in range(NQ if OST >= 3 else 0):
                        t0 = qt * TQ
                        tt = qt // 2
                        if qt % 2 == 0:
                            norm_tile(l, 0, tt, htile, htk)
                        hoff = (qt % 2) * TQ
                        S.dma(cs2[:, 0:2, :], rope_cos[t0:t0 + TQ, :].rearrange("(s p) d -> p s d", p=128), writes=['cs2'])
                        S.dma(sn2[:, 0:2, :], rope_sin[t0:t0 + TQ, :].rearrange("(s p) d -> p s d", p=128), writes=['sn2'])
                        S.dma(tA[:], selA[t0:t0 + TQ, :].rearrange("(s p) d -> p s d", p=128), writes=['tA'])
                        S.dma(tB[:], selB[t0:t0 + TQ, :].rearrange("(s p) d -> p s d", p=128), writes=['tB'])
                        S.dma(cmpb[0:127, :], cmpb_d[:, t0:t0 + TQ], writes=['cmpb'], q='pool')
                        W0, W0k = w_next()
                        W1, W1k = w_next()
                        for sub in range(2):
                            hc = slice(hoff + sub * 128, hoff + (sub + 1) * 128)
                            for blk in range(2):
                                W, Wk = (W0, W0k) if blk == 0 else (W1, W1k)
                                ps, pk, _ = PSS('st')
                                for k in range(8):
                                    mm(ps[:], htile[:, k, hc], W[:, k, :], k == 0, k == 7, reads=Wk + htk, writes=[pk])
                                pin = ps[:].rearrange("p (h m d) -> p h m d", h=2, m=4)
                                qo = qtk[:, blk * 512:(blk + 1) * 512].rearrange("p (m h d) -> p h m d", m=4, h=2)
                                S.op('act', lambda: A.copy(out=qo, in_=pin), reads=[pk], writes=['qtk'])
                                t12, t12k = s32.get()
                                t3, t3k = s32.get()
                                p3 = ps[:].rearrange("p (h d) -> p h d", h=8)
                                t1 = t12[:].rearrange("p (h d) -> p h d", h=8)
                                t2 = t3[:].rearrange("p (h d) -> p h d", h=8)
                                S.op('dve', lambda: V.tensor_tensor(out=t1, in0=p3, in1=cs2[:, sub, :].unsqueeze(1).to_broadcast([128, 8, 64]), op=ALU.mult),
                                     reads=[pk, 'cs2'], writes=[t12k])
                                S.op('dve', lambda: V.tensor_tensor(out=t2[:, :, 0:32], in0=p3[:, :, 32:64],
                                                                     in1=sn2[:, sub, 0:32].unsqueeze(1).to_broadcast([128, 8, 32]), op=ALU.mult),
                                     reads=[pk, 'sn2'], writes=[t3k])
                                S.op('dve', lambda: V.tensor_tensor(out=t2[:, :, 32:64], in0=p3[:, :, 0:32],
                                                                     in1=sn2[:, sub, 32:64].unsqueeze(1).to_broadcast([128, 8, 32]), op=ALU.mult),
                                     reads=[pk, 'sn2'], writes=[t3k])
                                qro = qrk[:, blk * 512:(blk + 1) * 512].rearrange("p (m h d) -> p h m d", m=4, h=2)
                                S.op('dve', lambda: V.tensor_tensor(out=qro, in0=t12[:].rearrange("p (h m d) -> p h m d", h=2, m=4),
                                                                     in1=t3[:].rearrange("p (h m d) -> p h m d", h=2, m=4), op=ALU.add),
                                     reads=[t12k, t3k], writes=['qrk'])
                            for (src, srck, dstT, dk) in ((qtk, 'qtk', qT, 'qT'), (qrk, 'qrk', qrT, 'qrT')):
                                for hb_ in range(2):
                                    ps, pk, _ = PSS('st')
                                    for c in range(4):
                                        cc = hb_ * 4 + c
                                        tr(ps[:, c * 128:(c + 1) * 128], src[:, cc * 128:(cc + 1) * 128], ident32[:], reads=[srck, 'ident32'], writes=[pk], inc=(c == 3))
                                    S.op('act' if hb_ == 0 else 'dve',
                                         (lambda: A.copy(out=dstT[:, hb_ * 4:hb_ * 4 + 4, sub * 128:(sub + 1) * 128], in_=ps[:].rearrange("p (a b) -> p a b", a=4))) if hb_ == 0 else
                                         (lambda: V.tensor_copy(out=dstT[:, hb_ * 4:hb_ * 4 + 4, sub * 128:(sub + 1) * 128], in_=ps[:].rearrange("p (a b) -> p a b", a=4))),
                                         reads=[pk], writes=[dk])
                        Wg_, Wgk_ = w_next()
                        for sub in range(2):
                            hc = slice(hoff + sub * 128, hoff + (sub + 1) * 128)
                            ps, pk, _ = PSS('st')
                            for k in range(8):
                                mm(ps[:, 0:48], htile[:, k, hc], Wg_[:, k, :], k == 0, k == 7, reads=Wgk_ + htk, writes=[pk])
                            S.op('act', lambda: A.activation(out=gtk[:], in_=ps[:, 0:48], func=AF.Sigmoid), reads=[pk], writes=['gtk'])
                            ps, pk, _ = PSS('misc')
                            tr(ps[0:48, 0:128], gtk[:], ident32[:], reads=['gtk', 'ident32'], writes=[pk])
                            S.op('dve', lambda: V.tensor_copy(out=gT[:, sub * 128:(sub + 1) * 128], in_=ps[0:48, 0:128]), reads=[pk], writes=['gT'])
                        ok = htk
                        for mg in range(2 if OST >= 4 else 0):
                            for mi in range(4):
                                m_ = 4 * mg + mi
                                po, pok, _ = PSS('acc')
                                recs = []
                                for half in range(2):
                                    kv = 2 * mg + half
                                    hs = slice(half * 64, half * 64 + 64)
                                    psm, psmk, _ = PSS('acc')
                                    softmax_block(hs, qT, m_, kv, None, None, None, [0], 'cmp', qt, po, pok, psm, psmk, 128)
                                    rec, reck = s32.get()
                                    S.op('dve', lambda: V.tensor_scalar(out=rec[:, 0:TQ], in0=psm[:, 0:TQ], scalar1=1e-30, scalar2=None, op0=ALU.max), reads=[psmk], writes=[reck])
                                    S.op('dve', lambda: V.reciprocal(out=rec[:, 0:TQ], in_=rec[:, 0:TQ]), reads=[reck], writes=[reck])
                                    Plast = pring.t[(pring.i - 1) % 6]
                                    Plk = ("pr_", (pring.i - 1) % 6)
                                    pg = pgA[half]
                                    if mi == 0:
                                        S.op('dve', lambda: V.tensor_tensor(out=pg[0:127, :], in0=Plast[0:127, :], in1=rec[0:127, 0:TQ], op=ALU.mult),
                                             reads=[Plk, reck], writes=[('pg', half)])
                                    else:
                                        t_, tk_ = s32.get()
                                        S.op('dve', lambda: V.tensor_tensor(out=t_[0:127, 0:TQ], in0=Plast[0:127, :], in1=rec[0:127, 0:TQ], op=ALU.mult),
                                             reads=[Plk, reck], writes=[tk_])
                                        S.op('dve', lambda: V.tensor_tensor(out=pg[0:127, :], in0=pg[0:127, :], in1=t_[0:127, 0:TQ], op=ALU.add),
                                             reads=[tk_, ('pg', half)], writes=[('pg', half)])
                                    recs.append((rec, reck))
                                pg_, pgk = gate_bc(m_, 0)
                                for half in range(2):
                                    hs = slice(half * 64, half * 64 + 64)
                                    rec, reck = recs[half]
                                    S.op('dve', lambda: V.tensor_tensor(out=rec[hs, 0:TQ], in0=rec[hs, 0:TQ], in1=pg_[hs, 0:TQ], op=ALU.mult), reads=[reck, pgk], writes=[reck])
                                    S.op('dve', lambda: V.tensor_tensor(out=oT[hs, m_, :], in0=po[hs, 0:TQ], in1=rec[hs, 0:TQ], op=ALU.mult),
                                         reads=[pok, reck], writes=[ok[m_]])
                            for half in range(2):
                                kv = 2 * mg + half
                                pg = pgA[half]
                                for sub in range(2):
                                    ps, pk, pb = PSS('misc')
                                    mm(ps[:, 0:32], pg[0:127, sub * 128:(sub + 1) * 128], ov32[0:127, :], True, True, reads=[('pg', half), 'ov32'], writes=[pk])
                                    S.op('dve', lambda: V.tensor_tensor(out=scr[:], in0=ps[:, 0:32], in1=tA[:, sub, :], op=ALU.mult), reads=[pk, 'tA'], writes=['scr'])
                                    S.op('dve', lambda: V.tensor_tensor(out=scr[:], in0=scr[:], in1=tB[:, sub, :], op=ALU.add), reads=['scr', 'tB'], writes=['scr'])
                                    S.op('dve', lambda: V.max(out=m8[:], in_=scr[:]), reads=['scr'], writes=['m8'])
                                    S.op('dve', lambda: V.tensor_scalar(out=selt[:], in0=scr[:], scalar1=m8[:, 7:8], scalar2=None, op0=ALU.is_ge), reads=['scr', 'm8'], writes=['selt'])
                                    S.op('dve', lambda: V.tensor_scalar(out=selb[:], in0=selt[:], scalar1=-1.0, scalar2=30000.0, op0=ALU.add, op1=ALU.mult), reads=['selt'], writes=['selb'])
                                    ps, pk, _ = PSS('misc')
                                    tr(ps[0:32, 0:128], selb[:], ident32[:], reads=['selb', 'ident32'], writes=[pk])
                                    S.op('act', lambda: A.copy(out=selbT[:, kv, sub * 128:(sub + 1) * 128], in_=ps[0:32, 0:128]), reads=[pk], writes=[('selbT', kv)])
                        for m_ in range(8 if OST >= 5 else 0):
                            for br, kind in ((1, 'sel'), (2, 'win')):
                                po, pok, _ = PSS('acc')
                                psm, psmk, _ = PSS('acc')
                                if kind == 'sel':
                                    kts = list(range(0, 2 * qt + 2))
                                    KT, VB = skT, svb
                                else:
                                    kts = list(range(max(0, 2 * qt - 4), 2 * qt + 2))
                                    KT, VB = wkT, wvb
                                for half in range(2):
                                    kv = 2 * (m_ // 4) + half
                                    hs = slice(half * 64, half * 64 + 64)
                                    softmax_block(hs, qrT, m_, kv, KT, VB, None, kts, kind, qt, po, pok, psm, psmk, 64)
                                pg_, pgk = gate_bc(m_, br)
                                rec, reck = s32.get()
                                S.op('dve', lambda: V.tensor_scalar(out=rec[:, 0:TQ], in0=psm[:, 0:TQ], scalar1=1e-30, scalar2=None, op0=ALU.max), reads=[psmk], writes=[reck])
                                S.op('dve', lambda: V.reciprocal(out=rec[:, 0:TQ], in_=rec[:, 0:TQ]), reads=[reck], writes=[reck])
                                S.op('dve', lambda: V.tensor_tensor(out=rec[:, 0:TQ], in0=rec[:, 0:TQ], in1=pg_[:, 0:TQ], op=ALU.mult), reads=[reck, pgk], writes=[reck])
                                if kind == 'sel':
                                    S.op('dve', lambda: V.tensor_tensor(out=acc[:], in0=po[:, 0:TQ], in1=rec[:, 0:TQ], op=ALU.mult), reads=[pok, reck], writes=['acc'])
                                else:
                                    S.op('dve', lambda: V.tensor_tensor(out=rec[:, 0:TQ], in0=po[:, 0:TQ], in1=rec[:, 0:TQ], op=ALU.mult), reads=[pok, reck], writes=[reck])
                                    S.op('dve', lambda: V.tensor_tensor(out=acc[:], in0=acc[:], in1=rec[:, 0:TQ], op=ALU.add), reads=['acc', reck], writes=['acc'])
                            S.op('dve', lambda: V.tensor_tensor(out=oT[:, m_, :], in0=oT[:, m_, :], in1=acc[:], op=ALU.add), reads=['acc', ok[m_]], writes=[ok[m_]])
                        if OST < 5:
                            continue
                        Wo1, Wo1k = w_next()
                        Wo2, Wo2k = w_next()
                        for i in range(8):
                            ps, pk, _ = PSS('st')
                            for m_ in range(8):
                                Wo, Wok = (Wo1, Wo1k) if m_ < 4 else (Wo2, Wo2k)
                                mm(ps[:, 0:TQ], Wo[:, m_ % 4, i * 128:(i + 1) * 128], oT[:, m_, :], m_ == 0, m_ == 7, reads=Wok + [ok[m_]], writes=[pk])
                            S.op('dve', lambda: V.scalar_tensor_tensor(out=xT[:, i, t0:t0 + TQ], in0=ps[:, 0:TQ], scalar=m[:, g1 + i, 0:1],
                                                                        in1=xT[:, i, t0:t0 + TQ], op0=ALU.mult, op1=ALU.add),
                                 reads=[pk, ('mod', l % 2)] + xkeys[tt], writes=[xkeys[tt][i // 4]])
                if OST >= 6:
                  with phase() as sp_:
                    hflat32 = hflat.bitcast(F32)
                    stg = [hflat32[:, i * 4096:(i + 1) * 4096].rearrange("p (j f) -> p j f", j=16) for i in range(2)]
                    stgk = [[('stg', i, j) for j in range(16)] for i in range(2)]
                    S.op('dve', lambda: V.memset(hflat[:, 16384:16392], 0.0), writes=allh + arena + stgk[0] + stgk[1])
                    qsT = sb("qsT", [128, 8, NS], BF16, sp_)
                    qrsT = sb("qrsT", [128, 8, NS], BF16, sp_)
                    knT = sb("knT", [128, 4, NS], BF16, sp_)
                    vnb = sb("vnb", [NS, 512], BF16, sp_)
                    gs_t = sb("gs_t", [NS, 48], F32, sp_)
                    gperm = sb("gperm", [NS, 3, 16], F32, sp_)
                    pgall = sb("pgall", [128, 4, NS], F32, sp_)
                    oTb = [sb("oTb%d" % i, [128, 8, NS], F32, sp_) for i in range(3)]
                    kcs = sb("kcs", [128, 2, 128], BF16, sp_)
                    vcs = sb("vcs", [128, 4, 64], BF16, sp_)
                    selbTs = sb("selbTs", [33, 4, NS], BF16, sp_)
                    idx = sb("idx", [128, 256], I32, sp_)
                    idxf = sb("idxf", [128, 256], F32, sp_)
                    poff = sb("poff", [128, 2], F32, sp_)
                    nb16 = sb("nb16", [16, 16], BF16, sp_)
                    Emat_s = sb("Emat_s", [32, 16, 128], BF16, sp_)
                    ov33 = sb("ov33", [128, 33], F32, sp_)
                    sAs = sb("sAs", [NS, 33], F32, sp_)
                    sBs = sb("sBs", [NS, 33], F32, sp_)
                    XTs = sb("XTs", [128, 2, SEQ], BF16, sp_)
                    Pc = sb("Pc", [128, 144], BF16, sp_)
                    recs = sb("recs", [128, 8], F32, sp_)
                    pns = sb("pns", [128, 8], F32, sp_)
                    S.dma(nb16[:], nb16_d, writes=['nb16'], q='pool')
                    S.dma(Emat_s[:], Emat_d, writes=['Emat_s'], q='pool')
                    S.dma(ov33[0:127, :], ov33_d, writes=['ov33'])
                    S.dma(sAs[:], selA_s.to_broadcast([NS, 33]), writes=['sAs'])
                    S.dma(sBs[:], selB_s.to_broadcast([NS, 33]), writes=['sBs'])
                    S.dma(poff[:], poff_d, writes=['poff'])
                    S.dma(idx[:], page_table.rearrange("b j -> (b j)").unsqueeze(0).to_broadcast([128, 256]), writes=['idx'])
                    S.op('dve', lambda: V.tensor_copy(out=idxf[:], in_=idx[:]), reads=['idx'], writes=['idxf'])
                    S.op('dve', lambda: V.tensor_scalar(out=idxf[:], in0=idxf[:], scalar1=128.0, scalar2=poff[:, o:o + 1], op0=ALU.mult, op1=ALU.add),
                         reads=['idxf', 'poff'], writes=['idxf'])
                    S.op('dve', lambda: V.tensor_copy(out=idx[:], in_=idxf[:]), reads=['idxf'], writes=['idx'])

                    def gather(cache, b, dst, dkeys):
                        for j in range(16):
                            S.dma_ind(dst[:, j, :], cache, idx[:, b * 16 + j:b * 16 + j + 1], reads=['idx'], writes=[dkeys[j]])

                    with phase() as s1:
                        zs = sb("zs", [NS, 2608], F32, s1)
                        qr = sb("qr", [NS, 1024], F32, s1)
                        tm = sb("tm", [NS, 1024], F32, s1)
                        qp = tm
                        c2s = sb("c2s", [NS, 64], F32, s1)
                        s2s = sb("s2s", [NS, 64], F32, s1)
                        S.dma(c2s[:], rope_cos[SEQ:SEQ + 1, :].to_broadcast([NS, 64]), writes=['c2s'])
                        S.dma(s2s[:], rope_sin[SEQ:SEQ + 1, :].to_broadcast([NS, 64]), writes=['s2s'])
                        for (c0, n) in SBLK:
                            W, Wk = w_next()
                            ps, pk = PS()
                            for k in range(8):
                                mm(ps[0:NS, 0:n], hs_t[:, k, :], W[:, k, 0:n], k == 0, k == 7, reads=Wk + ['hs_t'], writes=[pk])
                            S.op('act', lambda: A.copy(out=zs[:, c0:c0 + n], in_=ps[0:NS, 0:n]), reads=[pk], writes=['zs'])

                        def rope_tok(dst, src, nh, key_w):
                            s3 = src.rearrange("p (h d) -> p h d", h=nh)
                            d3 = dst.rearrange("p (h d) -> p h d", h=nh)
                            t3 = tm[:, 0:nh * 64].rearrange("p (h d) -> p h d", h=nh)
                            S.op('dve', lambda: V.tensor_tensor(out=t3[:, :, 0:32], in0=s3[:, :, 32:64], in1=s2s[:, 0:32].unsqueeze(1).to_broadcast([NS, nh, 32]), op=ALU.mult),
                                 reads=['zs', 's2s'], writes=['tm'])
                            S.op('dve', lambda: V.tensor_tensor(out=t3[:, :, 32:64], in0=s3[:, :, 0:32], in1=s2s[:, 32:64].unsqueeze(1).to_broadcast([NS, nh, 32]), op=ALU.mult),
                                 reads=['zs', 's2s'], writes=['tm'])
                            S.op('dve', lambda: V.tensor_tensor(out=d3, in0=s3, in1=c2s[:].unsqueeze(1).to_broadcast([NS, nh, 64]), op=ALU.mult),
                                 reads=['zs', 'c2s'], writes=[key_w])
                            S.op('dve', lambda: V.tensor_tensor(out=d3, in0=d3, in1=t3, op=ALU.add), reads=[key_w, 'tm'], writes=[key_w])

                        rope_tok(qr[:], zs[:, 0:1024], 16, 'qr')
                        rope_tok(zs[:, 1536:1792], zs[:, 1536:1792], 4, 'zs')
                        rope_tok(zs[:, 2048:2304], zs[:, 2048:2304], 4, 'zs')
                        S.dma(nsa_s[0][o], zs[:, 1024:1280], reads=['zs'])
                        S.dma(nsa_s[1][o], zs[:, 1280:1536], reads=['zs'])
                        S.dma(nsa_s[2][o], zs[:, 1536:1792], reads=['zs'])
                        S.dma(nsa_s[3][o], zs[:, 1792:2048], reads=['zs'])
                        for X in range(2):
                            S.dma(win_s[X][o, :, 0:511, :], state_win[X][o, :, 1:512, :])
                            S.dma(win_s[X][o, :, 511, :], zs[:, 2048 + 256 * X:2304 + 256 * X], reads=['zs'])
                        S.op('act', lambda: A.activation(out=gs_t[:], in_=zs[:, 2560:2608], func=AF.Sigmoid), reads=['zs'], writes=['gs_t'])
                        for br in range(3):
                            gsrc = gs_t[:].rearrange("p (g h m c) -> p g h m c", g=2, h=2, m=4)[:, :, :, :, br]
                            for g_ in range(2):
                                S.op('dve', lambda: V.tensor_copy(out=gperm[:, br, g_ * 8:(g_ + 1) * 8].rearrange("p (m h) -> p h m", m=4, h=2), in_=gsrc[:, g_]),
                                     reads=['gs_t'], writes=['gperm'])
                        S.op('act', lambda: A.copy(out=vnb[:, 0:256], in_=zs[:, 1792:2048]), reads=['zs'], writes=['vnb'])
                        S.op('act', lambda: A.copy(out=vnb[:, 256:512], in_=zs[:, 2304:2560]), reads=['zs'], writes=['vnb'])
                        for (src, srck, dstT, dk) in ((zs, 'zs', qsT, 'qsT'), (qr, 'qr', qrsT, 'qrsT')):
                            for g_ in range(2):
                                S.op('dve', lambda: V.tensor_copy(out=qp[:, g_ * 512:(g_ + 1) * 512].rearrange("p (m h d) -> p h m d", m=4, h=2),
                                                                   in_=src[:, g_ * 512:(g_ + 1) * 512].rearrange("p (h m d) -> p h m d", h=2, m=4)),
                                     reads=[srck], writes=['tm'])
                            ps, pk = PS()
                            for c in range(8):
                                tr(ps[:, c * NS:(c + 1) * NS], qp[:, c * 128:(c + 1) * 128], ident32[0:NS, 0:NS], reads=['tm', 'ident32'], writes=[pk], inc=(c == 7))
                            S.op('act', lambda: A.copy(out=dstT[:], in_=ps[:, 0:8 * NS].rearrange("p (a b) -> p a b", a=8)), reads=[pk], writes=[dk])
                        ps, pk = PS()
                        for ci, c0 in enumerate((1536, 1664, 2048, 2176)):
                            tr(ps[:, ci * NS:(ci + 1) * NS], zs[:, c0:c0 + 128], ident32[0:NS, 0:NS], reads=['zs', 'ident32'], writes=[pk], inc=(ci == 3))
                        S.op('act', lambda: A.copy(out=knT[:], in_=ps[:, 0:4 * NS].rearrange("p (a b) -> p a b", a=4)), reads=[pk], writes=['knT'])

                    S.serial_compute = int(os.environ.get("K_SSER", 0))

                    def transposeX(st, stks, ntile, dstT, dkey):
                        for j0 in range(0, ntile, 2):
                            ps, pk = PS()
                            for jj in range(2):
                                for c in range(2):
                                    tr(ps[:, (jj * 2 + c) * 128:(jj * 2 + c + 1) * 128], st[:, j0 + jj, c * 128:(c + 1) * 128], ident32[:],
                                       reads=[stks[j0 + jj], 'ident32'], writes=[pk], inc=(jj == 1 and c == 1))
                            S.op('act' if (j0 // 2) % 2 == 0 else 'dve',
                                 (lambda: A.copy(out=dstT[:, :, j0 * 128:(j0 + 2) * 128].rearrange("p c (jj t) -> p c jj t", jj=2),
                                                 in_=ps[:].rearrange("p (jj c t) -> p c jj t", jj=2, c=2))) if (j0 // 2) % 2 == 0 else
                                 (lambda: V.tensor_copy(out=dstT[:, :, j0 * 128:(j0 + 2) * 128].rearrange("p c (jj t) -> p c jj t", jj=2),
                                                        in_=ps[:].rearrange("p (jj c t) -> p c jj t", jj=2, c=2))),
                                 reads=[pk], writes=[dkey])

                    with phase() as p1:
                      if SST >= 2:
                        w1ts = [sb("w1ts%d" % X, [128, 32, 128], BF16, p1) for X in range(2)]
                        w2ts = [sb("w2ts%d" % X, [128, 64], BF16, p1) for X in range(2)]
                        peTs = sb("peTs", [128, 32], BF16, p1)
                        hbs = [sb("hbs%d" % X, [128, 1], F32, p1) for X in range(2)]
                        for X in range(2):
                            for half in range(2):
                                S.dma(w1ts[X][half * 64:(half + 1) * 64, :, :], cmp_w1[X][o].rearrange("l d e -> d l e"), writes=[('w1ts', X, half)], q='pool')
                            S.dma(peTs[0:64, :], cmp_pe[X][o].rearrange("l d -> d l"), writes=['peTs'], q='pool', allow_slow_non_contiguous=True)
                            S.dma(w2ts[X][:], cmp_w2[X][o], writes=[('w2ts', X)], q='pool')
                            ps, pk = PS()
                            for li in range(32):
                                mm(ps[:, 0:1], w1ts[X][0:64, li, :], peTs[0:64, li:li + 1], li == 0, li == 31, reads=[('w1ts', X, 0), 'peTs'], writes=[pk])
                            S.op('dve', lambda: V.tensor_copy(out=hbs[X][:], in_=ps[:, 0:1]), reads=[pk], writes=[('hbs', X)])
                        for b in range(NS):
                            for X in range(2):
                                si = (2 * b + X) % 2
                                gather(caches[X], b, stg[si], stgk[si])
                                if P1 >= 2:
                                    transposeX(stg[si], stgk[si], 16, XTs, 'XTs')
                                for kv in range(4 if P1 >= 3 else 0):
                                    half = kv % 2
                                    hs = slice(half * 64, half * 64 + 64)
                                    X3 = XTs[hs, kv // 2, :].rearrange("p (c s) -> p c s", s=16)
                                    ps, pk = PS()
                                    for li in range(32):
                                        mm(ps[:, 0:127], w1ts[X][hs, li, :], X3[:, li // 16:li // 16 + 127, li % 16], li == 0, li == 31,
                                           reads=[('w1ts', X, half), 'XTs'], writes=[pk])
                                    hg, hgk = s16.get()
                                    S.op('act', lambda: A.activation(out=hg[:, 0:127], in_=ps[:, 0:127], func=AF.Gelu_apprx_tanh, bias=hbs[X][:, 0:1], scale=1.0),
                                         reads=[pk, ('hbs', X)], writes=[hgk])
                                    ps2, pk2 = PS()
                                    if X == 0:
                                        mm(ps2[hs, 0:127], w2ts[0][:, 0:64], hg[:, 0:127], True, True, reads=[('w2ts', 0), hgk], writes=[pk2])
                                        S.op('dve', lambda: V.tensor_copy(out=kcs[hs, kv // 2, 0:127], in_=ps2[hs, 0:127]), reads=[pk2], writes=['kcs'])
                                    else:
                                        mm(ps2[0:127, 0:64], hg[:, 0:127], w2ts[1][:, 0:64], True, True, reads=[('w2ts', 1), hgk], writes=[pk2])
                                        S.op('dve', lambda: V.tensor_copy(out=vcs[0:127, kv, :], in_=ps2[0:127, 0:64]), reads=[pk2], writes=['vcs'])
                            for mg in range(2 if P1 >= 4 else 0):
                                for half in range(2):
                                    hs = slice(half * 64, half * 64 + 64)
                                    ps_s, psk = PS()
                                    mm(ps_s[0:127, 0:4], kcs[hs, mg, 0:127], qsT[hs, 4 * mg:4 * mg + 4, b], True, True,
                                       reads=['kcs', 'qsT'], writes=[psk])
                                    S.op('act', lambda: A.activation(out=Pc[0:127, half * 4:half * 4 + 4], in_=ps_s[0:127, 0:4], func=AF.Exp, scale=SC), reads=[psk], writes=['Pc'])
                                ps_m, pmk = PS()
                                mm(ps_m[:, 0:8], onesb[0:127, :], Pc[0:127, 0:8], True, True, reads=['onesb', 'Pc'], writes=[pmk])
                                S.op('dve', lambda: V.reciprocal(out=recs[:], in_=ps_m[:, 0:8]), reads=[pmk], writes=['recs'])
                                S.op('dve', lambda: V.tensor_tensor(out=pns[0:127, :], in0=Pc[0:127, 0:8], in1=recs[0:127, :], op=ALU.mult), reads=['Pc', 'recs'], writes=['pns'])
                                S.op('dve', lambda: V.tensor_reduce(out=pgall[0:127, 2 * mg:2 * mg + 2, b], in_=pns[0:127, :].rearrange("p (h j) -> p h j", h=2),
                                                                     axis=AX.X, op=ALU.add), reads=['pns'], writes=['pgall'])
                                ps_o, pok = PS()
                                for half in range(2):
                                    hs = slice(half * 64, half * 64 + 64)
                                    mm(ps_o[hs, 0:4], vcs[0:127, 2 * mg + half, :], Pc[0:127, half * 4:half * 4 + 4], True, True,
                                       reads=['vcs', 'Pc'], writes=[pok], inc=(half == 1))
                                for half in range(2):
                                    hs = slice(half * 64, half * 64 + 64)
                                    S.op('dve', lambda: V.tensor_tensor(out=oTb[0][hs, 4 * mg:4 * mg + 4, b], in0=ps_o[hs, 0:4], in1=recs[hs, half * 4:half * 4 + 4], op=ALU.mult),
                                         reads=[pok, 'recs'], writes=['oTb0'])
                    scs = sb("scs", [NS, 33], F32, sp_)
                    for kv in range(4 if SST >= 3 else 0):
                        ps, pk = PS()
                        mm(ps[0:NS, 0:33], pgall[0:127, kv, :], ov33[0:127, :], True, True, reads=['pgall', 'ov33'], writes=[pk])
                        S.op('dve', lambda: V.tensor_tensor(out=scs[:], in0=ps[0:NS, 0:33], in1=sAs[:], op=ALU.mult), reads=[pk, 'sAs'], writes=['scs'])
                        S.op('dve', lambda: V.tensor_tensor(out=scs[:], in0=scs[:], in1=sBs[:], op=ALU.add), reads=['scs', 'sBs'], writes=['scs'])
                        S.op('dve', lambda: V.max(out=recs[0:NS, 0:8], in_=scs[:]), reads=['scs'], writes=['recs'])
                        S.op('dve', lambda: V.tensor_scalar(out=scs[:], in0=scs[:], scalar1=recs[0:NS, 7:8], scalar2=None, op0=ALU.is_ge), reads=['scs', 'recs'], writes=['scs'])
                        ps, pk = PS()
                        tr(ps[0:33, 0:NS], scs[:], ident32[0:NS, 0:NS], reads=['scs', 'ident32'], writes=[pk])
                        S.op('act', lambda: A.copy(out=selbTs[:, kv, :], in_=ps[0:33, 0:NS]), reads=[pk], writes=['selbTs'])
                    with phase() as p2:
                        svs = sb("svs", [128, 16, 256], BF16, p2)
                        svk = [('svs', j) for j in range(16)]
                        wst = sb("wst", [128, 4, 256], F32, p2)
                        wvs = sb("wvs", [128, 4, 256], BF16, p2)
                        wkTs = sb("wkTs", [128, 2, 512], BF16, p2)
                        for b in range(NS if SST >= 4 else 0):
                            si = b % 2
                            gather(caches[2], b, stg[si], stgk[si])
                            gather(caches[3], b, svs, svk)
                            S.dma(wst[:], state_win[0][o, b].rearrange("(t p) f -> p t f", p=128), writes=['wst'])
                            S.dma(wvs[:], state_win[1][o, b].rearrange("(t p) f -> p t f", p=128), writes=['wvs'], q='pool')
                            transposeX(stg[si], stgk[si], 16, XTs, 'XTs')
                            transposeX(wst, ['wst'] * 4, 4, wkTs, 'wkTs')
                            for mg in range(2):
                                for br, KTs, ktk, VBs, vks, nkt, knc, vnc in ((1, XTs, 'XTs', svs, svk, 16, 0, 0), (2, wkTs, 'wkTs', wvs, ['wvs'] * 4, 4, 2, 256)):
                                    c0 = nkt * 4
                                    for half in range(2):
                                        hs = slice(half * 64, half * 64 + 64)
                                        kv = 2 * mg + half
                                        pcb = half * 72
                                        ps_s, psk = PS()
                                        for kt in range(nkt):
                                            mm(ps_s[:, kt * 4:kt * 4 + 4], KTs[hs, mg, kt * 128:(kt + 1) * 128], qrsT[hs, 4 * mg:4 * mg + 4, b], True, True,
                                               reads=[ktk, 'qrsT'], writes=[psk], inc=False)
                                        mm(ps_s[0:NS, c0:c0 + 4], knT[hs, knc + mg, :], qrsT[hs, 4 * mg:4 * mg + 4, b], True, True, reads=['knT', 'qrsT'], writes=[psk])
                                        S.op('act', lambda: A.activation(out=Pc[:, pcb:pcb + c0], in_=ps_s[:, 0:c0], func=AF.Exp, scale=SC), reads=[psk], writes=['Pc'])
                                        S.op('act', lambda: A.activation(out=Pc[0:NS, pcb + c0:pcb + c0 + 4], in_=ps_s[0:NS, c0:c0 + 4], func=AF.Exp, scale=SC), reads=[psk], writes=['Pc'])
                                        S.op('dve', lambda: V.tensor_scalar(out=Pc[0:NS, pcb + c0:pcb + c0 + 4], in0=Pc[0:NS, pcb + c0:pcb + c0 + 4],
                                                                             scalar1=ident32[0:NS, b:b + 1], scalar2=None, op0=ALU.mult), reads=['Pc', 'ident32'], writes=['Pc'])
                                        if br == 1:
                                            ps_k, pkk = PS()
                                            for kt in range(nkt):
                                                mm(ps_k[:, kt:kt + 1], Emat_s[0:32, kt, :], selbTs[0:32, kv, b:b + 1], True, True, reads=['Emat_s', 'selbTs'], writes=[pkk], inc=(kt == nkt - 1))
                                            S.op('dve', lambda: V.tensor_tensor(out=Pc[:, pcb:pcb + c0].rearrange("p (k h) -> p k h", h=4),
                                                                                 in0=Pc[:, pcb:pcb + c0].rearrange("p (k h) -> p k h", h=4),
                                                                                 in1=ps_k[:, 0:nkt].unsqueeze(2).to_broadcast([128, nkt, 4]), op=ALU.mult),
                                                 reads=['Pc', pkk], writes=['Pc'])
                                    ps_o, pok = PS()
                                    ps_m, pmk = PS()
                                    for half in range(2):
                                        hs = slice(half * 64, half * 64 + 64)
                                        kv = 2 * mg + half
                                        pcb = half * 72
                                        for kt in range(nkt):
                                            cols = slice(pcb + kt * 4, pcb + kt * 4 + 4)
                                            mm(ps_o[hs, 0:4], VBs[:, kt, kv * 64:(kv + 1) * 64], Pc[:, cols], kt == 0, False, reads=[vks[kt], 'Pc'], writes=[pok])
                                            mm(ps_m[hs, 0:4], onesb[:, 0:64], Pc[:, cols], kt == 0, False, reads=['onesb', 'Pc'], writes=[pmk])
                                        cols = slice(pcb + c0, pcb + c0 + 4)
                                        mm(ps_o[hs, 0:4], vnb[0:NS, vnc + kv * 64:vnc + (kv + 1) * 64], Pc[0:NS, cols], False, True, reads=['vnb', 'Pc'], writes=[pok])
                                        mm(ps_m[hs, 0:4], onesb[0:NS, 0:64], Pc[0:NS, cols], False, True, reads=['onesb', 'Pc'], writes=[pmk])
                                    S.op('dve', lambda: V.reciprocal(out=recs[:, 0:4], in_=ps_m[:, 0:4]), reads=[pmk], writes=['recs'])
                                    S.op('dve', lambda: V.tensor_tensor(out=oTb[br][:, 4 * mg:4 * mg + 4, b], in0=ps_o[:, 0:4], in1=recs[:, 0:4], op=ALU.mult),
                                         reads=[pok, 'recs'], writes=['oTb%d' % br])
                    with phase() as p3:
                      if SST >= 5:
                        otk = sb("otk", [NS, 1024], F32, p3)
                        osum = sb("osum", [NS, 1024], F32, p3)
                        oTs = sb("oTs", [128, 8, NS], BF16, p3)
                        for br in range(3):
                            for half_ in range(2):
                                ps, pk = PS()
                                for c in range(4):
                                    cc = half_ * 4 + c
                                    tr(ps[0:NS, c * 128:(c + 1) * 128], oTb[br][:, cc, :], ident32[:], reads=['oTb%d' % br, 'ident32'], writes=[pk], inc=(c == 3))
                                S.op('act', lambda: A.copy(out=otk[:, half_ * 512:(half_ + 1) * 512], in_=ps[0:NS, :]), reads=[pk], writes=['otk'])
                            gb = gperm[:, br, :].unsqueeze(2).to_broadcast([NS, 16, 64])
                            o3 = otk[:].rearrange("p (h d) -> p h d", h=16)
                            if br == 0:
                                S.op('dve', lambda: V.tensor_tensor(out=osum[:].rearrange("p (h d) -> p h d", h=16), in0=o3, in1=gb, op=ALU.mult), reads=['otk', 'gperm'], writes=['osum'])
                            else:
                                S.op('dve', lambda: V.tensor_tensor(out=o3, in0=o3, in1=gb, op=ALU.mult), reads=['otk', 'gperm'], writes=['otk'])
                                S.op('dve', lambda: V.tensor_tensor(out=osum[:], in0=osum[:], in1=otk[:], op=ALU.add), reads=['otk', 'osum'], writes=['osum'])
                        ps, pk = PS()
                        for c in range(8):
                            tr(ps[:, c * NS:(c + 1) * NS], osum[:, c * 128:(c + 1) * 128], ident32[0:NS, 0:NS], reads=['osum', 'ident32'], writes=[pk], inc=(c == 7))
                        S.op('act', lambda: A.copy(out=oTs[:], in_=ps[:, 0:8 * NS].rearrange("p (a b) -> p a b", a=8)), reads=[pk], writes=['oTs'])
                        Wo1, Wo1k = w_next()
                        Wo2, Wo2k = w_next()
                        ps, pk = PS()
                        for i in range(8):
                            for m_ in range(8):
                                Wo, Wok = (Wo1, Wo1k) if m_ < 4 else (Wo2, Wo2k)
                                mm(ps[:, i * NS:(i + 1) * NS], Wo[:, m_ % 4, i * 128:(i + 1) * 128], oTs[:, m_, :], m_ == 0, m_ == 7, reads=Wok + ['oTs'], writes=[pk])
                        t, tk = s32.get()
                        tv = t[:, 0:8 * NS].rearrange("p (a b) -> p a b", a=8)
                        S.op('dve', lambda: V.tensor_tensor(out=tv, in0=ps[:, 0:8 * NS].rearrange("p (a b) -> p a b", a=8), in1=m[:, g1:g1 + 8, 1:17], op=ALU.mult),
                             reads=[pk, ('mod', l % 2)], writes=[tk])
                        S.op('dve', lambda: V.tensor_tensor(out=xsT[:], in0=xsT[:], in1=tv, op=ALU.add), reads=[tk, 'xs'], writes=['xs'])
                S.serial_compute = False
                S.op('dve', lambda: V.memset(hflat[:, 16384:16392], 0.0), writes=allh + arena + [('stg', i, j) for i in range(2) for j in range(16)])

        def final():
            with phase() as ph:
                yt = [sb("yt%d" % i, [128, D], F32, ph) for i in range(2)]
                yf = sb("yf", [128, 8, TT], F32, ph)
                for tt in range(NT):
                    cols = slice(tt * TT, (tt + 1) * TT)
                    ps, pk = PS()
                    for c in range(8):
                        sq, sqk = s16.get()
                        S.op('act', lambda: A.activation(out=sq[:], in_=xT[:, c, cols], func=AF.Square), reads=xkeys[tt], writes=[sqk])
                        mm(ps[:], onesb[:], sq[:], c == 0, c == 7, reads=[sqk, 'onesb'], writes=[pk], inc=True)
                    rs, rsk = rstd_t, 'rstd_t'
                    S.op('act', lambda: A.activation(out=rs[:], in_=ps[:], func=AF.Sqrt, bias=eps_t[:, 0:1], scale=1.0 / D), reads=[pk, 'eps'], writes=[rsk])
                    S.op('dve', lambda: V.reciprocal(out=rs[:], in_=rs[:]), reads=[rsk], writes=[rsk])
                    for c in range(8):
                        S.op('dve', lambda: V.scalar_tensor_tensor(out=yf[:, c, :], in0=xT[:, c, cols], scalar=pfin[:, c:c + 1], in1=rs[:],
                                                                    op0=ALU.mult, op1=ALU.mult), reads=xkeys[tt] + [rsk, 'pfin'], writes=[('yf', c)])
                    for sub in range(4):
                        y = yt[sub % 2]
                        yk = ('yt', sub % 2)
                        for half in range(2):
                            ps, pk = PS()
                            for q in range(4):
                                c = half * 4 + q
                                tr(ps[:, q * 128:(q + 1) * 128], yf[:, c, sub * 128:(sub + 1) * 128], ident32[:],
                                   reads=[('yf', c), 'ident32'], writes=[pk], inc=(q == 3))
                            if half == 0:
                                S.op('act', lambda: A.copy(out=y[:, 0:512], in_=ps[:]), reads=[pk], writes=[yk])
                            else:
                                S.op('dve', lambda: V.tensor_copy(out=y[:, 512:1024], in_=ps[:]), reads=[pk], writes=[yk])
                        r0 = tt * TT + sub * 128
                        S.dma(y_prompt[r0:r0 + 128, :], y[:], reads=[yk])
                ps, pk = PS()
                sq, sqk = s16.get()
                S.op('act', lambda: A.activation(out=sq[:, 0:8 * NS], in_=xsT[:].rearrange("p a b -> p (a b)"), func=AF.Square), reads=['xs'], writes=[sqk])
                for c in range(8):
                    mm(ps[:, 0:NS], onesb[:], sq[:, c * NS:(c + 1) * NS], c == 0, c == 7, reads=[sqk, 'onesb'], writes=[pk])
                rs, rsk = s32.get()
                S.op('act', lambda: A.activation(out=rs[:, 0:NS], in_=ps[:, 0:NS], func=AF.Sqrt, bias=eps_t[:, 0:1], scale=1.0 / D), reads=[pk, 'eps'], writes=[rsk])
                S.op('dve', lambda: V.reciprocal(out=rs[:, 0:NS], in_=rs[:, 0:NS]), reads=[rsk], writes=[rsk])
                t, tk = s32.get()
                tv = t[:, 0:8 * NS].rearrange("p (a b) -> p a b", a=8)
                S.op('dve', lambda: V.tensor_tensor(out=tv, in0=xsT[:], in1=rs[:, 0:NS].unsqueeze(1).to_broadcast([128, 8, NS]), op=ALU.mult), reads=['xs', rsk], writes=[tk])
                S.op('dve', lambda: V.tensor_tensor(out=tv, in0=tv, in1=pfin[:].unsqueeze(2).to_broadcast([128, 8, NS]), op=ALU.mult), reads=[tk, 'pfin'], writes=[tk])
                ps, pk = PS()
                ps2, pk2 = PS()
                for c in range(8):
                    pp = ps if c < 4 else ps2
                    tr(pp[0:NS, (c % 4) * 128:(c % 4 + 1) * 128], t[:, c * NS:(c + 1) * NS], ident32[:], reads=[tk, 'ident32'],
                       writes=[pk if c < 4 else pk2], inc=(c % 4 == 3))
                y = yt[0]
                S.op('act', lambda: A.copy(out=y[0:NS, 0:512], in_=ps[0:NS, :]), reads=[pk], writes=[('yt', 0)])
                S.op('dve', lambda: V.tensor_copy(out=y[0:NS, 512:1024], in_=ps2[0:NS, :]), reads=[pk2], writes=[('yt', 0)])
                S.dma(y_sample, y[0:NS, :], reads=[('yt', 0)])

        for l in range(n_layers):
            ada_finish(l)
            if l % 2 == 0:
                norm_pass(l, 0)
                even_layer(l)
            elif do_odd:
                odd_layer(l)
            norm_pass(l, 1)
            mlp(l)
        final()
        assert wstate['used'] == len(wplan), (wstate, len(wplan))
        S.finish()
        print("instr counts", S.n_ins, "waits", S.n_wait, flush=True)
    return nc


_OUT_NAMES = ["y_prompt", "y_sample", "conv_p", "conv_s", "chunkv_p", "chunkv_s"]


def kernel(**inputs):
    f = lambda k: np.ascontiguousarray(np.asarray(inputs[k]))
    n_layers = int(os.environ.get("K_LAYERS", DEPTH))
    nc = build_program(n_layers=n_layers)
    p_layer = np.zeros((DEPTH, 128, 64), np.float32)
    p_layer[:, :, 0:48] = f("ada_b").reshape(DEPTH, 48, 128).transpose(0, 2, 1)
    p_layer[:, :, 48:56] = f("norm_mix_g").reshape(DEPTH, 8, 128).transpose(0, 2, 1)
    p_layer[:, :, 56:64] = f("norm_ffn_g").reshape(DEPTH, 8, 128).transpose(0, 2, 1)
    p_even = np.zeros((2, 128, 136), np.float32)
    p_even[:, :, 0:4] = f("conv_b").reshape(2, 4, 128).transpose(0, 2, 1)
    p_even[:, :, 4:8] = f("conv_ln_g").reshape(2, 4, 128).transpose(0, 2, 1)
    p_even[:, :, 8:12] = f("conv_ln_b").reshape(2, 4, 128).transpose(0, 2, 1)
    p_even[:, :, 12:136] = f("conv_w").reshape(2, 31, 4, 128).transpose(0, 3, 1, 2).reshape(2, 128, 124)
    p_final = np.ascontiguousarray(f("final_norm_g").reshape(8, 128).T)
    sgu_wT = np.ascontiguousarray(f("sgu_w").transpose(0, 3, 1, 2))
    cst = np.zeros((128, 256), np.float32)
    cst[:, 0:128] = np.eye(128, dtype=np.float32)
    cst[:, 128:256] = np.triu(np.ones((128, 128), np.float32))
    half = 32
    inv = np.power(np.float32(10000.0), -np.arange(half, dtype=np.float32) * np.float32(2.0 / 64))
    pos = np.arange(SEQ + 1, dtype=np.float32)
    ang = pos[:, None] * inv[None, :]
    cosv, sinv = np.cos(ang).astype(np.float32), np.sin(ang).astype(np.float32)
    rope_cos = np.concatenate([cosv, cosv], 1)
    rope_sin = np.concatenate([-sinv, sinv], 1)
    tpos = np.arange(SEQ)[:, None]
    blk = np.arange(32)[None, :]
    cur = tpos // 64
    valid = blk * 64 <= tpos
    forced = (blk == 0) | (blk == cur) | (blk == cur - 1)
    selA = (valid & ~forced).astype(np.float32)
    selB = np.where(valid, np.where(forced, 1e4, 0.0), -1e30).astype(np.float32)
    ncmp = np.arange(127)[:, None]
    cmpb = np.where(16 * ncmp + 31 <= np.arange(SEQ)[None, :], 0.0, -30000.0).astype(np.float32)
    key = np.arange(128)[:, None, None]
    dd = np.array([-4, -3, 0, 1])[None, :, None]
    tq = np.arange(256)[None, None, :]
    diff = tq - key - 128 * dd
    bandb = np.where((diff >= 0) & (diff <= 512), 0.0, -30000.0).astype(np.float32)
    jj = np.arange(32)[:, None, None]
    ktt = np.arange(16)[None, :, None]
    kk = np.arange(128)[None, None, :]
    Emat = (jj == 2 * ktt + kk // 64).astype(np.float32)
    ci = np.arange(127)[:, None] * 16
    sj = np.arange(32)[None, :] * 64
    ov = ((ci < sj + 64) & (ci + 32 > sj)).astype(np.float32)
    ci33 = np.arange(127)[:, None] * 16
    sj33 = np.arange(33)[None, :] * 64
    ov33 = ((ci33 < sj33 + 64) & (ci33 + 32 > sj33)).astype(np.float32)
    j33 = np.arange(33)
    forced33 = (j33 == 0) | (j33 == 32) | (j33 == 31)
    selA_s = (~forced33).astype(np.float32)[None, :]
    selB_s = np.where(forced33, 1e4, 0.0).astype(np.float32)[None, :]
    nb16 = np.where(np.eye(16, dtype=bool), 0.0, -30000.0).astype(np.float32)
    NPOOL = int(os.environ.get("K_POOLPAGES", 2560))
    poff = (np.arange(2)[None, :] * (NPOOL * 128) + np.arange(128)[:, None]).astype(np.float32)
    shared = dict(ada_w=f("ada_w"), ffn_w1=f("ffn_w1"), ffn_w2=f("ffn_w2"), even_w_in=f("even_w_in"),
                  even_w_out=f("even_w_out"), p_layer=p_layer, p_even=p_even, p_final=p_final,
                  conv_w=f("conv_w"), conv_b=f("conv_b"), conv_ln_g=f("conv_ln_g"), conv_ln_b=f("conv_ln_b"),
                  sgu_ln_g=f("sgu_ln_g"), sgu_ln_b=f("sgu_ln_b"), sgu_wT=sgu_wT, sgu_b=f("sgu_b"), cst=cst,
                  odd_w_in=f("odd_w_in"), odd_w_out=f("odd_w_out"),
                  cmp_pe_k=f("cmp_pe_k"), cmp_pe_v=f("cmp_pe_v"), cmp_w1_k=f("cmp_w1_k"), cmp_w1_v=f("cmp_w1_v"),
                  cmp_w2_k=f("cmp_w2_k"), cmp_w2_v=f("cmp_w2_v"),
                  rope_cos=rope_cos, rope_sin=rope_sin, selA=selA, selB=selB, cmpb=cmpb, bandb=bandb, Emat=Emat, ov=ov,
                  ov33=ov33, selA_s=selA_s, selB_s=selB_s, nb16=nb16, poff=poff,
                  cache_cmp_k=np.ascontiguousarray(f("cache_cmp_k")[:, :NPOOL]).reshape(-1, 256), cache_cmp_v=np.ascontiguousarray(f("cache_cmp_v")[:, :NPOOL]).reshape(-1, 256),
                  cache_sel_k=np.ascontiguousarray(f("cache_sel_k")[:, :NPOOL]).reshape(-1, 256), cache_sel_v=np.ascontiguousarray(f("cache_sel_v")[:, :NPOOL]).reshape(-1, 256))
    xp, xs = f("x_prompt"), f("x_sample")
    cp, cs = f("c_prompt"), f("c_sample")
    stc = f("state_conv")
    swk, swv, ptab = f("state_win_k"), f("state_win_v"), f("page_table").astype(np.int32)
    in_maps = []
    for i in range(NCORES):
        sl = slice(i * NS, (i + 1) * NS)
        m = dict(shared)
        m["x_prompt"] = xp[i]
        m["x_sample"] = xs[sl, 0, :]
        m["c_all"] = np.concatenate([cp[i:i + 1], cs[sl]], axis=0)
        m["state_conv"] = np.ascontiguousarray(stc[:, sl])
        m["state_win_k"] = np.ascontiguousarray(swk[:, sl]).reshape(2, NS, 512, 256)
        m["state_win_v"] = np.ascontiguousarray(swv[:, sl]).reshape(2, NS, 512, 256)
        m["page_table"] = np.ascontiguousarray(ptab[sl] % NPOOL) if NPOOL != 2560 else np.ascontiguousarray(ptab[sl])
        in_maps.append(m)
    res = run_bass_kernel_spmd(nc, in_maps, core_ids=list(range(NCORES)))
    R = res.results
    g = lambda name: [np.asarray(r[name]) for r in R]
    y_prompt = np.stack(g("y_prompt"), 0)
    y_sample = np.concatenate(g("y_sample"), 0)[:, None, :]
    conv_p = np.stack(g("conv_p"), 1)
    conv_s = np.concatenate(g("conv_s"), 1)
    chunkv_p = np.stack(g("chunkv_p"), 1)
    chunkv_s = np.concatenate(g("chunkv_s"), 1)[:, :, None, :]
    z = lambda *s: np.zeros(s, np.float32)
    pk = lambda name, T_: np.stack(g(name), 1).reshape(2, NCORES, T_, 4, 64)
    outs = [y_prompt, y_sample, conv_p, conv_s, chunkv_p, chunkv_s,
            pk("cmp_k_p", SEQ), pk("cmp_v_p", SEQ), pk("sel_k_p", SEQ), pk("sel_v_p", SEQ),
            pk("win_k_p", 512), pk("win_v_p", 512),
            *[np.concatenate(g(n), 1).reshape(2, NCORES * NS, 1, 4, 64) for n in ("cmp_k_s", "cmp_v_s", "sel_k_s", "sel_v_s")],
            *[np.concatenate(g(n), 1).reshape(2, NCORES * NS, 512, 4, 64) for n in ("win_k_s", "win_v_s")]]
    return tuple(np.ascontiguousarray(o, dtype=np.float32) for o in outs)
```

```python
import os
import numpy as np
from contextlib import ExitStack
import concourse.bass as bass
import concourse.mybir as mybir
from concourse.bass_utils import run_bass_kernel_spmd

F32 = mybir.dt.float32
BF16 = mybir.dt.bfloat16
I32 = mybir.dt.int32
AF = mybir.ActivationFunctionType
ALU = mybir.AluOpType
AX = mybir.AxisListType

NCORES = 8
D = 1024
SEQ = 2048
NS = 16
DEPTH = 4
EPS = 1e-6
TT = 512
NT = SEQ // TT
HC = SEQ + NS


class Sched:
    SAME_ENGINE_SYNC = True

    def __init__(self, nc, es, n_dma_slots=40):
        self.nc = nc
        self.engs = {'pe': nc.tensor, 'act': nc.scalar, 'dve': nc.vector, 'pool': nc.gpsimd, 'sp': nc.sync}
        self.sem = {}
        self.cnt = {}
        for k in self.engs:
            self.sem[k] = es.enter_context(nc.semaphore("s_" + k))
            self.cnt[k] = 0
        self.ndma = n_dma_slots
        for i in range(n_dma_slots):
            k = ('dma', i)
            self.sem[k] = es.enter_context(nc.semaphore("s_dma%d" % i))
            self.cnt[k] = 0
        self.dma_next = 0
        self.waited = {}
        self.res = {}
        self.n_wait = 0
        self.serial_compute = False
        self.epoch = 0
        self.fence_deps = {}
        self.key_epoch = {}
        self.n_ins = {k: 0 for k in self.engs}

    def _deps(self, reads, writes, e=None):
        deps = {}

        def add(kc):
            if kc is None:
                return
            k, c = kc
            if deps.get(k, 0) < c:
                deps[k] = c
        for r in reads:
            st = self.res.get(r)
            if st:
                add(st[0])
                if not isinstance(r, str) and r[0] == 'ps':
                    for k, c in st[1].items():
                        if k != e:
                            add((k, c))
        for w in writes:
            st = self.res.get(w)
            if st:
                add(st[0])
                for k, c in st[1].items():
                    add((k, c))
            name = w if isinstance(w, str) else w[0]
            if name not in self.PERSIST and self.key_epoch.get(w) != self.epoch:
                self.key_epoch[w] = self.epoch
                for k, c in self.fence_deps.items():
                    add((k, c))
        return deps

    PERSIST = {'x', 'xs', 'h', 'w', 'mod', 'gs', 'pl', 'pfin', 'scT', 'ident32', 'identb', 'trib', 'onesb',
               'eps', 's32_', 's16_', 'rstd_t', 'ps'}

    def fence(self):
        self.epoch += 1
        self.fence_deps = {k: c for k, c in self.cnt.items() if c > 0}

    def _emit_waits(self, e, deps):
        eng = self.engs[e]
        for k, c in deps.items():
            if k == e and (not self.SAME_ENGINE_SYNC or e == 'pe'):
                continue
            if self.waited.get((e, k), 0) >= c:
                continue
            eng.wait_ge(self.sem[k], c)
            self.n_wait += 1
            self.waited[(e, k)] = c

    def _record(self, k, c, reads, writes):
        for r in reads:
            st = self.res.setdefault(r, [None, {}])
            if st[1].get(k, 0) < c:
                st[1][k] = c
        for w in writes:
            self.res[w] = [(k, c), {}]

    SERIAL = set(os.environ.get("K_SERIAL", "").split(",")) - {""}

    def _all(self, deps):
        for k, c in self.cnt.items():
            if c > 0 and deps.get(k, 0) < c:
                deps[k] = c
        return deps

    def op(self, e, fn, reads=(), writes=(), inc=True):
        deps = self._deps(reads, writes, e)
        if e in self.SERIAL or "all" in self.SERIAL:
            deps = self._all(deps)
        if self.serial_compute == 2:
            deps = self._all(deps)
        elif self.serial_compute:
            for k in ('pe', 'act', 'dve'):
                c = self.cnt[k]
                if c > 0 and deps.get(k, 0) < c:
                    deps[k] = c
        self._emit_waits(e, deps)
        ins = fn()
        self.n_ins[e] += 1
        c = self.cnt[e] + 1
        if inc:
            ins.then_inc(self.sem[e], 1)
            self.cnt[e] = c
        self._record(e, c, reads, writes)
        return ins

    def dma(self, out, in_, reads=(), writes=(), q='sp', **kw):
        slot = self.dma_next
        self.dma_next = (self.dma_next + 1) % self.ndma
        k = ('dma', slot)
        deps = self._deps(reads, writes)
        if self.cnt[k] > 0:
            deps[k] = max(deps.get(k, 0), self.cnt[k])
        if "dma" in self.SERIAL or "all" in self.SERIAL or ("dma" + q) in self.SERIAL or self.serial_compute == 2:
            deps = self._all(deps)
        self._emit_waits(q, deps)
        ins = self.engs[q].dma_start(out=out, in_=in_, **kw)
        self.n_ins[q] += 1
        c = self.cnt[k] + 16
        ins.then_inc(self.sem[k], 16)
        self.cnt[k] = c
        self._record(k, c, reads, writes)
        return ins

    def dma_ind(self, out, in_, idx_ap, reads=(), writes=()):
        import concourse.bass as _b
        slot = self.dma_next
        self.dma_next = (self.dma_next + 1) % self.ndma
        k = ('dma', slot)
        deps = self._deps(reads, writes)
        if self.cnt[k] > 0:
            deps[k] = max(deps.get(k, 0), self.cnt[k])
        if self.serial_compute == 2 or "all" in self.SERIAL:
            deps = self._all(deps)
        self._emit_waits('pool', deps)
        ins = self.engs['pool'].indirect_dma_start(out=out, out_offset=None, in_=in_,
                                                   in_offset=_b.IndirectOffsetOnAxis(ap=idx_ap, axis=0))
        self.n_ins['pool'] += 1
        c = self.cnt[k] + 16
        ins.then_inc(self.sem[k], 16)
        self.cnt[k] = c
        self._record(k, c, reads, writes)
        return ins

    def finish(self):
        eng = self.engs['sp']
        for k, c in self.cnt.items():
            if c > 0 and k != 'sp' and self.waited.get(('sp', k), 0) < c:
                eng.wait_ge(self.sem[k], c)


class Ring:
    uid = 0

    def __init__(self, nc, es, name, n, shape, dt):
        Ring.uid += 1
        self.t = [es.enter_context(nc.sbuf_tensor("rg%d_%s%d" % (Ring.uid, name, i), shape, dt)) for i in range(n)]
        self.name = name
        self.i = 0

    def get(self):
        i = self.i
        self.i = (i + 1) % len(self.t)
        return self.t[i], (self.name, i)


def build_program(n_layers=DEPTH, do_odd=True):
    nc = bass.Bass("TRN2", target_bir_lowering=False)

    def din(name, shape, dt=F32):
        return nc.dram_tensor(name, list(shape), dt, kind="ExternalInput").ap()

    def dout(name, shape, dt=F32):
        return nc.dram_tensor(name, list(shape), dt, kind="ExternalOutput").ap()

    x_prompt = din("x_prompt", [SEQ, D])
    x_sample = din("x_sample", [NS, D])
    c_all = din("c_all", [1 + NS, D])
    state_conv = din("state_conv", [2, NS, 30, 512])
    ada_w = din("ada_w", [DEPTH, D, 6 * D])
    ffn_w1 = din("ffn_w1", [DEPTH, D, 4 * D])
    ffn_w2 = din("ffn_w2", [DEPTH, 4 * D, D])
    even_w_in = din("even_w_in", [2, D, 2048])
    even_w_out = din("even_w_out", [2, D, D])
    p_layer = din("p_layer", [DEPTH, 128, 64])
    p_even = din("p_even", [2, 128, 136])
    p_final = din("p_final", [128, 8])
    conv_w = din("conv_w", [2, 31, 512])
    conv_b = din("conv_b", [2, 512])
    conv_ln_g = din("conv_ln_g", [2, 512])
    conv_ln_b = din("conv_ln_b", [2, 512])
    sgu_ln_g = din("sgu_ln_g", [2, 512])
    sgu_ln_b = din("sgu_ln_b", [2, 512])
    sgu_wT = din("sgu_wT", [2, 128, 4, 128])
    sgu_b = din("sgu_b", [2, 4, 128])
    cst = din("cst", [128, 256])
    odd_w_in = din("odd_w_in", [2, D, 2608])
    odd_w_out = din("odd_w_out", [2, D, D])
    cmp_pe = [din("cmp_pe_k", [2, 32, 64]), din("cmp_pe_v", [2, 32, 64])]
    cmp_w1 = [din("cmp_w1_k", [2, 32, 64, 128]), din("cmp_w1_v", [2, 32, 64, 128])]
    cmp_w2 = [din("cmp_w2_k", [2, 128, 64]), din("cmp_w2_v", [2, 128, 64])]
    rope_cos = din("rope_cos", [SEQ + 1, 64])
    rope_sin = din("rope_sin", [SEQ + 1, 64])
    selA = din("selA", [SEQ, 32])
    selB = din("selB", [SEQ, 32])
    cmpb_d = din("cmpb", [127, SEQ])
    bandb_d = din("bandb", [128, 4, 256])
    Emat_d = din("Emat", [32, 16, 128])
    ov_d = din("ov", [127, 32])
    ov33_d = din("ov33", [127, 33])
    selA_s = din("selA_s", [1, 33])
    selB_s = din("selB_s", [1, 33])
    nb16_d = din("nb16", [16, 16])
    poff_d = din("poff", [128, 2])
    page_table = din("page_table", [NS, 16], I32)
    NPOOL = int(os.environ.get("K_POOLPAGES", 2560))
    caches = [din(n, [2 * NPOOL * 128, 256]) for n in ("cache_cmp_k", "cache_cmp_v", "cache_sel_k", "cache_sel_v")]
    state_win = [din(n, [2, NS, 512, 256]) for n in ("state_win_k", "state_win_v")]

    y_prompt = dout("y_prompt", [SEQ, D])
    y_sample = dout("y_sample", [NS, D])
    conv_p = dout("conv_p", [2, 30, 512])
    conv_s = dout("conv_s", [2, NS, 30, 512])
    chunkv_p = dout("chunkv_p", [2, 128, 512])
    chunkv_s = dout("chunkv_s", [2, NS, 512])
    nsa_p = [dout(n, [2, SEQ, 256]) for n in ("cmp_k_p", "cmp_v_p", "sel_k_p", "sel_v_p")]
    win_p = [dout(n, [2, 512, 256]) for n in ("win_k_p", "win_v_p")]
    nsa_s = [dout(n, [2, NS, 256]) for n in ("cmp_k_s", "cmp_v_s", "sel_k_s", "sel_v_s")]
    win_s = [dout(n, [2, NS, 512, 256]) for n in ("win_k_s", "win_v_s")]

    DBG = bool(os.environ.get("K_DBG"))
    with ExitStack() as es:
        S = Sched(nc, es)

        from contextlib import contextmanager

        @contextmanager
        def phase():
            S.fence()
            with ExitStack() as st_:
                yield st_
            S.fence()

        uid = [0]

        def sb(name, shape, dt, stack=es):
            uid[0] += 1
            return stack.enter_context(nc.sbuf_tensor("sb%d_%s" % (uid[0], name), list(shape), dt))

        xT = sb("xT", [128, 8, SEQ], F32)
        xsT = sb("xsT", [128, 8, NS], F32)
        hT = sb("hT", [128, 8, HC], BF16)
        NSLOT = 4
        wring = [sb("wr%d" % i, [128, 4096], BF16) for i in range(NSLOT)]
        mod = [sb("mod%d" % i, [128, 48, 17], F32) for i in range(2)]
        gsb = [sb("gs%d" % i, [128, 8, 17], F32) for i in range(2)]
        pl = sb("pl", [128, DEPTH, 64], F32)
        pfin = sb("pfin", [128, 8], F32)
        scT = sb("scT", [128, 8, 17], BF16)
        ident32 = sb("ident32", [128, 128], F32)
        identb = sb("identb", [128, 128], BF16)
        trib = sb("trib", [128, 128], BF16)
        onesb = sb("onesb", [128, 128], BF16)
        eps_t = sb("eps_t", [128, 1], F32)
        s32 = Ring(nc, es, "s32_", 6, [128, 512], F32)
        rstd_t = sb("rstd_t", [128, 512], F32)
        s16 = Ring(nc, es, "s16_", 3, [128, 512], BF16)
        ps_t = [es.enter_context(nc.psum_tensor("ps%d" % i, [128, 512], F32)) for i in range(8)]
        ps_i = [0]

        def PS():
            i = ps_i[0]
            ps_i[0] = (i + 1) % 8
            return ps_t[i], ('ps', i)

        V, A, G, T = nc.vector, nc.scalar, nc.gpsimd, nc.tensor

        def mm(out, lhsT, rhs, start, stop, reads, writes, inc=None, **kw):
            return S.op('pe', lambda: T.matmul(out, lhsT=lhsT, rhs=rhs, start=start, stop=stop, **kw),
                        reads=reads, writes=writes, inc=(stop if inc is None else inc))

        def tr(out, in_, ident, reads, writes, inc=True):
            return S.op('pe', lambda: T.transpose(out, in_, ident), reads=reads, writes=writes, inc=inc)

        wplan = []
        wstate = {'issued': 0, 'used': 0}

        def w_parts(entry):
            return entry if isinstance(entry, list) else [(0, 128, entry)]

        def w_view(i, p0=0, p1=128):
            parts = w_parts(wplan[i])
            a, b = parts[0][2].shape[1], parts[0][2].shape[2]
            slot = i % NSLOT
            return wring[slot][p0:p1, 0:a * b].rearrange("p (a b) -> p a b", a=a)

        def w_issue_upto(n):
            while wstate['issued'] < min(n, len(wplan)):
                i = wstate['issued']
                slot = i % NSLOT
                for pi, (p0, p1, src) in enumerate(w_parts(wplan[i])):
                    S.dma(w_view(i, p0, p1), src, writes=[('w', slot, pi), ('w', slot, 'r')], q='pool')
                wstate['issued'] += 1

        def w_next():
            i = wstate['used']
            wstate['used'] += 1
            w_issue_upto(i + NSLOT - 1)
            slot = i % NSLOT
            keys = [('w', slot, 'r')] + [('w', slot, pi) for pi in range(len(w_parts(wplan[i])))]
            return w_view(i), keys

        def wview(W2d):
            return W2d.rearrange("(kc p) n -> p kc n", p=128)

        def plan_ada(l):
            v = wview(ada_w[l])
            return [v[:, :, b * 512:(b + 1) * 512] for b in range(12)]

        def plan_even(e):
            vi = wview(even_w_in[e])
            vo = wview(even_w_out[e])
            out = []
            for tt in range(NT + 1):
                out += [vi[:, :, b * 512:(b + 1) * 512] for b in range(4)]
                out += [vo[:, 0:4, :], vo[:, 4:8, :]]
            return out

        def plan_mlp(l):
            v1 = wview(ffn_w1[l])
            v2 = wview(ffn_w2[l])
            out = []
            nada = 0
            for j in range(8):
                out += [v1[:, :, j * 512:(j + 1) * 512], v2[:, 4 * j:4 * j + 4, :]]
                if l + 1 < n_layers:
                    tgt = (12 * (j + 1)) // 8
                    pa = plan_ada(l + 1)
                    out += pa[nada:tgt]
                    nada = tgt
            return out

        TQ = 256
        NQ = SEQ // TQ
        OST = int(os.environ.get('K_OST', 6))
        SST = int(os.environ.get('K_SST', 5))
        P1 = int(os.environ.get('K_P1', 4))
        SBLK = [(0, 512), (512, 512), (1024, 512), (1536, 512), (2048, 512), (2560, 48)]

        def plan_odd(o):
            vi = wview(odd_w_in[o])
            out = []
            for tt in range(NT if OST >= 1 else 0):
                out += [vi[:, :, 1024 + b * 512:1024 + (b + 1) * 512] for b in range(3 if OST != 1 else int(os.environ.get('K_NBLK', 3)))]
            for qt in range(NQ if OST >= 3 else 0):
                out += [vi[:, :, 0:512], vi[:, :, 512:1024], vi[:, :, 2560:2608]]
                for b in range(2 if OST >= 5 else 0):
                    parts = []
                    for half in range(2):
                        r0 = (8 * b + 4 * half) * 64
                        parts.append((half * 64, half * 64 + 64, odd_w_out[o, r0:r0 + 256, :].rearrange("(m d) n -> d m n", d=64)))
                    out.append(parts)
            if OST >= 6:
                for (c0, n) in SBLK:
                    out.append(vi[:, :, c0:c0 + n])
                for b in range(2 if SST >= 5 else 0):
                    parts = []
                    for half in range(2):
                        r0 = (8 * b + 4 * half) * 64
                        parts.append((half * 64, half * 64 + 64, odd_w_out[o, r0:r0 + 256, :].rearrange("(m d) n -> d m n", d=64)))
                    out.append(parts)
            return out

        wplan += plan_ada(0)
        for l in range(n_layers):
            if l % 2 == 0:
                wplan += plan_even(l // 2)
            elif do_odd:
                wplan += plan_odd(l // 2)
            wplan += plan_mlp(l)

        S.dma(ident32[:], cst[:, 0:128], writes=['ident32'])
        S.dma(identb[:], cst[:, 0:128], writes=['identb'], q='pool')
        S.dma(trib[:], cst[:, 128:256], writes=['trib'], q='pool')
        S.dma(pl[:], p_layer.rearrange("l p c -> p l c"), writes=['pl'])
        S.dma(pfin[:], p_final, writes=['pfin'])
        S.op('dve', lambda: V.memset(onesb[:], 1.0), writes=['onesb'])
        S.op('dve', lambda: V.memset(eps_t[:], EPS), writes=['eps'])
        w_issue_upto(NSLOT - 1)

        with phase() as ph:
            tok = [sb("tok%d" % i, [128, D], F32, ph) for i in range(2)]
            for tt in range(SEQ // 128):
                tk = tok[tt % 2]
                key = ('tok', tt % 2)
                S.dma(tk[:], x_prompt[tt * 128:(tt + 1) * 128, :], writes=[key])
                for half in range(2):
                    ps, pk = PS()
                    for q in range(4):
                        c = half * 4 + q
                        tr(ps[:, q * 128:(q + 1) * 128], tk[:, c * 128:(c + 1) * 128], ident32[:],
                           reads=[key, 'ident32'], writes=[pk], inc=(q == 3))
                    dst = xT[:, half * 4:half * 4 + 4, tt * 128:(tt + 1) * 128]
                    src = ps[:].rearrange("p (a b) -> p a b", a=4)
                    if half == 0:
                        S.op('act', lambda: A.copy(out=dst, in_=src), reads=[pk], writes=[('x', tt // 4, half)])
                    else:
                        S.op('dve', lambda: V.tensor_copy(out=dst, in_=src), reads=[pk], writes=[('x', tt // 4, half)])
            tks = sb("toks", [NS, D], F32, ph)
            S.dma(tks[:], x_sample, writes=['toks'])
            ps, pk = PS()
            for c in range(8):
                tr(ps[:, c * NS:(c + 1) * NS], tks[:, c * 128:(c + 1) * 128], ident32[0:NS, 0:NS],
                   reads=['toks', 'ident32'], writes=[pk], inc=(c == 7))
            S.op('dve', lambda: V.tensor_copy(out=xsT[:], in_=ps[:, 0:8 * NS].rearrange("p (a b) -> p a b", a=8)),
                 reads=[pk], writes=['xs'])
            ctk = sb("ctk", [17, D], F32, ph)
            S.dma(ctk[:], c_all, writes=['ctk'])
            S.op('act', lambda: A.activation(out=ctk[:], in_=ctk[:], func=AF.Silu), reads=['ctk'], writes=['ctk'])
            ps, pk = PS()
            for c in range(8):
                tr(ps[:, c * 17:(c + 1) * 17], ctk[:, c * 128:(c + 1) * 128], ident32[0:17, 0:17],
                   reads=['ctk', 'ident32'], writes=[pk], inc=(c == 7))
            S.op('dve', lambda: V.tensor_copy(out=scT[:], in_=ps[:, 0:8 * 17].rearrange("p (a b) -> p a b", a=8)),
                 reads=[pk], writes=['scT'])
            xkeys = [[('x', t, 0), ('x', t, 1)] for t in range(NT)]

            def ada_block(l, b):
                wt, wk = w_next()
                m = mod[l % 2]
                ps, pk = PS()
                for oc in range(4):
                    for k in range(8):
                        mm(ps[:, oc * 17:(oc + 1) * 17], wt[:, k, oc * 128:(oc + 1) * 128], scT[:, k, :],
                           k == 0, k == 7, reads=wk + ['scT'], writes=[pk])
                bias = pl[:, l, 4 * b:4 * b + 4].unsqueeze(2).to_broadcast([128, 4, 17])
                S.op('dve', lambda: V.tensor_tensor(out=m[:, 4 * b:4 * b + 4, :],
                                                     in0=ps[:, 0:68].rearrange("p (a b) -> p a b", a=4),
                                                     in1=bias, op=ALU.add),
                     reads=[pk, 'pl'], writes=[('mod', l % 2)])

            def ada_finish(l):
                m = mod[l % 2]
                for which in range(2):
                    sc = m[:, (1 + 3 * which) * 8:(2 + 3 * which) * 8, :]
                    g = pl[:, l, 48 + 8 * which:56 + 8 * which].unsqueeze(2).to_broadcast([128, 8, 17])
                    S.op('dve', lambda: V.scalar_tensor_tensor(out=gsb[which][:], in0=sc, scalar=1.0, in1=g,
                                                                op0=ALU.add, op1=ALU.mult),
                         reads=[('mod', l % 2), 'pl'], writes=[('gs', which)])

            for b in range(12):
                ada_block(0, b)

        def norm_tile(l, which, tt, dst, dkeys):
            m = mod[l % 2]
            gs = gsb[which]
            sh = (3 * which) * 8
            cols = slice(tt * TT, (tt + 1) * TT)
            ps, pk = PS()
            for c in range(8):
                sq, sqk = s16.get()
                S.op('act', lambda: A.activation(out=sq[:], in_=xT[:, c, cols], func=AF.Square),
                     reads=xkeys[tt], writes=[sqk])
                mm(ps[:], onesb[:], sq[:], c == 0, c == 7, reads=[sqk, 'onesb'], writes=[pk], inc=True)
            rs, rsk = rstd_t, 'rstd_t'
            S.op('act', lambda: A.activation(out=rs[:], in_=ps[:], func=AF.Sqrt, bias=eps_t[:, 0:1], scale=1.0 / D),
                 reads=[pk, 'eps'], writes=[rsk])
            S.op('dve', lambda: V.reciprocal(out=rs[:], in_=rs[:]), reads=[rsk], writes=[rsk])
            for c in range(8):
                t, tk = s32.get()
                S.op('dve', lambda: V.tensor_tensor(out=t[:], in0=xT[:, c, cols], in1=rs[:], op=ALU.mult),
                     reads=xkeys[tt] + [rsk], writes=[tk])
                S.op('act', lambda: A.activation(out=dst[:, c, :], in_=t[:], func=AF.Identity,
                                                  scale=gs[:, c, 0:1], bias=m[:, sh + c, 0:1]),
                     reads=[tk, ('gs', which), ('mod', l % 2)], writes=[dkeys[c]])

        def norm_samples(l, which):
            m = mod[l % 2]
            gs = gsb[which]
            sh = (3 * which) * 8
            ps, pk = PS()
            sq, sqk = s16.get()
            S.op('act', lambda: A.activation(out=sq[:, 0:8 * NS], in_=xsT[:].rearrange("p a b -> p (a b)"), func=AF.Square),
                 reads=['xs'], writes=[sqk])
            for c in range(8):
                mm(ps[:, 0:NS], onesb[:], sq[:, c * NS:(c + 1) * NS], c == 0, c == 7, reads=[sqk, 'onesb'], writes=[pk])
            rs, rsk = s32.get()
            S.op('act', lambda: A.activation(out=rs[:, 0:NS], in_=ps[:, 0:NS], func=AF.Sqrt, bias=eps_t[:, 0:1], scale=1.0 / D),
                 reads=[pk, 'eps'], writes=[rsk])
            S.op('dve', lambda: V.reciprocal(out=rs[:, 0:NS], in_=rs[:, 0:NS]), reads=[rsk], writes=[rsk])
            t, tk = s32.get()
            tv = t[:, 0:8 * NS].rearrange("p (a b) -> p a b", a=8)
            S.op('dve', lambda: V.tensor_tensor(out=tv, in0=xsT[:], in1=rs[:, 0:NS].unsqueeze(1).to_broadcast([128, 8, NS]), op=ALU.mult),
                 reads=['xs', rsk], writes=[tk])
            S.op('dve', lambda: V.tensor_tensor(out=tv, in0=tv, in1=gs[:, :, 1:17], op=ALU.mult),
                 reads=[tk, ('gs', which)], writes=[tk])
            S.op('dve', lambda: V.tensor_tensor(out=hT[:, :, SEQ:HC], in0=tv, in1=m[:, sh:sh + 8, 1:17], op=ALU.add),
                 reads=[tk, ('mod', l % 2)], writes=[('h', 's')])

        def norm_pass(l, which):
            for tt in range(NT):
                norm_tile(l, which, tt, hT[:, :, tt * TT:(tt + 1) * TT], [('h', tt, c) for c in range(8)])
            norm_samples(l, which)

        hkeys = [[('h', t, c) for c in range(8)] for t in range(NT)]

        def mlp(l):
            m = mod[l % 2]
            g2 = 5 * 8
            nada = 0
            with phase() as ph:
                hid = [sb("hid%d" % i, [128, 4, TT], BF16, ph) for i in range(2)]
                hi = 0
                for j in range(8):
                    w1, w1k = w_next()
                    w2, w2k = w_next()
                    for tt in range(NT + 1):
                        samp = tt == NT
                        n = NS if samp else TT
                        cols = slice(SEQ, HC) if samp else slice(tt * TT, (tt + 1) * TT)
                        hk = [('h', 's')] if samp else hkeys[tt]
                        hd = hid[hi % 2]
                        hdk = ('hid', hi % 2)
                        hi += 1
                        for jj in range(4):
                            ps, pk = PS()
                            for k in range(8):
                                mm(ps[:, 0:n], w1[:, k, jj * 128:(jj + 1) * 128], hT[:, k, cols], k == 0, k == 7,
                                   reads=w1k + hk, writes=[pk])
                            r, rk = s32.get()
                            S.op('act', lambda: A.activation(out=r[:, 0:n], in_=ps[:, 0:n], func=AF.Relu), reads=[pk], writes=[rk])
                            S.op('dve', lambda: V.tensor_tensor(out=hd[:, jj, 0:n], in0=r[:, 0:n], in1=r[:, 0:n], op=ALU.mult),
                                 reads=[rk], writes=[hdk])
                        if samp:
                            ps, pk = PS()
                            for i in range(8):
                                for jj in range(4):
                                    mm(ps[:, i * NS:(i + 1) * NS], w2[:, jj, i * 128:(i + 1) * 128], hd[:, jj, 0:NS], jj == 0, jj == 3,
                                       reads=w2k + [hdk], writes=[pk])
                            t, tk = s32.get()
                            tv = t[:, 0:8 * NS].rearrange("p (a b) -> p a b", a=8)
                            S.op('dve', lambda: V.tensor_tensor(out=tv, in0=ps[:, 0:8 * NS].rearrange("p (a b) -> p a b", a=8),
                                                                 in1=m[:, g2:g2 + 8, 1:17], op=ALU.mult),
                                 reads=[pk, ('mod', l % 2)], writes=[tk])
                            S.op('dve', lambda: V.tensor_tensor(out=xsT[:], in0=xsT[:], in1=tv, op=ALU.add),
                                 reads=[tk, 'xs'], writes=['xs'])
                        else:
                            for i in range(8):
                                ps, pk = PS()
                                for jj in range(4):
                                    mm(ps[:], w2[:, jj, i * 128:(i + 1) * 128], hd[:, jj, :], jj == 0, jj == 3,
                                       reads=w2k + [hdk], writes=[pk])
                                S.op('dve', lambda: V.scalar_tensor_tensor(out=xT[:, i, cols], in0=ps[:], scalar=m[:, g2 + i, 0:1],
                                                                            in1=xT[:, i, cols], op0=ALU.mult, op1=ALU.add),
                                     reads=[pk, ('mod', l % 2)] + xkeys[tt], writes=[xkeys[tt][i // 4]])
                    if l + 1 < n_layers:
                        tgt = (12 * (j + 1)) // 8
                        while nada < tgt:
                            ada_block(l + 1, nada)
                            nada += 1

        def even_layer(l):
            e = l // 2
            m = mod[l % 2]
            g1 = 2 * 8
            with phase() as ph:
                pe = sb("pe", [128, 136], F32, ph)
                S.dma(pe[:], p_even[e], writes=['pe'])
                gbc = sb("gbc", [128, 512], F32, ph)
                bbc = sb("bbc", [128, 512], F32, ph)
                S.dma(gbc[:], sgu_ln_g[e:e + 1, :].to_broadcast([128, 512]), writes=['gbc'])
                S.dma(bbc[:], sgu_ln_b[e:e + 1, :].to_broadcast([128, 512]), writes=['bbc'])
                wmT = sb("wmT", [128, 4, 128], BF16, ph)
                S.dma(wmT[:], sgu_wT[e], writes=['wmT'], q='pool')
                S.op('dve', lambda: V.tensor_tensor(out=wmT[:], in0=wmT[:], in1=trib[:].unsqueeze(1).to_broadcast([128, 4, 128]), op=ALU.mult),
                     reads=['wmT', 'trib'], writes=['wmT'])
                bsb = sb("bsb", [1, 4, 128], BF16, ph)
                S.dma(bsb[:], sgu_b[e:e + 1], writes=['bsb'], q='pool')
                stat = sb("stat", [128, 8], F32, ph)

                with phase() as pa:
                    cg = sb("cg", [NS, 512], F32, pa)
                    cb = sb("cb", [NS, 512], F32, pa)
                    w30 = sb("w30", [NS, 512], F32, pa)
                    cpre = sb("cpre", [NS, 512], F32, pa)
                    sw0 = sb("sw0", [NS, 4], F32, pa)
                    sb0 = sb("sb0", [NS, 4], F32, pa)
                    S.dma(cg[:], conv_ln_g[e:e + 1, :].to_broadcast([NS, 512]), writes=['cg'])
                    S.dma(cb[:], conv_ln_b[e:e + 1, :].to_broadcast([NS, 512]), writes=['cb'])
                    S.dma(w30[:], conv_w[e, 30:31, :].to_broadcast([NS, 512]), writes=['w30'])
                    S.dma(cpre[:], conv_b[e:e + 1, :].to_broadcast([NS, 512]), writes=['cpre'])
                    S.dma(sw0[:], sgu_wT[e, 0:1, :, 0].to_broadcast([NS, 4]), writes=['sw0'], allow_slow_non_contiguous=True)
                    S.dma(sb0[:], sgu_b[e, :, 0].unsqueeze(0).to_broadcast([NS, 4]), writes=['sb0'], allow_slow_non_contiguous=True)
                    zl = sb("zl", [NS, 512], F32, pa)
                    a_s = sb("a_s", [NS, 512], F32, pa)
                    cv = sb("cv", [NS, 512], F32, pa)
                    us = sb("us", [NS, 512], F32, pa)
                    vs = sb("vs", [NS, 512], F32, pa)
                    abs_ = sb("abs", [NS, 1024], F32, pa)
                    abT = sb("abT", [128, 8, NS], BF16, pa)
                    st = sb("st", [NS, 30, 64], F32, pa)
                    wb_ = sb("wbc", [NS, 30, 64], F32, pa)
                    red = sb("red", [NS, 64], F32, pa)
                    for oc in range(8):
                        cs = slice(oc * 64, (oc + 1) * 64)
                        S.dma(st[:], state_conv[e, :, :, cs], writes=['st'])
                        S.dma(wb_[:], conv_w[e:e + 1, 0:30, cs].to_broadcast([NS, 30, 64]), writes=['wbc'])
                        S.op('dve', lambda: V.tensor_tensor(out=st[:], in0=st[:], in1=wb_[:], op=ALU.mult), reads=['st', 'wbc'], writes=['st'])
                        S.op('dve', lambda: V.tensor_reduce(out=red[:], in_=st[:].rearrange("p k c -> p c k"), axis=AX.X, op=ALU.add),
                             reads=['st'], writes=['red'])
                        S.op('dve', lambda: V.tensor_tensor(out=cpre[:, cs], in0=cpre[:, cs], in1=red[:], op=ALU.add),
                             reads=['red', 'cpre'], writes=['cpre'])
                    S.dma(conv_s[e, :, 0:29, :], state_conv[e, :, 1:30, :])

                    def zproj():
                        Wb, Wbk = w_next()
                        ps, pk = PS()
                        for k in range(8):
                            mm(ps[0:NS, :], hT[:, k, SEQ:HC], Wb[:, k, :], k == 0, k == 7, reads=Wbk + [('h', 's')], writes=[pk])
                        return ps, pk
                    ps, pk = zproj()
                    S.op('act', lambda: A.copy(out=zl[:], in_=ps[0:NS, :]), reads=[pk], writes=['zl'])
                    ps, pk = zproj()
                    S.op('act', lambda: A.activation(out=a_s[:], in_=ps[0:NS, :], func=AF.Sigmoid), reads=[pk], writes=['a_s'])
                    S.op('dve', lambda: V.tensor_tensor(out=a_s[:], in0=zl[:], in1=a_s[:], op=ALU.mult), reads=['zl', 'a_s'], writes=['a_s'])
                    S.dma(conv_s[e, :, 29, :], a_s[:], reads=['a_s'])
                    S.op('dve', lambda: V.tensor_tensor(out=cv[:], in0=a_s[:], in1=w30[:], op=ALU.mult), reads=['a_s', 'w30'], writes=['cv'])
                    S.op('dve', lambda: V.tensor_tensor(out=cv[:], in0=cv[:], in1=cpre[:], op=ALU.add), reads=['cv', 'cpre'], writes=['cv'])

                    def ln_tok(t, tk, gsrc, bsrc, gkey, bkey):
                        S.op('dve', lambda: V.bn_stats(out=stat[0:NS, 0:6], in_=t[:]), reads=[tk], writes=['stat'])
                        S.op('dve', lambda: V.bn_aggr(out=stat[0:NS, 6:8], in_=stat[0:NS, 0:6]), reads=['stat'], writes=['stat'])
                        S.op('act', lambda: A.activation(out=stat[0:NS, 7:8], in_=stat[0:NS, 7:8], func=AF.Sqrt, bias=eps_t[0:NS, 0:1], scale=1.0),
                             reads=['stat', 'eps'], writes=['stat'])
                        S.op('dve', lambda: V.reciprocal(out=stat[0:NS, 7:8], in_=stat[0:NS, 7:8]), reads=['stat'], writes=['stat'])
                        S.op('dve', lambda: V.tensor_scalar(out=t[:], in0=t[:], scalar1=stat[0:NS, 6:7], scalar2=stat[0:NS, 7:8],
                                                             op0=ALU.subtract, op1=ALU.mult), reads=[tk, 'stat'], writes=[tk])
                        S.op('dve', lambda: V.tensor_tensor(out=t[:], in0=t[:], in1=gsrc, op=ALU.mult), reads=[tk, gkey], writes=[tk])
                        S.op('dve', lambda: V.tensor_tensor(out=t[:], in0=t[:], in1=bsrc, op=ALU.add), reads=[tk, bkey], writes=[tk])

                    ln_tok(cv, 'cv', cg[:], cb[:], 'cg', 'cb')
                    S.op('act', lambda: A.activation(out=abs_[:, 0:512], in_=cv[:], func=AF.Silu), reads=['cv'], writes=['abs'])
                    ps, pk = zproj()
                    S.op('act', lambda: A.activation(out=us[:], in_=ps[0:NS, :], func=AF.Gelu_apprx_tanh), reads=[pk], writes=['us'])
                    ps, pk = zproj()
                    S.op('act', lambda: A.activation(out=vs[:], in_=ps[0:NS, :], func=AF.Gelu_apprx_tanh), reads=[pk], writes=['vs'])
                    ln_tok(vs, 'vs', gbc[0:NS, :], bbc[0:NS, :], 'gbc', 'bbc')
                    S.dma(chunkv_s[e], vs[:], reads=['vs'])
                    vs3 = vs[:].rearrange("p (g d) -> p g d", g=4)
                    S.op('dve', lambda: V.tensor_tensor(out=vs3, in0=vs3, in1=sw0[:].unsqueeze(2).to_broadcast([NS, 4, 128]), op=ALU.mult),
                         reads=['vs', 'sw0'], writes=['vs'])
                    S.op('dve', lambda: V.tensor_tensor(out=vs3, in0=vs3, in1=sb0[:].unsqueeze(2).to_broadcast([NS, 4, 128]), op=ALU.add),
                         reads=['vs', 'sb0'], writes=['vs'])
                    S.op('dve', lambda: V.tensor_tensor(out=abs_[:, 512:1024], in0=us[:], in1=vs[:], op=ALU.mult),
                         reads=['us', 'vs', 'abs'], writes=['abs'])
                    ps, pk = PS()
                    for c in range(8):
                        tr(ps[:, c * NS:(c + 1) * NS], abs_[:, c * 128:(c + 1) * 128], ident32[0:NS, 0:NS],
                           reads=['abs', 'ident32'], writes=[pk], inc=(c == 7))
                    S.op('dve', lambda: V.tensor_copy(out=abT[:], in_=ps[:, 0:8 * NS].rearrange("p (a b) -> p a b", a=8)), reads=[pk], writes=['abT'])
                    Wo1, Wo1k = w_next()
                    Wo2, Wo2k = w_next()
                    ps, pk = PS()
                    for i in range(8):
                        for k in range(8):
                            Wo, Wok = (Wo1, Wo1k) if k < 4 else (Wo2, Wo2k)
                            mm(ps[:, i * NS:(i + 1) * NS], Wo[:, k % 4, i * 128:(i + 1) * 128], abT[:, k, :], k == 0, k == 7,
                               reads=Wok + ['abT'], writes=[pk])
                    t, tk = s32.get()
                    tv = t[:, 0:8 * NS].rearrange("p (a b) -> p a b", a=8)
                    S.op('dve', lambda: V.tensor_tensor(out=tv, in0=ps[:, 0:8 * NS].rearrange("p (a b) -> p a b", a=8),
                                                         in1=m[:, g1:g1 + 8, 1:17], op=ALU.mult), reads=[pk, ('mod', l % 2)], writes=[tk])
                    S.op('dve', lambda: V.tensor_tensor(out=xsT[:], in0=xsT[:], in1=tv, op=ALU.add), reads=[tk, 'xs'], writes=['xs'])

                diag = [sb("diag%d" % i, [128, 31, 128], BF16, ph) for i in range(2)]
                apad1 = sb("apad", [128, 4, 30 + TT], BF16, ph)
                apad = [apad1, apad1]
                mean = sb("mean_t", [128, 512], F32, ph)
                msq = sb("msq_t", [128, 512], F32, ph)
                mk, msk = 'mean_t', 'msq_t'
                u_t = sb("u_t", [128, 4, TT], BF16, ph)
                bo_t = sb("bo_t", [128, 4, TT], BF16, ph)
                ao_t = sb("ao_t", [128, 4, TT], BF16, ph)
                acv = sb("acv", [128, 4, TT], BF16, ph)
                vb = sb("vb", [128, 512], BF16, ph)
                a32 = sb("a32", [128, 4, 32], F32, ph)
                S.op('dve', lambda: V.memset(apad1[:, :, 0:30], 0.0), writes=['apad'])
                ndiag = [0]

                for tt in range(NT):
                    t0 = tt * TT
                    cols = slice(t0, t0 + TT)
                    hk = hkeys[tt]
                    ap_t = apad[tt % 2]
                    apk = 'apad'
                    Wa, Wak = w_next()
                    Wg, Wgk = w_next()
                    for oc in range(4):
                        psl, plk = PS()
                        for k in range(8):
                            mm(psl[:], Wa[:, k, oc * 128:(oc + 1) * 128], hT[:, k, cols], k == 0, k == 7, reads=Wak + hk, writes=[plk])
                        psg, pgk = PS()
                        for k in range(8):
                            mm(psg[:], Wg[:, k, oc * 128:(oc + 1) * 128], hT[:, k, cols], k == 0, k == 7, reads=Wgk + hk, writes=[pgk])
                        sg, sgk = s32.get()
                        S.op('act', lambda: A.activation(out=sg[:], in_=psg[:], func=AF.Sigmoid), reads=[pgk], writes=[sgk])
                        S.op('dve', lambda: V.tensor_tensor(out=ap_t[:, oc, 30:30 + TT], in0=psl[:], in1=sg[:], op=ALU.mult),
                             reads=[plk, sgk], writes=[apk])
                        if tt == NT - 1:
                            S.op('dve', lambda: V.tensor_tensor(out=a32[:, oc, 0:32], in0=psl[:, TT - 32:TT], in1=sg[:, TT - 32:TT], op=ALU.mult),
                                 reads=[plk, sgk], writes=['a32'])
                    Wu, Wuk = w_next()
                    for oc in range(4):
                        ps, pk = PS()
                        for k in range(8):
                            mm(ps[:], Wu[:, k, oc * 128:(oc + 1) * 128], hT[:, k, cols], k == 0, k == 7, reads=Wuk + hk, writes=[pk])
                        S.op('act', lambda: A.activation(out=u_t[:, oc, :], in_=ps[:], func=AF.Gelu_apprx_tanh), reads=[pk], writes=['u_t'])
                    Wv, Wvk = w_next()
                    for sub in range(4):
                        c0 = t0 + sub * 128
                        ps, pk = PS()
                        for k in range(8):
                            mm(ps[:], hT[:, k, c0:c0 + 128], Wv[:, k, :], k == 0, k == 7, reads=Wvk + hk, writes=[pk])
                        gv, gvk = s32.get()
                        S.op('act', lambda: A.activation(out=gv[:], in_=ps[:], func=AF.Gelu_apprx_tanh), reads=[pk], writes=[gvk])
                        S.op('dve', lambda: V.bn_stats(out=stat[:, 0:6], in_=gv[:]), reads=[gvk], writes=['stat'])
                        S.op('dve', lambda: V.bn_aggr(out=stat[:, 6:8], in_=stat[:, 0:6]), reads=['stat'], writes=['stat'])
                        S.op('act', lambda: A.activation(out=stat[:, 7:8], in_=stat[:, 7:8], func=AF.Sqrt, bias=eps_t[:, 0:1], scale=1.0),
                             reads=['stat', 'eps'], writes=['stat'])
                        S.op('dve', lambda: V.reciprocal(out=stat[:, 7:8], in_=stat[:, 7:8]), reads=['stat'], writes=['stat'])
                        S.op('dve', lambda: V.tensor_scalar(out=gv[:], in0=gv[:], scalar1=stat[:, 6:7], scalar2=stat[:, 7:8],
                                                             op0=ALU.subtract, op1=ALU.mult), reads=[gvk, 'stat'], writes=[gvk])
                        S.op('dve', lambda: V.tensor_tensor(out=gv[:], in0=gv[:], in1=gbc[:], op=ALU.mult), reads=[gvk, 'gbc'], writes=[gvk])
                        last = (tt == NT - 1 and sub == 3)
                        if last:
                            S.op('dve', lambda: V.tensor_tensor(out=gv[:], in0=gv[:], in1=bbc[:], op=ALU.add), reads=[gvk, 'bbc'], writes=[gvk])
                            S.dma(chunkv_p[e], gv[:], reads=[gvk])
                            S.op('act', lambda: A.copy(out=vb[:], in_=gv[:]), reads=[gvk], writes=['vb'])
                        else:
                            S.op('dve', lambda: V.tensor_tensor(out=vb[:], in0=gv[:], in1=bbc[:], op=ALU.add), reads=[gvk, 'bbc'], writes=['vb'])
                        ps, pk = PS()
                        for g in range(4):
                            mm(ps[:, g * 128:(g + 1) * 128], vb[:, g * 128:(g + 1) * 128], wmT[:, g, :], True, False,
                               reads=['vb', 'wmT'], writes=[pk])
                            mm(ps[:, g * 128:(g + 1) * 128], onesb[0:1, :], bsb[0:1, g, :], False, True,
                               reads=['onesb', 'bsb'], writes=[pk])
                        S.op('dve', lambda: V.tensor_tensor(out=bo_t[:, :, sub * 128:(sub + 1) * 128],
                                                             in0=u_t[:, :, sub * 128:(sub + 1) * 128],
                                                             in1=ps[:].rearrange("p (a b) -> p a b", a=4), op=ALU.mult),
                             reads=['u_t', pk], writes=['bo_t'])
                    pss, pssk = PS()
                    psq, psqk = PS()
                    for oc in range(4):
                        dg = diag[ndiag[0] % 2]
                        dgk = ('diag', ndiag[0] % 2)
                        ndiag[0] += 1
                        S.op('dve', lambda: V.tensor_tensor(out=dg[:], in0=identb[:].unsqueeze(1).to_broadcast([128, 31, 128]),
                                                             in1=pe[:, 12 + oc:136:4].unsqueeze(2).to_broadcast([128, 31, 128]), op=ALU.mult),
                             reads=['identb', 'pe'], writes=[dgk])
                        ps, pk = PS()
                        for k in range(31):
                            mm(ps[:], dg[:, k, :], ap_t[:, oc, k:k + TT], k == 0, k == 30, reads=[dgk, apk], writes=[pk])
                        S.op('act', lambda: A.activation(out=acv[:, oc, :], in_=ps[:], func=AF.Identity, bias=pe[:, oc:oc + 1], scale=1.0),
                             reads=[pk, 'pe'], writes=[('acv', oc)])
                        sq, sqk = s16.get()
                        S.op('act', lambda: A.activation(out=sq[:], in_=ps[:], func=AF.Square, bias=pe[:, oc:oc + 1], scale=1.0),
                             reads=[pk, 'pe'], writes=[sqk])
                        mm(pss[:], onesb[:], acv[:, oc, :], oc == 0, oc == 3, reads=[('acv', oc), 'onesb'], writes=[pssk], inc=True)
                        mm(psq[:], onesb[:], sq[:], oc == 0, oc == 3, reads=[sqk, 'onesb'], writes=[psqk], inc=True)
                    S.op('act', lambda: A.activation(out=mean[:], in_=pss[:], func=AF.Identity, scale=1.0 / 512), reads=[pssk], writes=[mk])
                    if tt + 1 < NT:
                        hal, halk = s16.get()
                        S.op('dve', lambda: V.tensor_copy(out=hal[:, 0:120].rearrange("p (a b) -> p a b", a=4), in_=ap_t[:, :, TT:TT + 30]),
                             reads=[apk], writes=[halk])
                        S.op('dve', lambda: V.tensor_copy(out=ap_t[:, :, 0:30], in_=hal[:, 0:120].rearrange("p (a b) -> p a b", a=4)),
                             reads=[halk], writes=[apk])
                    S.op('dve', lambda: V.tensor_tensor(out=msq[:], in0=mean[:], in1=mean[:], op=ALU.mult), reads=[mk], writes=[msk])
                    S.op('dve', lambda: V.scalar_tensor_tensor(out=msq[:], in0=psq[:], scalar=1.0 / 512, in1=msq[:], op0=ALU.mult, op1=ALU.subtract),
                         reads=[psqk, msk], writes=[msk])
                    S.op('act', lambda: A.activation(out=msq[:], in_=msq[:], func=AF.Sqrt, bias=eps_t[:, 0:1], scale=1.0), reads=[msk, 'eps'], writes=[msk])
                    S.op('dve', lambda: V.reciprocal(out=msq[:], in_=msq[:]), reads=[msk], writes=[msk])
                    for oc in range(4):
                        xc, xck = s32.get()
                        S.op('dve', lambda: V.tensor_tensor(out=xc[:], in0=acv[:, oc, :], in1=mean[:], op=ALU.subtract), reads=[('acv', oc), mk], writes=[xck])
                        S.op('dve', lambda: V.tensor_tensor(out=xc[:], in0=xc[:], in1=msq[:], op=ALU.mult), reads=[xck, msk], writes=[xck])
                        S.op('act', lambda: A.activation(out=ao_t[:, oc, :], in_=xc[:], func=AF.Silu, scale=pe[:, 4 + oc:5 + oc], bias=pe[:, 8 + oc:9 + oc]),
                             reads=[xck, 'pe'], writes=['ao_t'])
                    if tt == NT - 1:
                        ps, pk = PS()
                        for oc in range(4):
                            tr(ps[0:32, oc * 128:(oc + 1) * 128], a32[:, oc, :], ident32[:], reads=['a32', 'ident32'], writes=[pk], inc=(oc == 3))
                        o32, o32k = s32.get()
                        S.op('act', lambda: A.copy(out=o32[0:32, :], in_=ps[0:32, :]), reads=[pk], writes=[o32k])
                        S.dma(conv_p[e], o32[2:32, :], reads=[o32k])
                    Wo1, Wo1k = w_next()
                    Wo2, Wo2k = w_next()
                    for i in range(8):
                        ps, pk = PS()
                        for k in range(8):
                            if k < 4:
                                mm(ps[:], Wo1[:, k, i * 128:(i + 1) * 128], ao_t[:, k, :], k == 0, False, reads=Wo1k + ['ao_t'], writes=[pk])
                            else:
                                mm(ps[:], Wo2[:, k - 4, i * 128:(i + 1) * 128], bo_t[:, k - 4, :], False, k == 7, reads=Wo2k + ['bo_t'], writes=[pk])
                        S.op('dve', lambda: V.scalar_tensor_tensor(out=xT[:, i, cols], in0=ps[:], scalar=m[:, g1 + i, 0:1],
                                                                    in1=xT[:, i, cols], op0=ALU.mult, op1=ALU.add),
                             reads=[pk, ('mod', l % 2)] + xkeys[tt], writes=[xkeys[tt][i // 4]])


        def odd_layer(l):
            o = l // 2
            m = mod[l % 2]
            g1 = 2 * 8
            SC = 0.125
            hflat = hT[:].rearrange("p a b -> p (a b)")
            skT = hflat[:, 0:4096].rearrange("p (c t) -> p c t", c=2)
            wkT = hflat[:, 4096:8192].rearrange("p (c t) -> p c t", c=2)
            svb = hflat[:, 8192:12288].rearrange("p (t f) -> p t f", t=16)
            wvb = hflat[:, 12288:16384].rearrange("p (t f) -> p t f", t=16)
            allh = [k for t in range(NT) for k in hkeys[t]] + [('h', 's')]
            arena = [(n, t) for n in ('skT', 'wkT', 'svb', 'wvb') for t in range(16)]
            with phase() as ph:
                hs_t = sb("hs_t", [128, 8, NS], BF16, ph)
                norm_samples(l, 0)
                S.op('dve', lambda: V.tensor_copy(out=hs_t[:], in_=hT[:, :, SEQ:HC]), reads=[('h', 's')], writes=['hs_t'])
                S.op('dve', lambda: V.memset(hflat[:, 16384:16392], 0.0), reads=['hs_t'], writes=allh + arena)
                kcT = sb("kcT", [128, 2, 128], BF16, ph)
                vc = sb("vc", [128, 4, 64], BF16, ph)
                htile = sb("htile", [128, 8, TT], BF16, ph)
                htk = [('ht', c) for c in range(8)]
                cs2 = sb("cs2", [128, 4, 64], F32, ph)
                sn2 = sb("sn2", [128, 4, 64], F32, ph)
                with phase() as pa:
                    ckcvT = sb("ckcvT", [128, 4, SEQ], BF16, pa)
                    for tt in range(NT if OST >= 1 else 0):
                        t0 = tt * TT
                        norm_tile(l, 0, tt, htile, htk)
                        S.dma(cs2[:], rope_cos[t0:t0 + TT, :].rearrange("(s p) d -> p s d", p=128), writes=['cs2'])
                        S.dma(sn2[:], rope_sin[t0:t0 + TT, :].rearrange("(s p) d -> p s d", p=128), writes=['sn2'])
                        for blk in range(3 if OST != 1 else int(os.environ.get('K_NBLK', 3))):
                            W, Wk = w_next()
                            for sub in range(4):
                                T128 = tt * 4 + sub
                                r0 = T128 * 128
                                ps, pk = PS()
                                for k in range(8):
                                    mm(ps[:], htile[:, k, sub * 128:(sub + 1) * 128], W[:, k, :], k == 0, k == 7, reads=Wk + htk, writes=[pk])
                                zt, ztk = s32.get()
                                if blk == 0:
                                    S.op('act', lambda: A.copy(out=zt[:], in_=ps[:]), reads=[pk], writes=[ztk])
                                    S.dma(nsa_p[0][o, r0:r0 + 128, :], zt[:, 0:256], reads=[ztk])
                                    S.dma(nsa_p[1][o, r0:r0 + 128, :], zt[:, 256:512], reads=[ztk])
                                    ps2, pk2 = PS()
                                    for q in range(4):
                                        tr(ps2[:, q * 128:(q + 1) * 128], zt[:, q * 128:(q + 1) * 128], ident32[:], reads=[ztk, 'ident32'], writes=[pk2], inc=(q == 3))
                                    S.op('dve', lambda: V.tensor_copy(out=ckcvT[:, :, r0:r0 + 128], in_=ps2[:].rearrange("p (a b) -> p a b", a=4)),
                                         reads=[pk2], writes=[('ckcvT', T128)])
                                else:
                                    t12, t12k = s32.get()
                                    p3 = ps[:, 0:256].rearrange("p (h d) -> p h d", h=4)
                                    t1 = t12[:, 0:256].rearrange("p (h d) -> p h d", h=4)
                                    t2 = t12[:, 256:512].rearrange("p (h d) -> p h d", h=4)
                                    cb_ = cs2[:, sub, :].unsqueeze(1).to_broadcast([128, 4, 64])
                                    S.op('dve', lambda: V.tensor_tensor(out=t1, in0=p3, in1=cb_, op=ALU.mult), reads=[pk, 'cs2'], writes=[t12k])
                                    S.op('dve', lambda: V.tensor_tensor(out=t2[:, :, 0:32], in0=p3[:, :, 32:64],
                                                                         in1=sn2[:, sub, 0:32].unsqueeze(1).to_broadcast([128, 4, 32]), op=ALU.mult),
                                         reads=[pk, 'sn2'], writes=[t12k])
                                    S.op('dve', lambda: V.tensor_tensor(out=t2[:, :, 32:64], in0=p3[:, :, 0:32],
                                                                         in1=sn2[:, sub, 32:64].unsqueeze(1).to_broadcast([128, 4, 32]), op=ALU.mult),
                                         reads=[pk, 'sn2'], writes=[t12k])
                                    S.op('dve', lambda: V.tensor_tensor(out=zt[:, 0:256], in0=t12[:, 0:256], in1=t12[:, 256:512], op=ALU.add),
                                         reads=[t12k], writes=[ztk])
                                    S.op('act', lambda: A.copy(out=zt[:, 256:512], in_=ps[:, 256:512]), reads=[pk], writes=[ztk])
                                    if blk == 1:
                                        S.dma(nsa_p[2][o, r0:r0 + 128, :], zt[:, 0:256], reads=[ztk])
                                        S.dma(nsa_p[3][o, r0:r0 + 128, :], zt[:, 256:512], reads=[ztk])
                                    elif T128 >= 12:
                                        S.dma(win_p[0][o, r0 - 1536:r0 - 1408, :], zt[:, 0:256], reads=[ztk])
                                        S.dma(win_p[1][o, r0 - 1536:r0 - 1408, :], zt[:, 256:512], reads=[ztk])
                                    VB, vbn = (svb, 'svb') if blk == 1 else (wvb, 'wvb')
                                    KT, ktn = (skT, 'skT') if blk == 1 else (wkT, 'wkT')
                                    S.op('dve', lambda: V.tensor_copy(out=VB[:, T128, :], in_=zt[:, 256:512]), reads=[ztk], writes=[(vbn, T128)])
                                    ps2, pk2 = PS()
                                    for q in range(2):
                                        tr(ps2[:, q * 128:(q + 1) * 128], zt[:, q * 128:(q + 1) * 128], ident32[:], reads=[ztk, 'ident32'], writes=[pk2], inc=(q == 1))
                                    S.op('act', lambda: A.copy(out=KT[:, :, r0:r0 + 128], in_=ps2[:, 0:256].rearrange("p (a b) -> p a b", a=2)),
                                         reads=[pk2], writes=[(ktn, T128)])
                    with phase() as pc:
                      if OST >= 2:
                            w1t = sb("w1t", [128, 32, 128], BF16, pc)
                            w2t = sb("w2t", [128, 64], BF16, pc)
                            peT = sb("peT", [128, 32], BF16, pc)
                            hb = sb("hb", [128, 1], F32, pc)
                            ckk = [('ckcvT', t) for t in range(16)]
                            for X in range(2):
                                for half in range(2):
                                    S.dma(w1t[half * 64:(half + 1) * 64, :, :], cmp_w1[X][o].rearrange("l d e -> d l e"), writes=[('w1t', half)], q='pool')
                                    S.dma(peT[half * 64:(half + 1) * 64, :], cmp_pe[X][o].rearrange("l d -> d l"), writes=[('peT', half)], q='pool',
                                          allow_slow_non_contiguous=True)
                                S.dma(w2t[:], cmp_w2[X][o], writes=['w2t'], q='pool')
                                ps, pk = PS()
                                for li in range(32):
                                    mm(ps[:, 0:1], w1t[0:64, li, :], peT[0:64, li:li + 1], li == 0, li == 31, reads=[('w1t', 0), ('peT', 0)], writes=[pk])
                                S.op('dve', lambda: V.tensor_copy(out=hb[:], in_=ps[:, 0:1]), reads=[pk], writes=['hb'])
                                for kv in range(4):
                                    half = kv % 2
                                    hs = slice(half * 64, half * 64 + 64)
                                    X3 = ckcvT[hs, 2 * X + kv // 2, :].rearrange("p (c s) -> p c s", s=16)
                                    ps, pk = PS()
                                    for li in range(32):
                                        mm(ps[:, 0:127], w1t[hs, li, :], X3[:, li // 16:li // 16 + 127, li % 16], li == 0, li == 31,
                                           reads=[('w1t', half)] + ckk, writes=[pk])
                                    hg, hgk = s16.get()
                                    S.op('act', lambda: A.activation(out=hg[:, 0:127], in_=ps[:, 0:127], func=AF.Gelu_apprx_tanh, bias=hb[:, 0:1], scale=1.0),
                                         reads=[pk, 'hb'], writes=[hgk])
                                    ps2, pk2 = PS()
                                    if X == 0:
                                        mm(ps2[hs, 0:127], w2t[:, 0:64], hg[:, 0:127], True, True, reads=['w2t', hgk], writes=[pk2])
                                        S.op('dve', lambda: V.tensor_copy(out=kcT[hs, kv // 2, 0:127], in_=ps2[hs, 0:127]), reads=[pk2], writes=['kcT'])
                                    else:
                                        mm(ps2[0:127, 0:64], hg[:, 0:127], w2t[:, 0:64], True, True, reads=['w2t', hgk], writes=[pk2])
                                        S.op('dve', lambda: V.tensor_copy(out=vc[0:127, kv, :], in_=ps2[0:127, 0:64]), reads=[pk2], writes=['vc'])
                with phase() as pb_:
                    qT = sb("qT", [128, 8, TQ], BF16, pb_)
                    qrT = sb("qrT", [128, 8, TQ], BF16, pb_)
                    oT = htile[:, :, 0:TQ]
                    qtk = sb("qtk", [128, 1024], F32, pb_)
                    qrk = sb("qrk", [128, 1024], F32, pb_)
                    gT = sb("gT", [48, TQ], BF16, pb_)
                    gtk = sb("gtk", [128, 48], F32, pb_)
                    bandb = sb("bandb", [128, 4, TQ], BF16, pb_)
                    Emat = sb("Emat", [32, 16, 128], BF16, pb_)
                    ov32 = sb("ov32", [128, 32], F32, pb_)
                    cmpb = sb("cmpb", [128, TQ], BF16, pb_)
                    selbT = sb("selbT", [32, 4, TQ], BF16, pb_)
                    tA = sb("tA", [128, 2, 32], F32, pb_)
                    tB = sb("tB", [128, 2, 32], F32, pb_)
                    pgA = [sb("pgA%d" % i, [128, TQ], F32, pb_) for i in range(2)]
                    scr = sb("scr", [128, 32], F32, pb_)
                    selt = sb("selt", [128, 32], F32, pb_)
                    selb = sb("selb", [128, 32], F32, pb_)
                    m8 = sb("m8", [128, 8], F32, pb_)
                    acc = sb("acc", [128, TQ], F32, pb_)
                    pring = Ring(nc, pb_, "pr_", 6, [128, TQ], BF16)
                    S.dma(bandb[:], bandb_d, writes=['bandb'], q='pool')
                    S.dma(Emat[:], Emat_d, writes=['Emat'], q='pool')
                    S.dma(ov32[0:127, :], ov_d, writes=['ov32'])
                    band_idx = {-4: 0, -3: 1, 0: 2, 1: 3}

                    sets = {'acc': [0, 1, 2], 'st': [3, 4, 5, 6], 'misc': [7]}
                    seti = {k: 0 for k in sets}

                    def PSS(name):
                        lst = sets[name]
                        i = lst[seti[name] % len(lst)]
                        seti[name] += 1
                        return ps_t[i], ('ps', i), None

                    def softmax_block(head_rows, qsrc, m_, kv, KT, VB, vrows, kts, kind, qt, po, pok, psm, psmk, M_sum):
                        hs = head_rows
                        n = len(kts)
                        LOOK = 2
                        pend = []

                        def emit_pv(i, kt, kr, P, Pk):
                            if kind == 'cmp':
                                lv = vc[0:127, kv, :]
                                vrd = ['vc']
                            else:
                                lv = VB[:, kt, kv * 64:(kv + 1) * 64]
                                vrd = [('svb' if kind == 'sel' else 'wvb', kt)]
                            mm(po[hs, 0:TQ], lv, P[0:kr, :], i == 0, i == n - 1, reads=vrd + [Pk], writes=[pok])
                            if M_sum == 128:
                                mm(psm[:, 0:TQ], onesb[0:kr, :], P[0:kr, :], i == 0, i == n - 1, reads=['onesb', Pk], writes=[psmk])
                            else:
                                mm(psm[hs, 0:TQ], onesb[0:kr, 0:64], P[0:kr, :], i == 0, i == n - 1, reads=['onesb', Pk], writes=[psmk])

                        for i, kt in enumerate(kts):
                            ps_s, psk, _ = PSS('st')
                            if kind == 'cmp':
                                kr = 127
                                mm(ps_s[0:kr, 0:TQ], kcT[hs, m_ // 4, 0:127], qsrc[hs, m_, :], True, False, reads=['kcT', 'qT'], writes=[psk])
                                mm(ps_s[0:kr, 0:TQ], identb[0:127, 0:127], cmpb[0:127, :], False, True, reads=['identb', 'cmpb'], writes=[psk])
                            else:
                                kr = 128
                                d = kt - 2 * qt
                                extra = []
                                if kind == 'sel':
                                    extra.append((Emat[0:32, kt, :], selbT[0:32, kv, :], ['Emat', ('selbT', kv)]))
                                if d in band_idx:
                                    extra.append((identb[:], bandb[:, band_idx[d], :], ['identb', 'bandb']))
                                ktn = 'skT' if kind == 'sel' else 'wkT'
                                mm(ps_s[:, 0:TQ], KT[hs, m_ // 4, kt * 128:(kt + 1) * 128], qsrc[hs, m_, :], True, len(extra) == 0,
                                   reads=[(ktn, kt), 'qrT'], writes=[psk])
                                for ei, (l_, r_, rd) in enumerate(extra):
                                    mm(ps_s[:, 0:TQ], l_, r_, False, ei == len(extra) - 1, reads=rd, writes=[psk])
                            P, Pk = pring.get()
                            S.op('act', lambda: A.activation(out=P[0:kr, :], in_=ps_s[0:kr, 0:TQ], func=AF.Exp, scale=SC), reads=[psk], writes=[Pk])
                            pend.append((i, kt, kr, P, Pk))
                            if len(pend) > LOOK:
                                emit_pv(*pend.pop(0))
                        while pend:
                            emit_pv(*pend.pop(0))

                    def gate_bc(m_, br):
                        pg_, pgk, _ = PSS('misc')
                        for half in range(2):
                            h_ = (8 * (m_ // 4) + 4 * half + (m_ % 4))
                            row = h_ * 3 + br
                            mm(pg_[half * 64:(half + 1) * 64, 0:TQ], identb[0:48, row:row + 1].to_broadcast([48, 64]), gT[0:48, :], True, True,
                               reads=['identb', 'gT'], writes=[pgk], inc=(half == 1))
                        return pg_, pgk

                    for qt in range(NQ if OST >= 3 else 0):
                        t0 = qt * TQ
                        tt = qt // 2
                        if qt % 2 == 0:
                            norm_tile(l, 0, tt, htile, htk)
                        hoff = (qt % 2) * TQ
                        S.dma(cs2[:, 0:2, :], rope_cos[t0:t0 + TQ, :].rearrange("(s p) d -> p s d", p=128), writes=['cs2'])
                        S.dma(sn2[:, 0:2, :], rope_sin[t0:t0 + TQ, :].rearrange("(s p) d -> p s d", p=128), writes=['sn2'])
                        S.dma(tA[:], selA[t0:t0 + TQ, :].rearrange("(s p) d -> p s d", p=128), writes=['tA'])
                        S.dma(tB[:], selB[t0:t0 + TQ, :].rearrange("(s p) d -> p s d", p=128), writes=['tB'])
                        S.dma(cmpb[0:127, :], cmpb_d[:, t0:t0 + TQ], writes=['cmpb'], q='pool')
                        W0, W0k = w_next()
                        W1, W1k = w_next()
                        for sub in range(2):
                            hc = slice(hoff + sub * 128, hoff + (sub + 1) * 128)
                            for blk in range(2):
                                W, Wk = (W0, W0k) if blk == 0 else (W1, W1k)
                                ps, pk, _ = PSS('st')
                                for k in range(8):
                                    mm(ps[:], htile[:, k, hc], W[:, k, :], k == 0, k == 7, reads=Wk + htk, writes=[pk])
                                pin = ps[:].rearrange("p (h m d) -> p h m d", h=2, m=4)
                                qo = qtk[:, blk * 512:(blk + 1) * 512].rearrange("p (m h d) -> p h m d", m=4, h=2)
                                S.op('act', lambda: A.copy(out=qo, in_=pin), reads=[pk], writes=['qtk'])
                                t12, t12k = s32.get()
                                t3, t3k = s32.get()
                                p3 = ps[:].rearrange("p (h d) -> p h d", h=8)
                                t1 = t12[:].rearrange("p (h d) -> p h d", h=8)
                                t2 = t3[:].rearrange("p (h d) -> p h d", h=8)
                                S.op('dve', lambda: V.tensor_tensor(out=t1, in0=p3, in1=cs2[:, sub, :].unsqueeze(1).to_broadcast([128, 8, 64]), op=ALU.mult),
                                     reads=[pk, 'cs2'], writes=[t12k])
                                S.op('dve', lambda: V.tensor_tensor(out=t2[:, :, 0:32], in0=p3[:, :, 32:64],
                                                                     in1=sn2[:, sub, 0:32].unsqueeze(1).to_broadcast([128, 8, 32]), op=ALU.mult),
                                     reads=[pk, 'sn2'], writes=[t3k])
                                S.op('dve', lambda: V.tensor_tensor(out=t2[:, :, 32:64], in0=p3[:, :, 0:32],
                                                                     in1=sn2[:, sub, 32:64].unsqueeze(1).to_broadcast([128, 8, 32]), op=ALU.mult),
                                     reads=[pk, 'sn2'], writes=[t3k])
                                qro = qrk[:, blk * 512:(blk + 1) * 512].rearrange("p (m h d) -> p h m d", m=4, h=2)
                                S.op('dve', lambda: V.tensor_tensor(out=qro, in0=t12[:].rearrange("p (h m d) -> p h m d", h=2, m=4),
                                                                     in1=t3[:].rearrange("p (h m d) -> p h m d", h=2, m=4), op=ALU.add),
                                     reads=[t12k, t3k], writes=['qrk'])
                            for (src, srck, dstT, dk) in ((qtk, 'qtk', qT, 'qT'), (qrk, 'qrk', qrT, 'qrT')):
                                for hb_ in range(2):
                                    ps, pk, _ = PSS('st')
                                    for c in range(4):
                                        cc = hb_ * 4 + c
                                        tr(ps[:, c * 128:(c + 1) * 128], src[:, cc * 128:(cc + 1) * 128], ident32[:], reads=[srck, 'ident32'], writes=[pk], inc=(c == 3))
                                    S.op('act' if hb_ == 0 else 'dve',
                                         (lambda: A.copy(out=dstT[:, hb_ * 4:hb_ * 4 + 4, sub * 128:(sub + 1) * 128], in_=ps[:].rearrange("p (a b) -> p a b", a=4))) if hb_ == 0 else
                                         (lambda: V.tensor_copy(out=dstT[:, hb_ * 4:hb_ * 4 + 4, sub * 128:(sub + 1) * 128], in_=ps[:].rearrange("p (a b) -> p a b", a=4))),
                                         reads=[pk], writes=[dk])
                        Wg_, Wgk_ = w_next()
                        for sub in range(2):
                            hc = slice(hoff + sub * 128, hoff + (sub + 1) * 128)
                            ps, pk, _ = PSS('st')
                            for k in range(8):
                                mm(ps[:, 0:48], htile[:, k, hc], Wg_[:, k, :], k == 0, k == 7, reads=Wgk_ + htk, writes=[pk])
                            S.op('act', lambda: A.activation(out=gtk[:], in_=ps[:, 0:48], func=AF.Sigmoid), reads=[pk], writes=['gtk'])
                            ps, pk, _ = PSS('misc')
                            tr(ps[0:48, 0:128], gtk[:], ident32[:], reads=['gtk', 'ident32'], writes=[pk])
                            S.op('dve', lambda: V.tensor_copy(out=gT[:, sub * 128:(sub + 1) * 128], in_=ps[0:48, 0:128]), reads=[pk], writes=['gT'])
                        ok = htk
                        for mg in range(2 if OST >= 4 else 0):
                            for mi in range(4):
                                m_ = 4 * mg + mi
                                po, pok, _ = PSS('acc')
                                recs = []
                                for half in range(2):
                                    kv = 2 * mg + half
                                    hs = slice(half * 64, half * 64 + 64)
                                    psm, psmk, _ = PSS('acc')
                                    softmax_block(hs, qT, m_, kv, None, None, None, [0], 'cmp', qt, po, pok, psm, psmk, 128)
                                    rec, reck = s32.get()
                                    S.op('dve', lambda: V.tensor_scalar(out=rec[:, 0:TQ], in0=psm[:, 0:TQ], scalar1=1e-30, scalar2=None, op0=ALU.max), reads=[psmk], writes=[reck])
                                    S.op('dve', lambda: V.reciprocal(out=rec[:, 0:TQ], in_=rec[:, 0:TQ]), reads=[reck], writes=[reck])
                                    Plast = pring.t[(pring.i - 1) % 6]
                                    Plk = ("pr_", (pring.i - 1) % 6)
                                    pg = pgA[half]
                                    if mi == 0:
                                        S.op('dve', lambda: V.tensor_tensor(out=pg[0:127, :], in0=Plast[0:127, :], in1=rec[0:127, 0:TQ], op=ALU.mult),
                                             reads=[Plk, reck], writes=[('pg', half)])
                                    else:
                                        t_, tk_ = s32.get()
                                        S.op('dve', lambda: V.tensor_tensor(out=t_[0:127, 0:TQ], in0=Plast[0:127, :], in1=rec[0:127, 0:TQ], op=ALU.mult),
                                             reads=[Plk, reck], writes=[tk_])
                                        S.op('dve', lambda: V.tensor_tensor(out=pg[0:127, :], in0=pg[0:127, :], in1=t_[0:127, 0:TQ], op=ALU.add),
                                             reads=[tk_, ('pg', half)], writes=[('pg', half)])
                                    recs.append((rec, reck))
                                pg_, pgk = gate_bc(m_, 0)
                                for half in range(2):
                                    hs = slice(half * 64, half * 64 + 64)
                                    rec, reck = recs[half]
                                    S.op('dve', lambda: V.tensor_tensor(out=rec[hs, 0:TQ], in0=rec[hs, 0:TQ], in1=pg_[hs, 0:TQ], op=ALU.mult), reads=[reck, pgk], writes=[reck])
                                    S.op('dve', lambda: V.tensor_tensor(out=oT[hs, m_, :], in0=po[hs, 0:TQ], in1=rec[hs, 0:TQ], op=ALU.mult),
                                         reads=[pok, reck], writes=[ok[m_]])
                            for half in range(2):
                                kv = 2 * mg + half
                                pg = pgA[half]
                                for sub in range(2):
                                    ps, pk, pb = PSS('misc')
                                    mm(ps[:, 0:32], pg[0:127, sub * 128:(sub + 1) * 128], ov32[0:127, :], True, True, reads=[('pg', half), 'ov32'], writes=[pk])
                                    S.op('dve', lambda: V.tensor_tensor(out=scr[:], in0=ps[:, 0:32], in1=tA[:, sub, :], op=ALU.mult), reads=[pk, 'tA'], writes=['scr'])
                                    S.op('dve', lambda: V.tensor_tensor(out=scr[:], in0=scr[:], in1=tB[:, sub, :], op=ALU.add), reads=['scr', 'tB'], writes=['scr'])
                                    S.op('dve', lambda: V.max(out=m8[:], in_=scr[:]), reads=['scr'], writes=['m8'])
                                    S.op('dve', lambda: V.tensor_scalar(out=selt[:], in0=scr[:], scalar1=m8[:, 7:8], scalar2=None, op0=ALU.is_ge), reads=['scr', 'm8'], writes=['selt'])
                                    S.op('dve', lambda: V.tensor_scalar(out=selb[:], in0=selt[:], scalar1=-1.0, scalar2=30000.0, op0=ALU.add, op1=ALU.mult), reads=['selt'], writes=['selb'])
                                    ps, pk, _ = PSS('misc')
                                    tr(ps[0:32, 0:128], selb[:], ident32[:], reads=['selb', 'ident32'], writes=[pk])
                                    S.op('act', lambda: A.copy(out=selbT[:, kv, sub * 128:(sub + 1) * 128], in_=ps[0:32, 0:128]), reads=[pk], writes=[('selbT', kv)])
                        for m_ in range(8 if OST >= 5 else 0):
                            for br, kind in ((1, 'sel'), (2, 'win')):
                                po, pok, _ = PSS('acc')
                                psm, psmk, _ = PSS('acc')
                                if kind == 'sel':
                                    kts = list(range(0, 2 * qt + 2))
                                    KT, VB = skT, svb
                                else:
                                    kts = list(range(max(0, 2 * qt - 4), 2 * qt + 2))
                                    KT, VB = wkT, wvb
                                for half in range(2):
                                    kv = 2 * (m_ // 4) + half
                                    hs = slice(half * 64, half * 64 + 64)
                                    softmax_block(hs, qrT, m_, kv, KT, VB, None, kts, kind, qt, po, pok, psm, psmk, 64)
                                pg_, pgk = gate_bc(m_, br)
                                rec, reck = s32.get()
                                S.op('dve', lambda: V.tensor_scalar(out=rec[:, 0:TQ], in0=psm[:, 0:TQ], scalar1=1e-30, scalar2=None, op0=ALU.max), reads=[psmk], writes=[reck])
                                S.op('dve', lambda: V.reciprocal(out=rec[:, 0:TQ], in_=rec[:, 0:TQ]), reads=[reck], writes=[reck])
                                S.op('dve', lambda: V.tensor_tensor(out=rec[:, 0:TQ], in0=rec[:, 0:TQ], in1=pg_[:, 0:TQ], op=ALU.mult), reads=[reck, pgk], writes=[reck])
                                if kind == 'sel':
                                    S.op('dve', lambda: V.tensor_tensor(out=acc[:], in0=po[:, 0:TQ], in1=rec[:, 0:TQ], op=ALU.mult), reads=[pok, reck], writes=['acc'])
                                else:
                                    S.op('dve', lambda: V.tensor_tensor(out=rec[:, 0:TQ], in0=po[:, 0:TQ], in1=rec[:, 0:TQ], op=ALU.mult), reads=[pok, reck], writes=[reck])
                                    S.op('dve', lambda: V.tensor_tensor(out=acc[:], in0=acc[:], in1=rec[:, 0:TQ], op=ALU.add), reads=['acc', reck], writes=['acc'])
                            S.op('dve', lambda: V.tensor_tensor(out=oT[:, m_, :], in0=oT[:, m_, :], in1=acc[:], op=ALU.add), reads=['acc', ok[m_]], writes=[ok[m_]])
                        if OST < 5:
                            continue
                        Wo1, Wo1k = w_next()
                        Wo2, Wo2k = w_next()
                        for i in range(8):
                            ps, pk, _ = PSS('st')
                            for m_ in range(8):
                                Wo, Wok = (Wo1, Wo1k) if m_ < 4 else (Wo2, Wo2k)
                                mm(ps[:, 0:TQ], Wo[:, m_ % 4, i * 128:(i + 1) * 128], oT[:, m_, :], m_ == 0, m_ == 7, reads=Wok + [ok[m_]], writes=[pk])
                            S.op('dve', lambda: V.scalar_tensor_tensor(out=xT[:, i, t0:t0 + TQ], in0=ps[:, 0:TQ], scalar=m[:, g1 + i, 0:1],
                                                                        in1=xT[:, i, t0:t0 + TQ], op0=ALU.mult, op1=ALU.add),
                                 reads=[pk, ('mod', l % 2)] + xkeys[tt], writes=[xkeys[tt][i // 4]])
                if OST >= 6:
                  with phase() as sp_:
                    hflat32 = hflat.bitcast(F32)
                    stg = [hflat32[:, i * 4096:(i + 1) * 4096].rearrange("p (j f) -> p j f", j=16) for i in range(2)]
                    stgk = [[('stg', i, j) for j in range(16)] for i in range(2)]
                    S.op('dve', lambda: V.memset(hflat[:, 16384:16392], 0.0), writes=allh + arena + stgk[0] + stgk[1])
                    qsT = sb("qsT", [128, 8, NS], BF16, sp_)
                    qrsT = sb("qrsT", [128, 8, NS], BF16, sp_)
                    knT = sb("knT", [128, 4, NS], BF16, sp_)
                    vnb = sb("vnb", [NS, 512], BF16, sp_)
                    gs_t = sb("gs_t", [NS, 48], F32, sp_)
                    gperm = sb("gperm", [NS, 3, 16], F32, sp_)
                    pgall = sb("pgall", [128, 4, NS], F32, sp_)
                    oTb = [sb("oTb%d" % i, [128, 8, NS], F32, sp_) for i in range(3)]
                    kcs = sb("kcs", [128, 2, 128], BF16, sp_)
                    vcs = sb("vcs", [128, 4, 64], BF16, sp_)
                    selbTs = sb("selbTs", [33, 4, NS], BF16, sp_)
                    idx = sb("idx", [128, 256], I32, sp_)
                    idxf = sb("idxf", [128, 256], F32, sp_)
                    poff = sb("poff", [128, 2], F32, sp_)
                    nb16 = sb("nb16", [16, 16], BF16, sp_)
                    Emat_s = sb("Emat_s", [32, 16, 128], BF16, sp_)
                    ov33 = sb("ov33", [128, 33], F32, sp_)
                    sAs = sb("sAs", [NS, 33], F32, sp_)
                    sBs = sb("sBs", [NS, 33], F32, sp_)
                    XTs = sb("XTs", [128, 2, SEQ], BF16, sp_)
                    Pc = sb("Pc", [128, 144], BF16, sp_)
                    recs = sb("recs", [128, 8], F32, sp_)
                    pns = sb("pns", [128, 8], F32, sp_)
                    S.dma(nb16[:], nb16_d, writes=['nb16'], q='pool')
                    S.dma(Emat_s[:], Emat_d, writes=['Emat_s'], q='pool')
                    S.dma(ov33[0:127, :], ov33_d, writes=['ov33'])
                    S.dma(sAs[:], selA_s.to_broadcast([NS, 33]), writes=['sAs'])
                    S.dma(sBs[:], selB_s.to_broadcast([NS, 33]), writes=['sBs'])
                    S.dma(poff[:], poff_d, writes=['poff'])
                    S.dma(idx[:], page_table.rearrange("b j -> (b j)").unsqueeze(0).to_broadcast([128, 256]), writes=['idx'])
                    S.op('dve', lambda: V.tensor_copy(out=idxf[:], in_=idx[:]), reads=['idx'], writes=['idxf'])
                    S.op('dve', lambda: V.tensor_scalar(out=idxf[:], in0=idxf[:], scalar1=128.0, scalar2=poff[:, o:o + 1], op0=ALU.mult, op1=ALU.add),
                         reads=['idxf', 'poff'], writes=['idxf'])
                    S.op('dve', lambda: V.tensor_copy(out=idx[:], in_=idxf[:]), reads=['idxf'], writes=['idx'])

                    def gather(cache, b, dst, dkeys):
                        for j in range(16):
                            S.dma_ind(dst[:, j, :], cache, idx[:, b * 16 + j:b * 16 + j + 1], reads=['idx'], writes=[dkeys[j]])

                    with phase() as s1:
                        zs = sb("zs", [NS, 2608], F32, s1)
                        qr = sb("qr", [NS, 1024], F32, s1)
                        tm = sb("tm", [NS, 1024], F32, s1)
                        qp = tm
                        c2s = sb("c2s", [NS, 64], F32, s1)
                        s2s = sb("s2s", [NS, 64], F32, s1)
                        S.dma(c2s[:], rope_cos[SEQ:SEQ + 1, :].to_broadcast([NS, 64]), writes=['c2s'])
                        S.dma(s2s[:], rope_sin[SEQ:SEQ + 1, :].to_broadcast([NS, 64]), writes=['s2s'])
                        for (c0, n) in SBLK:
                            W, Wk = w_next()
                            ps, pk = PS()
                            for k in range(8):
                                mm(ps[0:NS, 0:n], hs_t[:, k, :], W[:, k, 0:n], k == 0, k == 7, reads=Wk + ['hs_t'], writes=[pk])
                            S.op('act', lambda: A.copy(out=zs[:, c0:c0 + n], in_=ps[0:NS, 0:n]), reads=[pk], writes=['zs'])

                        def rope_tok(dst, src, nh, key_w):
                            s3 = src.rearrange("p (h d) -> p h d", h=nh)
                            d3 = dst.rearrange("p (h d) -> p h d", h=nh)
                            t3 = tm[:, 0:nh * 64].rearrange("p (h d) -> p h d", h=nh)
                            S.op('dve', lambda: V.tensor_tensor(out=t3[:, :, 0:32], in0=s3[:, :, 32:64], in1=s2s[:, 0:32].unsqueeze(1).to_broadcast([NS, nh, 32]), op=ALU.mult),
                                 reads=['zs', 's2s'], writes=['tm'])
                            S.op('dve', lambda: V.tensor_tensor(out=t3[:, :, 32:64], in0=s3[:, :, 0:32], in1=s2s[:, 32:64].unsqueeze(1).to_broadcast([NS, nh, 32]), op=ALU.mult),
                                 reads=['zs', 's2s'], writes=['tm'])
                            S.op('dve', lambda: V.tensor_tensor(out=d3, in0=s3, in1=c2s[:].unsqueeze(1).to_broadcast([NS, nh, 64]), op=ALU.mult),
                                 reads=['zs', 'c2s'], writes=[key_w])
                            S.op('dve', lambda: V.tensor_tensor(out=d3, in0=d3, in1=t3, op=ALU.add), reads=[key_w, 'tm'], writes=[key_w])

                        rope_tok(qr[:], zs[:, 0:1024], 16, 'qr')
                        rope_tok(zs[:, 1536:1792], zs[:, 1536:1792], 4, 'zs')
                        rope_tok(zs[:, 2048:2304], zs[:, 2048:2304], 4, 'zs')
                        S.dma(nsa_s[0][o], zs[:, 1024:1280], reads=['zs'])
                        S.dma(nsa_s[1][o], zs[:, 1280:1536], reads=['zs'])
                        S.dma(nsa_s[2][o], zs[:, 1536:1792], reads=['zs'])
                        S.dma(nsa_s[3][o], zs[:, 1792:2048], reads=['zs'])
                        for X in range(2):
                            S.dma(win_s[X][o, :, 0:511, :], state_win[X][o, :, 1:512, :])
                            S.dma(win_s[X][o, :, 511, :], zs[:, 2048 + 256 * X:2304 + 256 * X], reads=['zs'])
                        S.op('act', lambda: A.activation(out=gs_t[:], in_=zs[:, 2560:2608], func=AF.Sigmoid), reads=['zs'], writes=['gs_t'])
                        for br in range(3):
                            gsrc = gs_t[:].rearrange("p (g h m c) -> p g h m c", g=2, h=2, m=4)[:, :, :, :, br]
                            for g_ in range(2):
                                S.op('dve', lambda: V.tensor_copy(out=gperm[:, br, g_ * 8:(g_ + 1) * 8].rearrange("p (m h) -> p h m", m=4, h=2), in_=gsrc[:, g_]),
                                     reads=['gs_t'], writes=['gperm'])
                        S.op('act', lambda: A.copy(out=vnb[:, 0:256], in_=zs[:, 1792:2048]), reads=['zs'], writes=['vnb'])
                        S.op('act', lambda: A.copy(out=vnb[:, 256:512], in_=zs[:, 2304:2560]), reads=['zs'], writes=['vnb'])
                        for (src, srck, dstT, dk) in ((zs, 'zs', qsT, 'qsT'), (qr, 'qr', qrsT, 'qrsT')):
                            for g_ in range(2):
                                S.op('dve', lambda: V.tensor_copy(out=qp[:, g_ * 512:(g_ + 1) * 512].rearrange("p (m h d) -> p h m d", m=4, h=2),
                                                                   in_=src[:, g_ * 512:(g_ + 1) * 512].rearrange("p (h m d) -> p h m d", h=2, m=4)),
                                     reads=[srck], writes=['tm'])
                            ps, pk = PS()
                            for c in range(8):
                                tr(ps[:, c * NS:(c + 1) * NS], qp[:, c * 128:(c + 1) * 128], ident32[0:NS, 0:NS], reads=['tm', 'ident32'], writes=[pk], inc=(c == 7))
                            S.op('act', lambda: A.copy(out=dstT[:], in_=ps[:, 0:8 * NS].rearrange("p (a b) -> p a b", a=8)), reads=[pk], writes=[dk])
                        ps, pk = PS()
                        for ci, c0 in enumerate((1536, 1664, 2048, 2176)):
                            tr(ps[:, ci * NS:(ci + 1) * NS], zs[:, c0:c0 + 128], ident32[0:NS, 0:NS], reads=['zs', 'ident32'], writes=[pk], inc=(ci == 3))
                        S.op('act', lambda: A.copy(out=knT[:], in_=ps[:, 0:4 * NS].rearrange("p (a b) -> p a b", a=4)), reads=[pk], writes=['knT'])

                    S.serial_compute = int(os.environ.get("K_SSER", 0))

                    def transposeX(st, stks, ntile, dstT, dkey):
                        for j0 in range(0, ntile, 2):
                            ps, pk = PS()
                            for jj in range(2):
                                for c in range(2):
                                    tr(ps[:, (jj * 2 + c) * 128:(jj * 2 + c + 1) * 128], st[:, j0 + jj, c * 128:(c + 1) * 128], ident32[:],
                                       reads=[stks[j0 + jj], 'ident32'], writes=[pk], inc=(jj == 1 and c == 1))
                            S.op('act' if (j0 // 2) % 2 == 0 else 'dve',
                                 (lambda: A.copy(out=dstT[:, :, j0 * 128:(j0 + 2) * 128].rearrange("p c (jj t) -> p c jj t", jj=2),
                                                 in_=ps[:].rearrange("p (jj c t) -> p c jj t", jj=2, c=2))) if (j0 // 2) % 2 == 0 else
                                 (lambda: V.tensor_copy(out=dstT[:, :, j0 * 128:(j0 + 2) * 128].rearrange("p c (jj t) -> p c jj t", jj=2),
                                                        in_=ps[:].rearrange("p (jj c t) -> p c jj t", jj=2, c=2))),
                                 reads=[pk], writes=[dkey])

                    with phase() as p1:
                      if SST >= 2:
                        w1ts = [sb("w1ts%d" % X, [128, 32, 128], BF16, p1) for X in range(2)]
                        w2ts = [sb("w2ts%d" % X, [128, 64], BF16, p1) for X in range(2)]
                        peTs = sb("peTs", [128, 32], BF16, p1)
                        hbs = [sb("hbs%d" % X, [128, 1], F32, p1) for X in range(2)]
                        for X in range(2):
                            for half in range(2):
                                S.dma(w1ts[X][half * 64:(half + 1) * 64, :, :], cmp_w1[X][o].rearrange("l d e -> d l e"), writes=[('w1ts', X, half)], q='pool')
                            S.dma(peTs[0:64, :], cmp_pe[X][o].rearrange("l d -> d l"), writes=['peTs'], q='pool', allow_slow_non_contiguous=True)
                            S.dma(w2ts[X][:], cmp_w2[X][o], writes=[('w2ts', X)], q='pool')
                            ps, pk = PS()
                            for li in range(32):
                                mm(ps[:, 0:1], w1ts[X][0:64, li, :], peTs[0:64, li:li + 1], li == 0, li == 31, reads=[('w1ts', X, 0), 'peTs'], writes=[pk])
                            S.op('dve', lambda: V.tensor_copy(out=hbs[X][:], in_=ps[:, 0:1]), reads=[pk], writes=[('hbs', X)])
                        for b in range(NS):
                            for X in range(2):
                                si = (2 * b + X) % 2
                                gather(caches[X], b, stg[si], stgk[si])
                                if P1 >= 2:
                                    transposeX(stg[si], stgk[si], 16, XTs, 'XTs')
                                for kv in range(4 if P1 >= 3 else 0):
                                    half = kv % 2
                                    hs = slice(half * 64, half * 64 + 64)
                                    X3 = XTs[hs, kv // 2, :].rearrange("p (c s) -> p c s", s=16)
                                    ps, pk = PS()
                                    for li in range(32):
                                        mm(ps[:, 0:127], w1ts[X][hs, li, :], X3[:, li // 16:li // 16 + 127, li % 16], li == 0, li == 31,
                                           reads=[('w1ts', X, half), 'XTs'], writes=[pk])
                                    hg, hgk = s16.get()
                                    S.op('act', lambda: A.activation(out=hg[:, 0:127], in_=ps[:, 0:127], func=AF.Gelu_apprx_tanh, bias=hbs[X][:, 0:1], scale=1.0),
                                         reads=[pk, ('hbs', X)], writes=[hgk])
                                    ps2, pk2 = PS()
                                    if X == 0:
                                        mm(ps2[hs, 0:127], w2ts[0][:, 0:64], hg[:, 0:127], True, True, reads=[('w2ts', 0), hgk], writes=[pk2])
                                        S.op('dve', lambda: V.tensor_copy(out=kcs[hs, kv // 2, 0:127], in_=ps2[hs, 0:127]), reads=[pk2], writes=['kcs'])
                                    else:
                                        mm(ps2[0:127, 0:64], hg[:, 0:127], w2ts[1][:, 0:64], True, True, reads=[('w2ts', 1), hgk], writes=[pk2])
                                        S.op('dve', lambda: V.tensor_copy(out=vcs[0:127, kv, :], in_=ps2[0:127, 0:64]), reads=[pk2], writes=['vcs'])
                            for mg in range(2 if P1 >= 4 else 0):
                                for half in range(2):
                                    hs = slice(half * 64, half * 64 + 64)
                                    ps_s, psk = PS()
                                    mm(ps_s[0:127, 0:4], kcs[hs, mg, 0:127], qsT[hs, 4 * mg:4 * mg + 4, b], True, True,
                                       reads=['kcs', 'qsT'], writes=[psk])
                                    S.op('act', lambda: A.activation(out=Pc[0:127, half * 4:half * 4 + 4], in_=ps_s[0:127, 0:4], func=AF.Exp, scale=SC), reads=[psk], writes=['Pc'])
                                ps_m, pmk = PS()
                                mm(ps_m[:, 0:8], onesb[0:127, :], Pc[0:127, 0:8], True, True, reads=['onesb', 'Pc'], writes=[pmk])
                                S.op('dve', lambda: V.reciprocal(out=recs[:], in_=ps_m[:, 0:8]), reads=[pmk], writes=['recs'])
                                S.op('dve', lambda: V.tensor_tensor(out=pns[0:127, :], in0=Pc[0:127, 0:8], in1=recs[0:127, :], op=ALU.mult), reads=['Pc', 'recs'], writes=['pns'])
                                S.op('dve', lambda: V.tensor_reduce(out=pgall[0:127, 2 * mg:2 * mg + 2, b], in_=pns[0:127, :].rearrange("p (h j) -> p h j", h=2),
                                                                     axis=AX.X, op=ALU.add), reads=['pns'], writes=['pgall'])
                                ps_o, pok = PS()
                                for half in range(2):
                                    hs = slice(half * 64, half * 64 + 64)
                                    mm(ps_o[hs, 0:4], vcs[0:127, 2 * mg + half, :], Pc[0:127, half * 4:half * 4 + 4], True, True,
                                       reads=['vcs', 'Pc'], writes=[pok], inc=(half == 1))
                                for half in range(2):
                                    hs = slice(half * 64, half * 64 + 64)
                                    S.op('dve', lambda: V.tensor_tensor(out=oTb[0][hs, 4 * mg:4 * mg + 4, b], in0=ps_o[hs, 0:4], in1=recs[hs, half * 4:half * 4 + 4], op=ALU.mult),
                                         reads=[pok, 'recs'], writes=['oTb0'])
                    scs = sb("scs", [NS, 33], F32, sp_)
                    for kv in range(4 if SST >= 3 else 0):
                        ps, pk = PS()
                        mm(ps[0:NS, 0:33], pgall[0:127, kv, :], ov33[0:127, :], True, True, reads=['pgall', 'ov33'], writes=[pk])
                        S.op('dve', lambda: V.tensor_tensor(out=scs[:], in0=ps[0:NS, 0:33], in1=sAs[:], op=ALU.mult), reads=[pk, 'sAs'], writes=['scs'])
                        S.op('dve', lambda: V.tensor_tensor(out=scs[:], in0=scs[:], in1=sBs[:], op=ALU.add), reads=['scs', 'sBs'], writes=['scs'])
                        S.op('dve', lambda: V.max(out=recs[0:NS, 0:8], in_=scs[:]), reads=['scs'], writes=['recs'])
                        S.op('dve', lambda: V.tensor_scalar(out=scs[:], in0=scs[:], scalar1=recs[0:NS, 7:8], scalar2=None, op0=ALU.is_ge), reads=['scs', 'recs'], writes=['scs'])
                        ps, pk = PS()
                        tr(ps[0:33, 0:NS], scs[:], ident32[0:NS, 0:NS], reads=['scs', 'ident32'], writes=[pk])
                        S.op('act', lambda: A.copy(out=selbTs[:, kv, :], in_=ps[0:33, 0:NS]), reads=[pk], writes=['selbTs'])
                    with phase() as p2:
                        svs = sb("svs", [128, 16, 256], BF16, p2)
                        svk = [('svs', j) for j in range(16)]
                        wst = sb("wst", [128, 4, 256], F32, p2)
                        wvs = sb("wvs", [128, 4, 256], BF16, p2)
                        wkTs = sb("wkTs", [128, 2, 512], BF16, p2)
                        for b in range(NS if SST >= 4 else 0):
                            si = b % 2
                            gather(caches[2], b, stg[si], stgk[si])
                            gather(caches[3], b, svs, svk)
                            S.dma(wst[:], state_win[0][o, b].rearrange("(t p) f -> p t f", p=128), writes=['wst'])
                            S.dma(wvs[:], state_win[1][o, b].rearrange("(t p) f -> p t f", p=128), writes=['wvs'], q='pool')
                            transposeX(stg[si], stgk[si], 16, XTs, 'XTs')
                            transposeX(wst, ['wst'] * 4, 4, wkTs, 'wkTs')
                            for mg in range(2):
                                for br, KTs, ktk, VBs, vks, nkt, knc, vnc in ((1, XTs, 'XTs', svs, svk, 16, 0, 0), (2, wkTs, 'wkTs', wvs, ['wvs'] * 4, 4, 2, 256)):
                                    c0 = nkt * 4
                                    for half in range(2):
                                        hs = slice(half * 64, half * 64 + 64)
                                        kv = 2 * mg + half
                                        pcb = half * 72
                                        ps_s, psk = PS()
                                        for kt in range(nkt):
                                            mm(ps_s[:, kt * 4:kt * 4 + 4], KTs[hs, mg, kt * 128:(kt + 1) * 128], qrsT[hs, 4 * mg:4 * mg + 4, b], True, True,
                                               reads=[ktk, 'qrsT'], writes=[psk], inc=False)
                                        mm(ps_s[0:NS, c0:c0 + 4], knT[hs, knc + mg, :], qrsT[hs, 4 * mg:4 * mg + 4, b], True, True, reads=['knT', 'qrsT'], writes=[psk])
                                        S.op('act', lambda: A.activation(out=Pc[:, pcb:pcb + c0], in_=ps_s[:, 0:c0], func=AF.Exp, scale=SC), reads=[psk], writes=['Pc'])
                                        S.op('act', lambda: A.activation(out=Pc[0:NS, pcb + c0:pcb + c0 + 4], in_=ps_s[0:NS, c0:c0 + 4], func=AF.Exp, scale=SC), reads=[psk], writes=['Pc'])
                                        S.op('dve', lambda: V.tensor_scalar(out=Pc[0:NS, pcb + c0:pcb + c0 + 4], in0=Pc[0:NS, pcb + c0:pcb + c0 + 4],
                                                                             scalar1=ident32[0:NS, b:b + 1], scalar2=None, op0=ALU.mult), reads=['Pc', 'ident32'], writes=['Pc'])
                                        if br == 1:
                                            ps_k, pkk = PS()
                                            for kt in range(nkt):
                                                mm(ps_k[:, kt:kt + 1], Emat_s[0:32, kt, :], selbTs[0:32, kv, b:b + 1], True, True, reads=['Emat_s', 'selbTs'], writes=[pkk], inc=(kt == nkt - 1))
                                            S.op('dve', lambda: V.tensor_tensor(out=Pc[:, pcb:pcb + c0].rearrange("p (k h) -> p k h", h=4),
                                                                                 in0=Pc[:, pcb:pcb + c0].rearrange("p (k h) -> p k h", h=4),
                                                                                 in1=ps_k[:, 0:nkt].unsqueeze(2).to_broadcast([128, nkt, 4]), op=ALU.mult),
                                                 reads=['Pc', pkk], writes=['Pc'])
                                    ps_o, pok = PS()
                                    ps_m, pmk = PS()
                                    for half in range(2):
                                        hs = slice(half * 64, half * 64 + 64)
                                        kv = 2 * mg + half
                                        pcb = half * 72
                                        for kt in range(nkt):
                                            cols = slice(pcb + kt * 4, pcb + kt * 4 + 4)
                                            mm(ps_o[hs, 0:4], VBs[:, kt, kv * 64:(kv + 1) * 64], Pc[:, cols], kt == 0, False, reads=[vks[kt], 'Pc'], writes=[pok])
                                            mm(ps_m[hs, 0:4], onesb[:, 0:64], Pc[:, cols], kt == 0, False, reads=['onesb', 'Pc'], writes=[pmk])
                                        cols = slice(pcb + c0, pcb + c0 + 4)
                                        mm(ps_o[hs, 0:4], vnb[0:NS, vnc + kv * 64:vnc + (kv + 1) * 64], Pc[0:NS, cols], False, True, reads=['vnb', 'Pc'], writes=[pok])
                                        mm(ps_m[hs, 0:4], onesb[0:NS, 0:64], Pc[0:NS, cols], False, True, reads=['onesb', 'Pc'], writes=[pmk])
                                    S.op('dve', lambda: V.reciprocal(out=recs[:, 0:4], in_=ps_m[:, 0:4]), reads=[pmk], writes=['recs'])
                                    S.op('dve', lambda: V.tensor_tensor(out=oTb[br][:, 4 * mg:4 * mg + 4, b], in0=ps_o[:, 0:4], in1=recs[:, 0:4], op=ALU.mult),
                                         reads=[pok, 'recs'], writes=['oTb%d' % br])
                    with phase() as p3:
                      if SST >= 5:
                        otk = sb("otk", [NS, 1024], F32, p3)
                        osum = sb("osum", [NS, 1024], F32, p3)
                        oTs = sb("oTs", [128, 8, NS], BF16, p3)
                        for br in range(3):
                            for half_ in range(2):
                                ps, pk = PS()
                                for c in range(4):
                                    cc = half_ * 4 + c
                                    tr(ps[0:NS, c * 128:(c + 1) * 128], oTb[br][:, cc, :], ident32[:], reads=['oTb%d' % br, 'ident32'], writes=[pk], inc=(c == 3))
                                S.op('act', lambda: A.copy(out=otk[:, half_ * 512:(half_ + 1) * 512], in_=ps[0:NS, :]), reads=[pk], writes=['otk'])
                            gb = gperm[:, br, :].unsqueeze(2).to_broadcast([NS, 16, 64])
                            o3 = otk[:].rearrange("p (h d) -> p h d", h=16)
                            if br == 0:
                                S.op('dve', lambda: V.tensor_tensor(out=osum[:].rearrange("p (h d) -> p h d", h=16), in0=o3, in1=gb, op=ALU.mult), reads=['otk', 'gperm'], writes=['osum'])
                            else:
                                S.op('dve', lambda: V.tensor_tensor(out=o3, in0=o3, in1=gb, op=ALU.mult), reads=['otk', 'gperm'], writes=['otk'])
                                S.op('dve', lambda: V.tensor_tensor(out=osum[:], in0=osum[:], in1=otk[:], op=ALU.add), reads=['otk', 'osum'], writes=['osum'])
                        ps, pk = PS()
                        for c in range(8):
                            tr(ps[:, c * NS:(c + 1) * NS], osum[:, c * 128:(c + 1) * 128], ident32[0:NS, 0:NS], reads=['osum', 'ident32'], writes=[pk], inc=(c == 7))
                        S.op('act', lambda: A.copy(out=oTs[:], in_=ps[:, 0:8 * NS].rearrange("p (a b) -> p a b", a=8)), reads=[pk], writes=['oTs'])
                        Wo1, Wo1k = w_next()
                        Wo2, Wo2k = w_next()
                        ps, pk = PS()
                        for i in range(8):
                            for m_ in range(8):
                                Wo, Wok = (Wo1, Wo1k) if m_ < 4 else (Wo2, Wo2k)
                                mm(ps[:, i * NS:(i + 1) * NS], Wo[:, m_ % 4, i * 128:(i + 1) * 128], oTs[:, m_, :], m_ == 0, m_ == 7, reads=Wok + ['oTs'], writes=[pk])
                        t, tk = s32.get()
                        tv = t[:, 0:8 * NS].rearrange("p (a b) -> p a b", a=8)
                        S.op('dve', lambda: V.tensor_tensor(out=tv, in0=ps[:, 0:8 * NS].rearrange("p (a b) -> p a b", a=8), in1=m[:, g1:g1 + 8, 1:17], op=ALU.mult),
                             reads=[pk, ('mod', l % 2)], writes=[tk])
                        S.op('dve', lambda: V.tensor_tensor(out=xsT[:], in0=xsT[:], in1=tv, op=ALU.add), reads=[tk, 'xs'], writes=['xs'])
                S.serial_compute = False
                S.op('dve', lambda: V.memset(hflat[:, 16384:16392], 0.0), writes=allh + arena + [('stg', i, j) for i in range(2) for j in range(16)])

        def final():
            with phase() as ph:
                yt = [sb("yt%d" % i, [128, D], F32, ph) for i in range(2)]
                yf = sb("yf", [128, 8, TT], F32, ph)
                for tt in range(NT):
                    cols = slice(tt * TT, (tt + 1) * TT)
                    ps, pk = PS()
                    for c in range(8):
                        sq, sqk = s16.get()
                        S.op('act', lambda: A.activation(out=sq[:], in_=xT[:, c, cols], func=AF.Square), reads=xkeys[tt], writes=[sqk])
                        mm(ps[:], onesb[:], sq[:], c == 0, c == 7, reads=[sqk, 'onesb'], writes=[pk], inc=True)
                    rs, rsk = rstd_t, 'rstd_t'
                    S.op('act', lambda: A.activation(out=rs[:], in_=ps[:], func=AF.Sqrt, bias=eps_t[:, 0:1], scale=1.0 / D), reads=[pk, 'eps'], writes=[rsk])
                    S.op('dve', lambda: V.reciprocal(out=rs[:], in_=rs[:]), reads=[rsk], writes=[rsk])
                    for c in range(8):
                        S.op('dve', lambda: V.scalar_tensor_tensor(out=yf[:, c, :], in0=xT[:, c, cols], scalar=pfin[:, c:c + 1], in1=rs[:],
                                                                    op0=ALU.mult, op1=ALU.mult), reads=xkeys[tt] + [rsk, 'pfin'], writes=[('yf', c)])
                    for sub in range(4):
                        y = yt[sub % 2]
                        yk = ('yt', sub % 2)
                        for half in range(2):
                            ps, pk = PS()
                            for q in range(4):
                                c = half * 4 + q
                                tr(ps[:, q * 128:(q + 1) * 128], yf[:, c, sub * 128:(sub + 1) * 128], ident32[:],
                                   reads=[('yf', c), 'ident32'], writes=[pk], inc=(q == 3))
                            if half == 0:
                                S.op('act', lambda: A.copy(out=y[:, 0:512], in_=ps[:]), reads=[pk], writes=[yk])
                            else:
                                S.op('dve', lambda: V.tensor_copy(out=y[:, 512:1024], in_=ps[:]), reads=[pk], writes=[yk])
                        r0 = tt * TT + sub * 128
                        S.dma(y_prompt[r0:r0 + 128, :], y[:], reads=[yk])
                ps, pk = PS()
                sq, sqk = s16.get()
                S.op('act', lambda: A.activation(out=sq[:, 0:8 * NS], in_=xsT[:].rearrange("p a b -> p (a b)"), func=AF.Square), reads=['xs'], writes=[sqk])
                for c in range(8):
                    mm(ps[:, 0:NS], onesb[:], sq[:, c * NS:(c + 1) * NS], c == 0, c == 7, reads=[sqk, 'onesb'], writes=[pk])
                rs, rsk = s32.get()
                S.op('act', lambda: A.activation(out=rs[:, 0:NS], in_=ps[:, 0:NS], func=AF.Sqrt, bias=eps_t[:, 0:1], scale=1.0 / D), reads=[pk, 'eps'], writes=[rsk])
                S.op('dve', lambda: V.reciprocal(out=rs[:, 0:NS], in_=rs[:, 0:NS]), reads=[rsk], writes=[rsk])
                t, tk = s32.get()
                tv = t[:, 0:8 * NS].rearrange("p (a b) -> p a b", a=8)
                S.op('dve', lambda: V.tensor_tensor(out=tv, in0=xsT[:], in1=rs[:, 0:NS].unsqueeze(1).to_broadcast([128, 8, NS]), op=ALU.mult), reads=['xs', rsk], writes=[tk])
                S.op('dve', lambda: V.tensor_tensor(out=tv, in0=tv, in1=pfin[:].unsqueeze(2).to_broadcast([128, 8, NS]), op=ALU.mult), reads=[tk, 'pfin'], writes=[tk])
                ps, pk = PS()
                ps2, pk2 = PS()
                for c in range(8):
                    pp = ps if c < 4 else ps2
                    tr(pp[0:NS, (c % 4) * 128:(c % 4 + 1) * 128], t[:, c * NS:(c + 1) * NS], ident32[:], reads=[tk, 'ident32'],
                       writes=[pk if c < 4 else pk2], inc=(c % 4 == 3))
                y = yt[0]
                S.op('act', lambda: A.copy(out=y[0:NS, 0:512], in_=ps[0:NS, :]), reads=[pk], writes=[('yt', 0)])
                S.op('dve', lambda: V.tensor_copy(out=y[0:NS, 512:1024], in_=ps2[0:NS, :]), reads=[pk2], writes=[('yt', 0)])
                S.dma(y_sample, y[0:NS, :], reads=[('yt', 0)])

        for l in range(n_layers):
            ada_finish(l)
            if l % 2 == 0:
                norm_pass(l, 0)
                even_layer(l)
            elif do_odd:
                odd_layer(l)
            norm_pass(l, 1)
            mlp(l)
        final()
        assert wstate['used'] == len(wplan), (wstate, len(wplan))
        S.finish()
        print("instr counts", S.n_ins, "waits", S.n_wait, flush=True)
    return nc


_OUT_NAMES = ["y_prompt", "y_sample", "conv_p", "conv_s", "chunkv_p", "chunkv_s"]


def kernel(**inputs):
    f = lambda k: np.ascontiguousarray(np.asarray(inputs[k]))
    n_layers = int(os.environ.get("K_LAYERS", DEPTH))
    nc = build_program(n_layers=n_layers)
    p_layer = np.zeros((DEPTH, 128, 64), np.float32)
    p_layer[:, :, 0:48] = f("ada_b").reshape(DEPTH, 48, 128).transpose(0, 2, 1)
    p_layer[:, :, 48:56] = f("norm_mix_g").reshape(DEPTH, 8, 128).transpose(0, 2, 1)
    p_layer[:, :, 56:64] = f("norm_ffn_g").reshape(DEPTH, 8, 128).transpose(0, 2, 1)
    p_even = np.zeros((2, 128, 136), np.float32)
    p_even[:, :, 0:4] = f("conv_b").reshape(2, 4, 128).transpose(0, 2, 1)
    p_even[:, :, 4:8] = f("conv_ln_g").reshape(2, 4, 128).transpose(0, 2, 1)
    p_even[:, :, 8:12] = f("conv_ln_b").reshape(2, 4, 128).transpose(0, 2, 1)
    p_even[:, :, 12:136] = f("conv_w").reshape(2, 31, 4, 128).transpose(0, 3, 1, 2).reshape(2, 128, 124)
    p_final = np.ascontiguousarray(f("final_norm_g").reshape(8, 128).T)
    sgu_wT = np.ascontiguousarray(f("sgu_w").transpose(0, 3, 1, 2))
    cst = np.zeros((128, 256), np.float32)
    cst[:, 0:128] = np.eye(128, dtype=np.float32)
    cst[:, 128:256] = np.triu(np.ones((128, 128), np.float32))
    half = 32
    inv = np.power(np.float32(10000.0), -np.arange(half, dtype=np.float32) * np.float32(2.0 / 64))
    pos = np.arange(SEQ + 1, dtype=np.float32)
    ang = pos[:, None] * inv[None, :]
    cosv, sinv = np.cos(ang).astype(np.float32), np.sin(ang).astype(np.float32)
    rope_cos = np.concatenate([cosv, cosv], 1)
    rope_sin = np.concatenate([-sinv, sinv], 1)
    tpos = np.arange(SEQ)[:, None]
    blk = np.arange(32)[None, :]
    cur = tpos // 64
    valid = blk * 64 <= tpos
    forced = (blk == 0) | (blk == cur) | (blk == cur - 1)
    selA = (valid & ~forced).astype(np.float32)
    selB = np.where(valid, np.where(forced, 1e4, 0.0), -1e30).astype(np.float32)
    ncmp = np.arange(127)[:, None]
    cmpb = np.where(16 * ncmp + 31 <= np.arange(SEQ)[None, :], 0.0, -30000.0).astype(np.float32)
    key = np.arange(128)[:, None, None]
    dd = np.array([-4, -3, 0, 1])[None, :, None]
    tq = np.arange(256)[None, None, :]
    diff = tq - key - 128 * dd
    bandb = np.where((diff >= 0) & (diff <= 512), 0.0, -30000.0).astype(np.float32)
    jj = np.arange(32)[:, None, None]
    ktt = np.arange(16)[None, :, None]
    kk = np.arange(128)[None, None, :]
    Emat = (jj == 2 * ktt + kk // 64).astype(np.float32)
    ci = np.arange(127)[:, None] * 16
    sj = np.arange(32)[None, :] * 64
    ov = ((ci < sj + 64) & (ci + 32 > sj)).astype(np.float32)
    ci33 = np.arange(127)[:, None] * 16
    sj33 = np.arange(33)[None, :] * 64
    ov33 = ((ci33 < sj33 + 64) & (ci33 + 32 > sj33)).astype(np.float32)
    j33 = np.arange(33)
    forced33 = (j33 == 0) | (j33 == 32) | (j33 == 31)
    selA_s = (~forced33).astype(np.float32)[None, :]
    selB_s = np.where(forced33, 1e4, 0.0).astype(np.float32)[None, :]
    nb16 = np.where(np.eye(16, dtype=bool), 0.0, -30000.0).astype(np.float32)
    NPOOL = int(os.environ.get("K_POOLPAGES", 2560))
    poff = (np.arange(2)[None, :] * (NPOOL * 128) + np.arange(128)[:, None]).astype(np.float32)
    shared = dict(ada_w=f("ada_w"), ffn_w1=f("ffn_w1"), ffn_w2=f("ffn_w2"), even_w_in=f("even_w_in"),
                  even_w_out=f("even_w_out"), p_layer=p_layer, p_even=p_even, p_final=p_final,
                  conv_w=f("conv_w"), conv_b=f("conv_b"), conv_ln_g=f("conv_ln_g"), conv_ln_b=f("conv_ln_b"),
                  sgu_ln_g=f("sgu_ln_g"), sgu_ln_b=f("sgu_ln_b"), sgu_wT=sgu_wT, sgu_b=f("sgu_b"), cst=cst,
                  odd_w_in=f("odd_w_in"), odd_w_out=f("odd_w_out"),
                  cmp_pe_k=f("cmp_pe_k"), cmp_pe_v=f("cmp_pe_v"), cmp_w1_k=f("cmp_w1_k"), cmp_w1_v=f("cmp_w1_v"),
                  cmp_w2_k=f("cmp_w2_k"), cmp_w2_v=f("cmp_w2_v"),
                  rope_cos=rope_cos, rope_sin=rope_sin, selA=selA, selB=selB, cmpb=cmpb, bandb=bandb, Emat=Emat, ov=ov,
                  ov33=ov33, selA_s=selA_s, selB_s=selB_s, nb16=nb16, poff=poff,
                  cache_cmp_k=np.ascontiguousarray(f("cache_cmp_k")[:, :NPOOL]).reshape(-1, 256), cache_cmp_v=np.ascontiguousarray(f("cache_cmp_v")[:, :NPOOL]).reshape(-1, 256),
                  cache_sel_k=np.ascontiguousarray(f("cache_sel_k")[:, :NPOOL]).reshape(-1, 256), cache_sel_v=np.ascontiguousarray(f("cache_sel_v")[:, :NPOOL]).reshape(-1, 256))
    xp, xs = f("x_prompt"), f("x_sample")
    cp, cs = f("c_prompt"), f("c_sample")
    stc = f("state_conv")
    swk, swv, ptab = f("state_win_k"), f("state_win_v"), f("page_table").astype(np.int32)
    in_maps = []
    for i in range(NCORES):
        sl = slice(i * NS, (i + 1) * NS)
        m = dict(shared)
        m["x_prompt"] = xp[i]
        m["x_sample"] = xs[sl, 0, :]
        m["c_all"] = np.concatenate([cp[i:i + 1], cs[sl]], axis=0)
        m["state_conv"] = np.ascontiguousarray(stc[:, sl])
        m["state_win_k"] = np.ascontiguousarray(swk[:, sl]).reshape(2, NS, 512, 256)
        m["state_win_v"] = np.ascontiguousarray(swv[:, sl]).reshape(2, NS, 512, 256)
        m["page_table"] = np.ascontiguousarray(ptab[sl] % NPOOL) if NPOOL != 2560 else np.ascontiguousarray(ptab[sl])
        in_maps.append(m)
    res = run_bass_kernel_spmd(nc, in_maps, core_ids=list(range(NCORES)))
    if os.environ.get("K_PRINT_TIME"):
        print("EXEC_TIME_NS", getattr(res, "exec_time_ns", None), flush=True)
    R = res.results
    g = lambda name: [np.asarray(r[name]) for r in R]
    y_prompt = np.stack(g("y_prompt"), 0)
    y_sample = np.concatenate(g("y_sample"), 0)[:, None, :]
    conv_p = np.stack(g("conv_p"), 1)
    conv_s = np.concatenate(g("conv_s"), 1)
    chunkv_p = np.stack(g("chunkv_p"), 1)
    chunkv_s = np.concatenate(g("chunkv_s"), 1)[:, :, None, :]
    z = lambda *s: np.zeros(s, np.float32)
    pk = lambda name, T_: np.stack(g(name), 1).reshape(2, NCORES, T_, 4, 64)
    outs = [y_prompt, y_sample, conv_p, conv_s, chunkv_p, chunkv_s,
            pk("cmp_k_p", SEQ), pk("cmp_v_p", SEQ), pk("sel_k_p", SEQ), pk("sel_v_p", SEQ),
            pk("win_k_p", 512), pk("win_v_p", 512),
            *[np.concatenate(g(n), 1).reshape(2, NCORES * NS, 1, 4, 64) for n in ("cmp_k_s", "cmp_v_s", "sel_k_s", "sel_v_s")],
            *[np.concatenate(g(n), 1).reshape(2, NCORES * NS, 512, 4, 64) for n in ("win_k_s", "win_v_s")]]
    return tuple(np.ascontiguousarray(o, dtype=np.float32) for o in outs)
```

```python
import os
import numpy as np
from contextlib import ExitStack
import concourse.bass as bass
import concourse.mybir as mybir
from concourse.bass_utils import run_bass_kernel_spmd

F32 = mybir.dt.float32
BF16 = mybir.dt.bfloat16
I32 = mybir.dt.int32
AF = mybir.ActivationFunctionType
ALU = mybir.AluOpType
AX = mybir.AxisListType

NCORES = 8
D = 1024
SEQ = 2048
NS = 16
DEPTH = 4
EPS = 1e-6
TT = 512
NT = SEQ // TT
HC = SEQ + NS


class Sched:
    SAME_ENGINE_SYNC = True

    def __init__(self, nc, es, n_dma_slots=40):
        self.nc = nc
        self.engs = {'pe': nc.tensor, 'act': nc.scalar, 'dve': nc.vector, 'pool': nc.gpsimd, 'sp': nc.sync}
        self.sem = {}
        self.cnt = {}
        for k in self.engs:
            self.sem[k] = es.enter_context(nc.semaphore("s_" + k))
            self.cnt[k] = 0
        self.ndma = n_dma_slots
        for i in range(n_dma_slots):
            k = ('dma', i)
            self.sem[k] = es.enter_context(nc.semaphore("s_dma%d" % i))
            self.cnt[k] = 0
        self.dma_next = 0
        self.waited = {}
        self.res = {}
        self.n_wait = 0
        self.serial_compute = False
        self.epoch = 0
        self.fence_deps = {}
        self.key_epoch = {}
        self.n_ins = {k: 0 for k in self.engs}

    def _deps(self, reads, writes, e=None):
        deps = {}

        def add(kc):
            if kc is None:
                return
            k, c = kc
            if deps.get(k, 0) < c:
                deps[k] = c
        for r in reads:
            st = self.res.get(r)
            if st:
                add(st[0])
                if not isinstance(r, str) and r[0] == 'ps':
                    for k, c in st[1].items():
                        if k != e:
                            add((k, c))
        for w in writes:
            st = self.res.get(w)
            if st:
                add(st[0])
                for k, c in st[1].items():
                    add((k, c))
            name = w if isinstance(w, str) else w[0]
            if name not in self.PERSIST and self.key_epoch.get(w) != self.epoch:
                self.key_epoch[w] = self.epoch
                for k, c in self.fence_deps.items():
                    add((k, c))
        return deps

    PERSIST = {'x', 'xs', 'h', 'w', 'mod', 'gs', 'pl', 'pfin', 'scT', 'ident32', 'identb', 'trib', 'onesb',
               'eps', 's32_', 's16_', 'rstd_t', 'ps'}

    def fence(self):
        self.epoch += 1
        self.fence_deps = {k: c for k, c in self.cnt.items() if c > 0}

    def _emit_waits(self, e, deps):
        eng = self.engs[e]
        for k, c in deps.items():
            if k == e and (not self.SAME_ENGINE_SYNC or e == 'pe'):
                continue
            if self.waited.get((e, k), 0) >= c:
                continue
            eng.wait_ge(self.sem[k], c)
            self.n_wait += 1
            self.waited[(e, k)] = c

    def _record(self, k, c, reads, writes):
        for r in reads:
            st = self.res.setdefault(r, [None, {}])
            if st[1].get(k, 0) < c:
                st[1][k] = c
        for w in writes:
            self.res[w] = [(k, c), {}]

    SERIAL = set(os.environ.get("K_SERIAL", "").split(",")) - {""}

    def _all(self, deps):
        for k, c in self.cnt.items():
            if c > 0 and deps.get(k, 0) < c:
                deps[k] = c
        return deps

    def op(self, e, fn, reads=(), writes=(), inc=True):
        deps = self._deps(reads, writes, e)
        if e in self.SERIAL or "all" in self.SERIAL:
            deps = self._all(deps)
        if self.serial_compute == 2:
            deps = self._all(deps)
        elif self.serial_compute:
            for k in ('pe', 'act', 'dve'):
                c = self.cnt[k]
                if c > 0 and deps.get(k, 0) < c:
                    deps[k] = c
        self._emit_waits(e, deps)
        ins = fn()
        self.n_ins[e] += 1
        c = self.cnt[e] + 1
        if inc:
            ins.then_inc(self.sem[e], 1)
            self.cnt[e] = c
        self._record(e, c, reads, writes)
        return ins

    def dma(self, out, in_, reads=(), writes=(), q='sp', **kw):
        slot = self.dma_next
        self.dma_next = (self.dma_next + 1) % self.ndma
        k = ('dma', slot)
        deps = self._deps(reads, writes)
        if self.cnt[k] > 0:
            deps[k] = max(deps.get(k, 0), self.cnt[k])
        if "dma" in self.SERIAL or "all" in self.SERIAL or ("dma" + q) in self.SERIAL or self.serial_compute == 2:
            deps = self._all(deps)
        self._emit_waits(q, deps)
        ins = self.engs[q].dma_start(out=out, in_=in_, **kw)
        self.n_ins[q] += 1
        c = self.cnt[k] + 16
        ins.then_inc(self.sem[k], 16)
        self.cnt[k] = c
        self._record(k, c, reads, writes)
        return ins

    def dma_ind(self, out, in_, idx_ap, reads=(), writes=()):
        import concourse.bass as _b
        slot = self.dma_next
        self.dma_next = (self.dma_next + 1) % self.ndma
        k = ('dma', slot)
        deps = self._deps(reads, writes)
        if self.cnt[k] > 0:
            deps[k] = max(deps.get(k, 0), self.cnt[k])
        if self.serial_compute == 2 or "all" in self.SERIAL:
            deps = self._all(deps)
        self._emit_waits('pool', deps)
        ins = self.engs['pool'].indirect_dma_start(out=out, out_offset=None, in_=in_,
                                                   in_offset=_b.IndirectOffsetOnAxis(ap=idx_ap, axis=0))
        self.n_ins['pool'] += 1
        c = self.cnt[k] + 16
        ins.then_inc(self.sem[k], 16)
        self.cnt[k] = c
        self._record(k, c, reads, writes)
        return ins

    def finish(self):
        eng = self.engs['sp']
        for k, c in self.cnt.items():
            if c > 0 and k != 'sp' and self.waited.get(('sp', k), 0) < c:
                eng.wait_ge(self.sem[k], c)


class Ring:
    uid = 0

    def __init__(self, nc, es, name, n, shape, dt):
        Ring.uid += 1
        self.t = [es.enter_context(nc.sbuf_tensor("rg%d_%s%d" % (Ring.uid, name, i), shape, dt)) for i in range(n)]
        self.name = name
        self.i = 0

    def get(self):
        i = self.i
        self.i = (i + 1) % len(self.t)
        return self.t[i], (self.name, i)


def build_program(n_layers=DEPTH, do_odd=True):
    nc = bass.Bass("TRN2", target_bir_lowering=False)

    def din(name, shape, dt=F32):
        return nc.dram_tensor(name, list(shape), dt, kind="ExternalInput").ap()

    def dout(name, shape, dt=F32):
        return nc.dram_tensor(name, list(shape), dt, kind="ExternalOutput").ap()

    x_prompt = din("x_prompt", [SEQ, D])
    x_sample = din("x_sample", [NS, D])
    c_all = din("c_all", [1 + NS, D])
    state_conv = din("state_conv", [2, NS, 30, 512])
    ada_w = din("ada_w", [DEPTH, D, 6 * D])
    ffn_w1 = din("ffn_w1", [DEPTH, D, 4 * D])
    ffn_w2 = din("ffn_w2", [DEPTH, 4 * D, D])
    even_w_in = din("even_w_in", [2, D, 2048])
    even_w_out = din("even_w_out", [2, D, D])
    p_layer = din("p_layer", [DEPTH, 128, 64])
    p_even = din("p_even", [2, 128, 136])
    p_final = din("p_final", [128, 8])
    conv_w = din("conv_w", [2, 31, 512])
    conv_b = din("conv_b", [2, 512])
    conv_ln_g = din("conv_ln_g", [2, 512])
    conv_ln_b = din("conv_ln_b", [2, 512])
    sgu_ln_g = din("sgu_ln_g", [2, 512])
    sgu_ln_b = din("sgu_ln_b", [2, 512])
    sgu_wT = din("sgu_wT", [2, 128, 4, 128])
    sgu_b = din("sgu_b", [2, 4, 128])
    cst = din("cst", [128, 256])
    odd_w_in = din("odd_w_in", [2, D, 2608])
    odd_w_out = din("odd_w_out", [2, D, D])
    cmp_pe = [din("cmp_pe_k", [2, 32, 64]), din("cmp_pe_v", [2, 32, 64])]
    cmp_w1 = [din("cmp_w1_k", [2, 32, 64, 128]), din("cmp_w1_v", [2, 32, 64, 128])]
    cmp_w2 = [din("cmp_w2_k", [2, 128, 64]), din("cmp_w2_v", [2, 128, 64])]
    rope_cos = din("rope_cos", [SEQ + 1, 64])
    rope_sin = din("rope_sin", [SEQ + 1, 64])
    selA = din("selA", [SEQ, 32])
    selB = din("selB", [SEQ, 32])
    cmpb_d = din("cmpb", [127, SEQ])
    bandb_d = din("bandb", [128, 4, 256])
    Emat_d = din("Emat", [32, 16, 128])
    ov_d = din("ov", [127, 32])
    ov33_d = din("ov33", [127, 33])
    selA_s = din("selA_s", [1, 33])
    selB_s = din("selB_s", [1, 33])
    nb16_d = din("nb16", [16, 16])
    poff_d = din("poff", [128, 2])
    page_table = din("page_table", [NS, 16], I32)
    NPOOL = int(os.environ.get("K_POOLPAGES", 2560))
    caches = [din(n, [2 * NPOOL * 128, 256]) for n in ("cache_cmp_k", "cache_cmp_v", "cache_sel_k", "cache_sel_v")]
    state_win = [din(n, [2, NS, 512, 256]) for n in ("state_win_k", "state_win_v")]

    y_prompt = dout("y_prompt", [SEQ, D])
    y_sample = dout("y_sample", [NS, D])
    conv_p = dout("conv_p", [2, 30, 512])
    conv_s = dout("conv_s", [2, NS, 30, 512])
    chunkv_p = dout("chunkv_p", [2, 128, 512])
    chunkv_s = dout("chunkv_s", [2, NS, 512])
    nsa_p = [dout(n, [2, SEQ, 256]) for n in ("cmp_k_p", "cmp_v_p", "sel_k_p", "sel_v_p")]
    win_p = [dout(n, [2, 512, 256]) for n in ("win_k_p", "win_v_p")]
    nsa_s = [dout(n, [2, NS, 256]) for n in ("cmp_k_s", "cmp_v_s", "sel_k_s", "sel_v_s")]
    win_s = [dout(n, [2, NS, 512, 256]) for n in ("win_k_s", "win_v_s")]

    DBG = bool(os.environ.get("K_DBG"))
    with ExitStack() as es:
        S = Sched(nc, es)

        from contextlib import contextmanager

        @contextmanager
        def phase():
            S.fence()
            with ExitStack() as st_:
                yield st_
            S.fence()

        uid = [0]

        def sb(name, shape, dt, stack=es):
            uid[0] += 1
            return stack.enter_context(nc.sbuf_tensor("sb%d_%s" % (uid[0], name), list(shape), dt))

        xT = sb("xT", [128, 8, SEQ], F32)
        xsT = sb("xsT", [128, 8, NS], F32)
        hT = sb("hT", [128, 8, HC], BF16)
        NSLOT = 4
        wring = [sb("wr%d" % i, [128, 4096], BF16) for i in range(NSLOT)]
        mod = [sb("mod%d" % i, [128, 48, 17], F32) for i in range(2)]
        gsb = [sb("gs%d" % i, [128, 8, 17], F32) for i in range(2)]
        pl = sb("pl", [128, DEPTH, 64], F32)
        pfin = sb("pfin", [128, 8], F32)
        scT = sb("scT", [128, 8, 17], BF16)
        ident32 = sb("ident32", [128, 128], F32)
        identb = sb("identb", [128, 128], BF16)
        trib = sb("trib", [128, 128], BF16)
        onesb = sb("onesb", [128, 128], BF16)
        eps_t = sb("eps_t", [128, 1], F32)
        s32 = Ring(nc, es, "s32_", 6, [128, 512], F32)
        rstd_t = sb("rstd_t", [128, 512], F32)
        s16 = Ring(nc, es, "s16_", 3, [128, 512], BF16)
        ps_t = [es.enter_context(nc.psum_tensor("ps%d" % i, [128, 512], F32)) for i in range(8)]
        ps_i = [0]

        def PS():
            i = ps_i[0]
            ps_i[0] = (i + 1) % 8
            return ps_t[i], ('ps', i)

        V, A, G, T = nc.vector, nc.scalar, nc.gpsimd, nc.tensor

        def mm(out, lhsT, rhs, start, stop, reads, writes, inc=None, **kw):
            return S.op('pe', lambda: T.matmul(out, lhsT=lhsT, rhs=rhs, start=start, stop=stop, **kw),
                        reads=reads, writes=writes, inc=(stop if inc is None else inc))

        def tr(out, in_, ident, reads, writes, inc=True):
            return S.op('pe', lambda: T.transpose(out, in_, ident), reads=reads, writes=writes, inc=inc)

        wplan = []
        wstate = {'issued': 0, 'used': 0}

        def w_parts(entry):
            return entry if isinstance(entry, list) else [(0, 128, entry)]

        def w_view(i, p0=0, p1=128):
            parts = w_parts(wplan[i])
            a, b = parts[0][2].shape[1], parts[0][2].shape[2]
            slot = i % NSLOT
            return wring[slot][p0:p1, 0:a * b].rearrange("p (a b) -> p a b", a=a)

        def w_issue_upto(n):
            while wstate['issued'] < min(n, len(wplan)):
                i = wstate['issued']
                slot = i % NSLOT
                for pi, (p0, p1, src) in enumerate(w_parts(wplan[i])):
                    S.dma(w_view(i, p0, p1), src, writes=[('w', slot, pi), ('w', slot, 'r')], q='pool')
                wstate['issued'] += 1

        def w_next():
            i = wstate['used']
            wstate['used'] += 1
            w_issue_upto(i + NSLOT - 1)
            slot = i % NSLOT
            keys = [('w', slot, 'r')] + [('w', slot, pi) for pi in range(len(w_parts(wplan[i])))]
            return w_view(i), keys

        def wview(W2d):
            return W2d.rearrange("(kc p) n -> p kc n", p=128)

        def plan_ada(l):
            v = wview(ada_w[l])
            return [v[:, :, b * 512:(b + 1) * 512] for b in range(12)]

        def plan_even(e):
            vi = wview(even_w_in[e])
            vo = wview(even_w_out[e])
            out = []
            for tt in range(NT + 1):
                out += [vi[:, :, b * 512:(b + 1) * 512] for b in range(4)]
                out += [vo[:, 0:4, :], vo[:, 4:8, :]]
            return out

        def plan_mlp(l):
            v1 = wview(ffn_w1[l])
            v2 = wview(ffn_w2[l])
            out = []
            nada = 0
            for j in range(8):
                out += [v1[:, :, j * 512:(j + 1) * 512], v2[:, 4 * j:4 * j + 4, :]]
                if l + 1 < n_layers:
                    tgt = (12 * (j + 1)) // 8
                    pa = plan_ada(l + 1)
                    out += pa[nada:tgt]
                    nada = tgt
            return out

        TQ = 256
        NQ = SEQ // TQ
        OST = int(os.environ.get('K_OST', 6))
        SST = int(os.environ.get('K_SST', 5))
        P1 = int(os.environ.get('K_P1', 4))
        SBLK = [(0, 512), (512, 512), (1024, 512), (1536, 512), (2048, 512), (2560, 48)]

        def plan_odd(o):
            vi = wview(odd_w_in[o])
            out = []
            for tt in range(NT if OST >= 1 else 0):
                out += [vi[:, :, 1024 + b * 512:1024 + (b + 1) * 512] for b in range(3 if OST != 1 else int(os.environ.get('K_NBLK', 3)))]
            for qt in range(NQ if OST >= 3 else 0):
                out += [vi[:, :, 0:512], vi[:, :, 512:1024], vi[:, :, 2560:2608]]
                for b in range(2 if OST >= 5 else 0):
                    parts = []
                    for half in range(2):
                        r0 = (8 * b + 4 * half) * 64
                        parts.append((half * 64, half * 64 + 64, odd_w_out[o, r0:r0 + 256, :].rearrange("(m d) n -> d m n", d=64)))
                    out.append(parts)
            if OST >= 6:
                for (c0, n) in SBLK:
                    out.append(vi[:, :, c0:c0 + n])
                for b in range(2 if SST >= 5 else 0):
                    parts = []
                    for half in range(2):
                        r0 = (8 * b + 4 * half) * 64
                        parts.append((half * 64, half * 64 + 64, odd_w_out[o, r0:r0 + 256, :].rearrange("(m d) n -> d m n", d=64)))
                    out.append(parts)
            return out

        wplan += plan_ada(0)
        for l in range(n_layers):
            if l % 2 == 0:
                wplan += plan_even(l // 2)
            elif do_odd:
                wplan += plan_odd(l // 2)
            wplan += plan_mlp(l)

        S.dma(ident32[:], cst[:, 0:128], writes=['ident32'])
        S.dma(identb[:], cst[:, 0:128], writes=['identb'], q='pool')
        S.dma(trib[:], cst[:, 128:256], writes=['trib'], q='pool')
        S.dma(pl[:], p_layer.rearrange("l p c -> p l c"), writes=['pl'])
        S.dma(pfin[:], p_final, writes=['pfin'])
        S.op('dve', lambda: V.memset(onesb[:], 1.0), writes=['onesb'])
        S.op('dve', lambda: V.memset(eps_t[:], EPS), writes=['eps'])
        w_issue_upto(NSLOT - 1)

        with phase() as ph:
            tok = [sb("tok%d" % i, [128, D], F32, ph) for i in range(2)]
            for tt in range(SEQ // 128):
                tk = tok[tt % 2]
                key = ('tok', tt % 2)
                S.dma(tk[:], x_prompt[tt * 128:(tt + 1) * 128, :], writes=[key])
                for half in range(2):
                    ps, pk = PS()
                    for q in range(4):
                        c = half * 4 + q
                        tr(ps[:, q * 128:(q + 1) * 128], tk[:, c * 128:(c + 1) * 128], ident32[:],
                           reads=[key, 'ident32'], writes=[pk], inc=(q == 3))
                    dst = xT[:, half * 4:half * 4 + 4, tt * 128:(tt + 1) * 128]
                    src = ps[:].rearrange("p (a b) -> p a b", a=4)
                    if half == 0:
                        S.op('act', lambda: A.copy(out=dst, in_=src), reads=[pk], writes=[('x', tt // 4, half)])
                    else:
                        S.op('dve', lambda: V.tensor_copy(out=dst, in_=src), reads=[pk], writes=[('x', tt // 4, half)])
            tks = sb("toks", [NS, D], F32, ph)
            S.dma(tks[:], x_sample, writes=['toks'])
            ps, pk = PS()
            for c in range(8):
                tr(ps[:, c * NS:(c + 1) * NS], tks[:, c * 128:(c + 1) * 128], ident32[0:NS, 0:NS],
                   reads=['toks', 'ident32'], writes=[pk], inc=(c == 7))
            S.op('dve', lambda: V.tensor_copy(out=xsT[:], in_=ps[:, 0:8 * NS].rearrange("p (a b) -> p a b", a=8)),
                 reads=[pk], writes=['xs'])
            ctk = sb("ctk", [17, D], F32, ph)
            S.dma(ctk[:], c_all, writes=['ctk'])
            S.op('act', lambda: A.activation(out=ctk[:], in_=ctk[:], func=AF.Silu), reads=['ctk'], writes=['ctk'])
            ps, pk = PS()
            for c in range(8):
                tr(ps[:, c * 17:(c + 1) * 17], ctk[:, c * 128:(c + 1) * 128], ident32[0:17, 0:17],
                   reads=['ctk', 'ident32'], writes=[pk], inc=(c == 7))
            S.op('dve', lambda: V.tensor_copy(out=scT[:], in_=ps[:, 0:8 * 17].rearrange("p (a b) -> p a b", a=8)),
                 reads=[pk], writes=['scT'])
            xkeys = [[('x', t, 0), ('x', t, 1)] for t in range(NT)]

            def ada_block(l, b):
                wt, wk = w_next()
                m = mod[l % 2]
                ps, pk = PS()
                for oc in range(4):
                    for k in range(8):
                        mm(ps[:, oc * 17:(oc + 1) * 17], wt[:, k, oc * 128:(oc + 1) * 128], scT[:, k, :],
                           k == 0, k == 7, reads=wk + ['scT'], writes=[pk])
                bias = pl[:, l, 4 * b:4 * b + 4].unsqueeze(2).to_broadcast([128, 4, 17])
                S.op('dve', lambda: V.tensor_tensor(out=m[:, 4 * b:4 * b + 4, :],
                                                     in0=ps[:, 0:68].rearrange("p (a b) -> p a b", a=4),
                                                     in1=bias, op=ALU.add),
                     reads=[pk, 'pl'], writes=[('mod', l % 2)])

            def ada_finish(l):
                m = mod[l % 2]
                for which in range(2):
                    sc = m[:, (1 + 3 * which) * 8:(2 + 3 * which) * 8, :]
                    g = pl[:, l, 48 + 8 * which:56 + 8 * which].unsqueeze(2).to_broadcast([128, 8, 17])
                    S.op('dve', lambda: V.scalar_tensor_tensor(out=gsb[which][:], in0=sc, scalar=1.0, in1=g,
                                                                op0=ALU.add, op1=ALU.mult),
                         reads=[('mod', l % 2), 'pl'], writes=[('gs', which)])

            for b in range(12):
                ada_block(0, b)

        def norm_tile(l, which, tt, dst, dkeys):
            m = mod[l % 2]
            gs = gsb[which]
            sh = (3 * which) * 8
            cols = slice(tt * TT, (tt + 1) * TT)
            ps, pk = PS()
            for c in range(8):
                sq, sqk = s16.get()
                S.op('act', lambda: A.activation(out=sq[:], in_=xT[:, c, cols], func=AF.Square),
                     reads=xkeys[tt], writes=[sqk])
                mm(ps[:], onesb[:], sq[:], c == 0, c == 7, reads=[sqk, 'onesb'], writes=[pk], inc=True)
            rs, rsk = rstd_t, 'rstd_t'
            S.op('act', lambda: A.activation(out=rs[:], in_=ps[:], func=AF.Sqrt, bias=eps_t[:, 0:1], scale=1.0 / D),
                 reads=[pk, 'eps'], writes=[rsk])
            S.op('dve', lambda: V.reciprocal(out=rs[:], in_=rs[:]), reads=[rsk], writes=[rsk])
            for c in range(8):
                t, tk = s32.get()
                S.op('dve', lambda: V.tensor_tensor(out=t[:], in0=xT[:, c, cols], in1=rs[:], op=ALU.mult),
                     reads=xkeys[tt] + [rsk], writes=[tk])
                S.op('act', lambda: A.activation(out=dst[:, c, :], in_=t[:], func=AF.Identity,
                                                  scale=gs[:, c, 0:1], bias=m[:, sh + c, 0:1]),
                     reads=[tk, ('gs', which), ('mod', l % 2)], writes=[dkeys[c]])

        def norm_samples(l, which):
            m = mod[l % 2]
            gs = gsb[which]
            sh = (3 * which) * 8
            ps, pk = PS()
            sq, sqk = s16.get()
            S.op('act', lambda: A.activation(out=sq[:, 0:8 * NS], in_=xsT[:].rearrange("p a b -> p (a b)"), func=AF.Square),
                 reads=['xs'], writes=[sqk])
            for c in range(8):
                mm(ps[:, 0:NS], onesb[:], sq[:, c * NS:(c + 1) * NS], c == 0, c == 7, reads=[sqk, 'onesb'], writes=[pk])
            rs, rsk = s32.get()
            S.op('act', lambda: A.activation(out=rs[:, 0:NS], in_=ps[:, 0:NS], func=AF.Sqrt, bias=eps_t[:, 0:1], scale=1.0 / D),
                 reads=[pk, 'eps'], writes=[rsk])
            S.op('dve', lambda: V.reciprocal(out=rs[:, 0:NS], in_=rs[:, 0:NS]), reads=[rsk], writes=[rsk])
            t, tk = s32.get()
            tv = t[:, 0:8 * NS].rearrange("p (a b) -> p a b", a=8)
            S.op('dve', lambda: V.tensor_tensor(out=tv, in0=xsT[:], in1=rs[:, 0:NS].unsqueeze(1).to_broadcast([128, 8, NS]), op=ALU.mult),
                 reads=['xs', rsk], writes=[tk])
            S.op('dve', lambda: V.tensor_tensor(out=tv, in0=tv, in1=gs[:, :, 1:17], op=ALU.mult),
                 reads=[tk, ('gs', which)], writes=[tk])
            S.op('dve', lambda: V.tensor_tensor(out=hT[:, :, SEQ:HC], in0=tv, in1=m[:, sh:sh + 8, 1:17], op=ALU.add),
                 reads=[tk, ('mod', l % 2)], writes=[('h', 's')])

        def norm_pass(l, which):
            for tt in range(NT):
                norm_tile(l, which, tt, hT[:, :, tt * TT:(tt + 1) * TT], [('h', tt, c) for c in range(8)])
            norm_samples(l, which)

        hkeys = [[('h', t, c) for c in range(8)] for t in range(NT)]

        def mlp(l):
            m = mod[l % 2]
            g2 = 5 * 8
            nada = 0
            with phase() as ph:
                hid = [sb("hid%d" % i, [128, 4, TT], BF16, ph) for i in range(2)]
                hi = 0
                for j in range(8):
                    w1, w1k = w_next()
                    w2, w2k = w_next()
                    for tt in range(NT + 1):
                        samp = tt == NT
                        n = NS if samp else TT
                        cols = slice(SEQ, HC) if samp else slice(tt * TT, (tt + 1) * TT)
                        hk = [('h', 's')] if samp else hkeys[tt]
                        hd = hid[hi % 2]
                        hdk = ('hid', hi % 2)
                        hi += 1
                        for jj in range(4):
                            ps, pk = PS()
                            for k in range(8):
                                mm(ps[:, 0:n], w1[:, k, jj * 128:(jj + 1) * 128], hT[:, k, cols], k == 0, k == 7,
                                   reads=w1k + hk, writes=[pk])
                            r, rk = s32.get()
                            S.op('act', lambda: A.activation(out=r[:, 0:n], in_=ps[:, 0:n], func=AF.Relu), reads=[pk], writes=[rk])
                            S.op('dve', lambda: V.tensor_tensor(out=hd[:, jj, 0:n], in0=r[:, 0:n], in1=r[:, 0:n], op=ALU.mult),
                                 reads=[rk], writes=[hdk])
                        if samp:
                            ps, pk = PS()
                            for i in range(8):
                                for jj in range(4):
                                    mm(ps[:, i * NS:(i + 1) * NS], w2[:, jj, i * 128:(i + 1) * 128], hd[:, jj, 0:NS], jj == 0, jj == 3,
                                       reads=w2k + [hdk], writes=[pk])
                            t, tk = s32.get()
                            tv = t[:, 0:8 * NS].rearrange("p (a b) -> p a b", a=8)
                            S.op('dve', lambda: V.tensor_tensor(out=tv, in0=ps[:, 0:8 * NS].rearrange("p (a b) -> p a b", a=8),
                                                                 in1=m[:, g2:g2 + 8, 1:17], op=ALU.mult),
                                 reads=[pk, ('mod', l % 2)], writes=[tk])
                            S.op('dve', lambda: V.tensor_tensor(out=xsT[:], in0=xsT[:], in1=tv, op=ALU.add),
                                 reads=[tk, 'xs'], writes=['xs'])
                        else:
                            for i in range(8):
                                ps, pk = PS()
                                for jj in range(4):
                                    mm(ps[:], w2[:, jj, i * 128:(i + 1) * 128], hd[:, jj, :], jj == 0, jj == 3,
                                       reads=w2k + [hdk], writes=[pk])
                                S.op('dve', lambda: V.scalar_tensor_tensor(out=xT[:, i, cols], in0=ps[:], scalar=m[:, g2 + i, 0:1],
                                                                            in1=xT[:, i, cols], op0=ALU.mult, op1=ALU.add),
                                     reads=[pk, ('mod', l % 2)] + xkeys[tt], writes=[xkeys[tt][i // 4]])
                    if l + 1 < n_layers:
                        tgt = (12 * (j + 1)) // 8
                        while nada < tgt:
                            ada_block(l + 1, nada)
                            nada += 1

        def even_layer(l):
            e = l // 2
            m = mod[l % 2]
            g1 = 2 * 8
            with phase() as ph:
                pe = sb("pe", [128, 136], F32, ph)
                S.dma(pe[:], p_even[e], writes=['pe'])
                gbc = sb("gbc", [128, 512], F32, ph)
                bbc = sb("bbc", [128, 512], F32, ph)
                S.dma(gbc[:], sgu_ln_g[e:e + 1, :].to_broadcast([128, 512]), writes=['gbc'])
                S.dma(bbc[:], sgu_ln_b[e:e + 1, :].to_broadcast([128, 512]), writes=['bbc'])
                wmT = sb("wmT", [128, 4, 128], BF16, ph)
                S.dma(wmT[:], sgu_wT[e], writes=['wmT'], q='pool')
                S.op('dve', lambda: V.tensor_tensor(out=wmT[:], in0=wmT[:], in1=trib[:].unsqueeze(1).to_broadcast([128, 4, 128]), op=ALU.mult),
                     reads=['wmT', 'trib'], writes=['wmT'])
                bsb = sb("bsb", [1, 4, 128], BF16, ph)
                S.dma(bsb[:], sgu_b[e:e + 1], writes=['bsb'], q='pool')
                stat = sb("stat", [128, 8], F32, ph)

                with phase() as pa:
                    cg = sb("cg", [NS, 512], F32, pa)
                    cb = sb("cb", [NS, 512], F32, pa)
                    w30 = sb("w30", [NS, 512], F32, pa)
                    cpre = sb("cpre", [NS, 512], F32, pa)
                    sw0 = sb("sw0", [NS, 4], F32, pa)
                    sb0 = sb("sb0", [NS, 4], F32, pa)
                    S.dma(cg[:], conv_ln_g[e:e + 1, :].to_broadcast([NS, 512]), writes=['cg'])
                    S.dma(cb[:], conv_ln_b[e:e + 1, :].to_broadcast([NS, 512]), writes=['cb'])
                    S.dma(w30[:], conv_w[e, 30:31, :].to_broadcast([NS, 512]), writes=['w30'])
                    S.dma(cpre[:], conv_b[e:e + 1, :].to_broadcast([NS, 512]), writes=['cpre'])
                    S.dma(sw0[:], sgu_wT[e, 0:1, :, 0].to_broadcast([NS, 4]), writes=['sw0'], allow_slow_non_contiguous=True)
                    S.dma(sb0[:], sgu_b[e, :, 0].unsqueeze(0).to_broadcast([NS, 4]), writes=['sb0'], allow_slow_non_contiguous=True)
                    zl = sb("zl", [NS, 512], F32, pa)
                    a_s = sb("a_s", [NS, 512], F32, pa)
                    cv = sb("cv", [NS, 512], F32, pa)
                    us = sb("us", [NS, 512], F32, pa)
                    vs = sb("vs", [NS, 512], F32, pa)
                    abs_ = sb("abs", [NS, 1024], F32, pa)
                    abT = sb("abT", [128, 8, NS], BF16, pa)
                    st = sb("st", [NS, 30, 64], F32, pa)
                    wb_ = sb("wbc", [NS, 30, 64], F32, pa)
                    red = sb("red", [NS, 64], F32, pa)
                    for oc in range(8):
                        cs = slice(oc * 64, (oc + 1) * 64)
                        S.dma(st[:], state_conv[e, :, :, cs], writes=['st'])
                        S.dma(wb_[:], conv_w[e:e + 1, 0:30, cs].to_broadcast([NS, 30, 64]), writes=['wbc'])
                        S.op('dve', lambda: V.tensor_tensor(out=st[:], in0=st[:], in1=wb_[:], op=ALU.mult), reads=['st', 'wbc'], writes=['st'])
                        S.op('dve', lambda: V.tensor_reduce(out=red[:], in_=st[:].rearrange("p k c -> p c k"), axis=AX.X, op=ALU.add),
                             reads=['st'], writes=['red'])
                        S.op('dve', lambda: V.tensor_tensor(out=cpre[:, cs], in0=cpre[:, cs], in1=red[:], op=ALU.add),
                             reads=['red', 'cpre'], writes=['cpre'])
                    S.dma(conv_s[e, :, 0:29, :], state_conv[e, :, 1:30, :])

                    def zproj():
                        Wb, Wbk = w_next()
                        ps, pk = PS()
                        for k in range(8):
                            mm(ps[0:NS, :], hT[:, k, SEQ:HC], Wb[:, k, :], k == 0, k == 7, reads=Wbk + [('h', 's')], writes=[pk])
                        return ps, pk
                    ps, pk = zproj()
                    S.op('act', lambda: A.copy(out=zl[:], in_=ps[0:NS, :]), reads=[pk], writes=['zl'])
                    ps, pk = zproj()
                    S.op('act', lambda: A.activation(out=a_s[:], in_=ps[0:NS, :], func=AF.Sigmoid), reads=[pk], writes=['a_s'])
                    S.op('dve', lambda: V.tensor_tensor(out=a_s[:], in0=zl[:], in1=a_s[:], op=ALU.mult), reads=['zl', 'a_s'], writes=['a_s'])
                    S.dma(conv_s[e, :, 29, :], a_s[:], reads=['a_s'])
                    S.op('dve', lambda: V.tensor_tensor(out=cv[:], in0=a_s[:], in1=w30[:], op=ALU.mult), reads=['a_s', 'w30'], writes=['cv'])
                    S.op('dve', lambda: V.tensor_tensor(out=cv[:], in0=cv[:], in1=cpre[:], op=ALU.add), reads=['cv', 'cpre'], writes=['cv'])

                    def ln_tok(t, tk, gsrc, bsrc, gkey, bkey):
                        S.op('dve', lambda: V.bn_stats(out=stat[0:NS, 0:6], in_=t[:]), reads=[tk], writes=['stat'])
                        S.op('dve', lambda: V.bn_aggr(out=stat[0:NS, 6:8], in_=stat[0:NS, 0:6]), reads=['stat'], writes=['stat'])
                        S.op('act', lambda: A.activation(out=stat[0:NS, 7:8], in_=stat[0:NS, 7:8], func=AF.Sqrt, bias=eps_t[0:NS, 0:1], scale=1.0),
                             reads=['stat', 'eps'], writes=['stat'])
                        S.op('dve', lambda: V.reciprocal(out=stat[0:NS, 7:8], in_=stat[0:NS, 7:8]), reads=['stat'], writes=['stat'])
                        S.op('dve', lambda: V.tensor_scalar(out=t[:], in0=t[:], scalar1=stat[0:NS, 6:7], scalar2=stat[0:NS, 7:8],
                                                             op0=ALU.subtract, op1=ALU.mult), reads=[tk, 'stat'], writes=[tk])
                        S.op('dve', lambda: V.tensor_tensor(out=t[:], in0=t[:], in1=gsrc, op=ALU.mult), reads=[tk, gkey], writes=[tk])
                        S.op('dve', lambda: V.tensor_tensor(out=t[:], in0=t[:], in1=bsrc, op=ALU.add), reads=[tk, bkey], writes=[tk])

                    ln_tok(cv, 'cv', cg[:], cb[:], 'cg', 'cb')
                    S.op('act', lambda: A.activation(out=abs_[:, 0:512], in_=cv[:], func=AF.Silu), reads=['cv'], writes=['abs'])
                    ps, pk = zproj()
                    S.op('act', lambda: A.activation(out=us[:], in_=ps[0:NS, :], func=AF.Gelu_apprx_tanh), reads=[pk], writes=['us'])
                    ps, pk = zproj()
                    S.op('act', lambda: A.activation(out=vs[:], in_=ps[0:NS, :], func=AF.Gelu_apprx_tanh), reads=[pk], writes=['vs'])
                    ln_tok(vs, 'vs', gbc[0:NS, :], bbc[0:NS, :], 'gbc', 'bbc')
                    S.dma(chunkv_s[e], vs[:], reads=['vs'])
                    vs3 = vs[:].rearrange("p (g d) -> p g d", g=4)
                    S.op('dve', lambda: V.tensor_tensor(out=vs3, in0=vs3, in1=sw0[:].unsqueeze(2).to_broadcast([NS, 4, 128]), op=ALU.mult),
                         reads=['vs', 'sw0'], writes=['vs'])
                    S.op('dve', lambda: V.tensor_tensor(out=vs3, in0=vs3, in1=sb0[:].unsqueeze(2).to_broadcast([NS, 4, 128]), op=ALU.add),
                         reads=['vs', 'sb0'], writes=['vs'])
                    S.op('dve', lambda: V.tensor_tensor(out=abs_[:, 512:1024], in0=us[:], in1=vs[:], op=ALU.mult),
                         reads=['us', 'vs', 'abs'], writes=['abs'])
                    ps, pk = PS()
                    for c in range(8):
                        tr(ps[:, c * NS:(c + 1) * NS], abs_[:, c * 128:(c + 1) * 128], ident32[0:NS, 0:NS],
                           reads=['abs', 'ident32'], writes=[pk], inc=(c == 7))
                    S.op('dve', lambda: V.tensor_copy(out=abT[:], in_=ps[:, 0:8 * NS].rearrange("p (a b) -> p a b", a=8)), reads=[pk], writes=['abT'])
                    Wo1, Wo1k = w_next()
                    Wo2, Wo2k = w_next()
                    ps, pk = PS()
                    for i in range(8):
                        for k in range(8):
                            Wo, Wok = (Wo1, Wo1k) if k < 4 else (Wo2, Wo2k)
                            mm(ps[:, i * NS:(i + 1) * NS], Wo[:, k % 4, i * 128:(i + 1) * 128], abT[:, k, :], k == 0, k == 7,
                               reads=Wok + ['abT'], writes=[pk])
                    t, tk = s32.get()
                    tv = t[:, 0:8 * NS].rearrange("p (a b) -> p a b", a=8)
                    S.op('dve', lambda: V.tensor_tensor(out=tv, in0=ps[:, 0:8 * NS].rearrange("p (a b) -> p a b", a=8),
                                                         in1=m[:, g1:g1 + 8, 1:17], op=ALU.mult), reads=[pk, ('mod', l % 2)], writes=[tk])
                    S.op('dve', lambda: V.tensor_tensor(out=xsT[:], in0=xsT[:], in1=tv, op=ALU.add), reads=[tk, 'xs'], writes=['xs'])

                diag = [sb("diag%d" % i, [128, 31, 128], BF16, ph) for i in range(2)]
                apad1 = sb("apad", [128, 4, 30 + TT], BF16, ph)
                apad = [apad1, apad1]
                mean = sb("mean_t", [128, 512], F32, ph)
                msq = sb("msq_t", [128, 512], F32, ph)
                mk, msk = 'mean_t', 'msq_t'
                u_t = sb("u_t", [128, 4, TT], BF16, ph)
                bo_t = sb("bo_t", [128, 4, TT], BF16, ph)
                ao_t = sb("ao_t", [128, 4, TT], BF16, ph)
                acv = sb("acv", [128, 4, TT], BF16, ph)
                vb = sb("vb", [128, 512], BF16, ph)
                a32 = sb("a32", [128, 4, 32], F32, ph)
                S.op('dve', lambda: V.memset(apad1[:, :, 0:30], 0.0), writes=['apad'])
                ndiag = [0]

                for tt in range(NT):
                    t0 = tt * TT
                    cols = slice(t0, t0 + TT)
                    hk = hkeys[tt]
                    ap_t = apad[tt % 2]
                    apk = 'apad'
                    Wa, Wak = w_next()
                    Wg, Wgk = w_next()
                    for oc in range(4):
                        psl, plk = PS()
                        for k in range(8):
                            mm(psl[:], Wa[:, k, oc * 128:(oc + 1) * 128], hT[:, k, cols], k == 0, k == 7, reads=Wak + hk, writes=[plk])
                        psg, pgk = PS()
                        for k in range(8):
                            mm(psg[:], Wg[:, k, oc * 128:(oc + 1) * 128], hT[:, k, cols], k == 0, k == 7, reads=Wgk + hk, writes=[pgk])
                        sg, sgk = s32.get()
                        S.op('act', lambda: A.activation(out=sg[:], in_=psg[:], func=AF.Sigmoid), reads=[pgk], writes=[sgk])
                        S.op('dve', lambda: V.tensor_tensor(out=ap_t[:, oc, 30:30 + TT], in0=psl[:], in1=sg[:], op=ALU.mult),
                             reads=[plk, sgk], writes=[apk])
                        if tt == NT - 1:
                            S.op('dve', lambda: V.tensor_tensor(out=a32[:, oc, 0:32], in0=psl[:, TT - 32:TT], in1=sg[:, TT - 32:TT], op=ALU.mult),
                                 reads=[plk, sgk], writes=['a32'])
                    Wu, Wuk = w_next()
                    for oc in range(4):
                        ps, pk = PS()
                        for k in range(8):
                            mm(ps[:], Wu[:, k, oc * 128:(oc + 1) * 128], hT[:, k, cols], k == 0, k == 7, reads=Wuk + hk, writes=[pk])
                        S.op('act', lambda: A.activation(out=u_t[:, oc, :], in_=ps[:], func=AF.Gelu_apprx_tanh), reads=[pk], writes=['u_t'])
                    Wv, Wvk = w_next()
                    for sub in range(4):
                        c0 = t0 + sub * 128
                        ps, pk = PS()
                        for k in range(8):
                            mm(ps[:], hT[:, k, c0:c0 + 128], Wv[:, k, :], k == 0, k == 7, reads=Wvk + hk, writes=[pk])
                        gv, gvk = s32.get()
                        S.op('act', lambda: A.activation(out=gv[:], in_=ps[:], func=AF.Gelu_apprx_tanh), reads=[pk], writes=[gvk])
                        S.op('dve', lambda: V.bn_stats(out=stat[:, 0:6], in_=gv[:]), reads=[gvk], writes=['stat'])
                        S.op('dve', lambda: V.bn_aggr(out=stat[:, 6:8], in_=stat[:, 0:6]), reads=['stat'], writes=['stat'])
                        S.op('act', lambda: A.activation(out=stat[:, 7:8], in_=stat[:, 7:8], func=AF.Sqrt, bias=eps_t[:, 0:1], scale=1.0),
                             reads=['stat', 'eps'], writes=['stat'])
                        S.op('dve', lambda: V.reciprocal(out=stat[:, 7:8], in_=stat[:, 7:8]), reads=['stat'], writes=['stat'])
                        S.op('dve', lambda: V.tensor_scalar(out=gv[:], in0=gv[:], scalar1=stat[:, 6:7], scalar2=stat[:, 7:8],
                                                             op0=ALU.subtract, op1=ALU.mult), reads=[gvk, 'stat'], writes=[gvk])
                        S.op('dve', lambda: V.tensor_tensor(out=gv[:], in0=gv[:], in1=gbc[:], op=ALU.mult), reads=[gvk, 'gbc'], writes=[gvk])
                        last = (tt == NT - 1 and sub == 3)
                        if last:
                            S.op('dve', lambda: V.tensor_tensor(out=gv[:], in0=gv[:], in1=bbc[:], op=ALU.add), reads=[gvk, 'bbc'], writes=[gvk])
                            S.dma(chunkv_p[e], gv[:], reads=[gvk])
                            S.op('act', lambda: A.copy(out=vb[:], in_=gv[:]), reads=[gvk], writes=['vb'])
                        else:
                            S.op('dve', lambda: V.tensor_tensor(out=vb[:], in0=gv[:], in1=bbc[:], op=ALU.add), reads=[gvk, 'bbc'], writes=['vb'])
                        ps, pk = PS()
                        for g in range(4):
                            mm(ps[:, g * 128:(g + 1) * 128], vb[:, g * 128:(g + 1) * 128], wmT[:, g, :], True, False,
                               reads=['vb', 'wmT'], writes=[pk])
                            mm(ps[:, g * 128:(g + 1) * 128], onesb[0:1, :], bsb[0:1, g, :], False, True,
                               reads=['onesb', 'bsb'], writes=[pk])
                        S.op('dve', lambda: V.tensor_tensor(out=bo_t[:, :, sub * 128:(sub + 1) * 128],
                                                             in0=u_t[:, :, sub * 128:(sub + 1) * 128],
                                                             in1=ps[:].rearrange("p (a b) -> p a b", a=4), op=ALU.mult),
                             reads=['u_t', pk], writes=['bo_t'])
                    pss, pssk = PS()
                    psq, psqk = PS()
                    for oc in range(4):
                        dg = diag[ndiag[0] % 2]
                        dgk = ('diag', ndiag[0] % 2)
                        ndiag[0] += 1
                        S.op('dve', lambda: V.tensor_tensor(out=dg[:], in0=identb[:].unsqueeze(1).to_broadcast([128, 31, 128]),
                                                             in1=pe[:, 12 + oc:136:4].unsqueeze(2).to_broadcast([128, 31, 128]), op=ALU.mult),
                             reads=['identb', 'pe'], writes=[dgk])
                        ps, pk = PS()
                        for k in range(31):
                            mm(ps[:], dg[:, k, :], ap_t[:, oc, k:k + TT], k == 0, k == 30, reads=[dgk, apk], writes=[pk])
                        S.op('act', lambda: A.activation(out=acv[:, oc, :], in_=ps[:], func=AF.Identity, bias=pe[:, oc:oc + 1], scale=1.0),
                             reads=[pk, 'pe'], writes=[('acv', oc)])
                        sq, sqk = s16.get()
                        S.op('act', lambda: A.activation(out=sq[:], in_=ps[:], func=AF.Square, bias=pe[:, oc:oc + 1], scale=1.0),
                             reads=[pk, 'pe'], writes=[sqk])
                        mm(pss[:], onesb[:], acv[:, oc, :], oc == 0, oc == 3, reads=[('acv', oc), 'onesb'], writes=[pssk], inc=True)
                        mm(psq[:], onesb[:], sq[:], oc == 0, oc == 3, reads=[sqk, 'onesb'], writes=[psqk], inc=True)
                    S.op('act', lambda: A.activation(out=mean[:], in_=pss[:], func=AF.Identity, scale=1.0 / 512), reads=[pssk], writes=[mk])
                    if tt + 1 < NT:
                        hal, halk = s16.get()
                        S.op('dve', lambda: V.tensor_copy(out=hal[:, 0:120].rearrange("p (a b) -> p a b", a=4), in_=ap_t[:, :, TT:TT + 30]),
                             reads=[apk], writes=[halk])
                        S.op('dve', lambda: V.tensor_copy(out=ap_t[:, :, 0:30], in_=hal[:, 0:120].rearrange("p (a b) -> p a b", a=4)),
                             reads=[halk], writes=[apk])
                    S.op('dve', lambda: V.tensor_tensor(out=msq[:], in0=mean[:], in1=mean[:], op=ALU.mult), reads=[mk], writes=[msk])
                    S.op('dve', lambda: V.scalar_tensor_tensor(out=msq[:], in0=psq[:], scalar=1.0 / 512, in1=msq[:], op0=ALU.mult, op1=ALU.subtract),
                         reads=[psqk, msk], writes=[msk])
                    S.op('act', lambda: A.activation(out=msq[:], in_=msq[:], func=AF.Sqrt, bias=eps_t[:, 0:1], scale=1.0), reads=[msk, 'eps'], writes=[msk])
                    S.op('dve', lambda: V.reciprocal(out=msq[:], in_=msq[:]), reads=[msk], writes=[msk])
                    for oc in range(4):
                        xc, xck = s32.get()
                        S.op('dve', lambda: V.tensor_tensor(out=xc[:], in0=acv[:, oc, :], in1=mean[:], op=ALU.subtract), reads=[('acv', oc), mk], writes=[xck])
                        S.op('dve', lambda: V.tensor_tensor(out=xc[:], in0=xc[:], in1=msq[:], op=ALU.mult), reads=[xck, msk], writes=[xck])
                        S.op('act', lambda: A.activation(out=ao_t[:, oc, :], in_=xc[:], func=AF.Silu, scale=pe[:, 4 + oc:5 + oc], bias=pe[:, 8 + oc:9 + oc]),
                             reads=[xck, 'pe'], writes=['ao_t'])
                    if tt == NT - 1:
                        ps, pk = PS()
                        for oc in range(4):
                            tr(ps[0:32, oc * 128:(oc + 1) * 128], a32[:, oc, :], ident32[:], reads=['a32', 'ident32'], writes=[pk], inc=(oc == 3))
                        o32, o32k = s32.get()
                        S.op('act', lambda: A.copy(out=o32[0:32, :], in_=ps[0:32, :]), reads=[pk], writes=[o32k])
                        S.dma(conv_p[e], o32[2:32, :], reads=[o32k])
                    Wo1, Wo1k = w_next()
                    Wo2, Wo2k = w_next()
                    for i in range(8):
                        ps, pk = PS()
                        for k in range(8):
                            if k < 4:
                                mm(ps[:], Wo1[:, k, i * 128:(i + 1) * 128], ao_t[:, k, :], k == 0, False, reads=Wo1k + ['ao_t'], writes=[pk])
                            else:
                                mm(ps[:], Wo2[:, k - 4, i * 128:(i + 1) * 128], bo_t[:, k - 4, :], False, k == 7, reads=Wo2k + ['bo_t'], writes=[pk])
                        S.op('dve', lambda: V.scalar_tensor_tensor(out=xT[:, i, cols], in0=ps[:], scalar=m[:, g1 + i, 0:1],
                                                                    in1=xT[:, i, cols], op0=ALU.mult, op1=ALU.add),
                             reads=[pk, ('mod', l % 2)] + xkeys[tt], writes=[xkeys[tt][i // 4]])


        def odd_layer(l):
            o = l // 2
            m = mod[l % 2]
            g1 = 2 * 8
            SC = 0.125
            hflat = hT[:].rearrange("p a b -> p (a b)")
            skT = hflat[:, 0:4096].rearrange("p (c t) -> p c t", c=2)
            wkT = hflat[:, 4096:8192].rearrange("p (c t) -> p c t", c=2)
            svb = hflat[:, 8192:12288].rearrange("p (t f) -> p t f", t=16)
            wvb = hflat[:, 12288:16384].rearrange("p (t f) -> p t f", t=16)
            allh = [k for t in range(NT) for k in hkeys[t]] + [('h', 's')]
            arena = [(n, t) for n in ('skT', 'wkT', 'svb', 'wvb') for t in range(16)]
            with phase() as ph:
                hs_t = sb("hs_t", [128, 8, NS], BF16, ph)
                norm_samples(l, 0)
                S.op('dve', lambda: V.tensor_copy(out=hs_t[:], in_=hT[:, :, SEQ:HC]), reads=[('h', 's')], writes=['hs_t'])
                S.op('dve', lambda: V.memset(hflat[:, 16384:16392], 0.0), reads=['hs_t'], writes=allh + arena)
                kcT = sb("kcT", [128, 2, 128], BF16, ph)
                vc = sb("vc", [128, 4, 64], BF16, ph)
                htile = sb("htile", [128, 8, TT], BF16, ph)
                htk = [('ht', c) for c in range(8)]
                cs2 = sb("cs2", [128, 4, 64], F32, ph)
                sn2 = sb("sn2", [128, 4, 64], F32, ph)
                with phase() as pa:
                    ckcvT = sb("ckcvT", [128, 4, SEQ], BF16, pa)
                    for tt in range(NT if OST >= 1 else 0):
                        t0 = tt * TT
                        norm_tile(l, 0, tt, htile, htk)
                        S.dma(cs2[:], rope_cos[t0:t0 + TT, :].rearrange("(s p) d -> p s d", p=128), writes=['cs2'])
                        S.dma(sn2[:], rope_sin[t0:t0 + TT, :].rearrange("(s p) d -> p s d", p=128), writes=['sn2'])
                        for blk in range(3 if OST != 1 else int(os.environ.get('K_NBLK', 3))):
                            W, Wk = w_next()
                            for sub in range(4):
                                T128 = tt * 4 + sub
                                r0 = T128 * 128
                                ps, pk = PS()
                                for k in range(8):
                                    mm(ps[:], htile[:, k, sub * 128:(sub + 1) * 128], W[:, k, :], k == 0, k == 7, reads=Wk + htk, writes=[pk])
                                zt, ztk = s32.get()
                                if blk == 0:
                                    S.op('act', lambda: A.copy(out=zt[:], in_=ps[:]), reads=[pk], writes=[ztk])
                                    S.dma(nsa_p[0][o, r0:r0 + 128, :], zt[:, 0:256], reads=[ztk])
                                    S.dma(nsa_p[1][o, r0:r0 + 128, :], zt[:, 256:512], reads=[ztk])
                                    ps2, pk2 = PS()
                                    for q in range(4):
                                        tr(ps2[:, q * 128:(q + 1) * 128], zt[:, q * 128:(q + 1) * 128], ident32[:], reads=[ztk, 'ident32'], writes=[pk2], inc=(q == 3))
                                    S.op('dve', lambda: V.tensor_copy(out=ckcvT[:, :, r0:r0 + 128], in_=ps2[:].rearrange("p (a b) -> p a b", a=4)),
                                         reads=[pk2], writes=[('ckcvT', T128)])
                                else:
                                    t12, t12k = s32.get()
                                    p3 = ps[:, 0:256].rearrange("p (h d) -> p h d", h=4)
                                    t1 = t12[:, 0:256].rearrange("p (h d) -> p h d", h=4)
                                    t2 = t12[:, 256:512].rearrange("p (h d) -> p h d", h=4)
                                    cb_ = cs2[:, sub, :].unsqueeze(1).to_broadcast([128, 4, 64])
                                    S.op('dve', lambda: V.tensor_tensor(out=t1, in0=p3, in1=cb_, op=ALU.mult), reads=[pk, 'cs2'], writes=[t12k])
                                    S.op('dve', lambda: V.tensor_tensor(out=t2[:, :, 0:32], in0=p3[:, :, 32:64],
                                                                         in1=sn2[:, sub, 0:32].unsqueeze(1).to_broadcast([128, 4, 32]), op=ALU.mult),
                                         reads=[pk, 'sn2'], writes=[t12k])
                                    S.op('dve', lambda: V.tensor_tensor(out=t2[:, :, 32:64], in0=p3[:, :, 0:32],
                                                                         in1=sn2[:, sub, 32:64].unsqueeze(1).to_broadcast([128, 4, 32]), op=ALU.mult),
                                         reads=[pk, 'sn2'], writes=[t12k])
                                    S.op('dve', lambda: V.tensor_tensor(out=zt[:, 0:256], in0=t12[:, 0:256], in1=t12[:, 256:512], op=ALU.add),
                                         reads=[t12k], writes=[ztk])
                                    S.op('act', lambda: A.copy(out=zt[:, 256:512], in_=ps[:, 256:512]), reads=[pk], writes=[ztk])
                                    if blk == 1:
                                        S.dma(nsa_p[2][o, r0:r0 + 128, :], zt[:, 0:256], reads=[ztk])
                                        S.dma(nsa_p[3][o, r0:r0 + 128, :], zt[:, 256:512], reads=[ztk])
                                    elif T128 >= 12:
                                        S.dma(win_p[0][o, r0 - 1536:r0 - 1408, :], zt[:, 0:256], reads=[ztk])
                                        S.dma(win_p[1][o, r0 - 1536:r0 - 1408, :], zt[:, 256:512], reads=[ztk])
                                    VB, vbn = (svb, 'svb') if blk == 1 else (wvb, 'wvb')
                                    KT, ktn = (skT, 'skT') if blk == 1 else (wkT, 'wkT')
                                    S.op('dve', lambda: V.tensor_copy(out=VB[:, T128, :], in_=zt[:, 256:512]), reads=[ztk], writes=[(vbn, T128)])
                                    ps2, pk2 = PS()
                                    for q in range(2):
                                        tr(ps2[:, q * 128:(q + 1) * 128], zt[:, q * 128:(q + 1) * 128], ident32[:], reads=[ztk, 'ident32'], writes=[pk2], inc=(q == 1))
                                    S.op('act', lambda: A.copy(out=KT[:, :, r0:r0 + 128], in_=ps2[:, 0:256].rearrange("p (a b) -> p a b", a=2)),
                                         reads=[pk2], writes=[(ktn, T128)])
                    with phase() as pc:
                      if OST >= 2:
                            w1t = sb("w1t", [128, 32, 128], BF16, pc)
                            w2t = sb("w2t", [128, 64], BF16, pc)
                            peT = sb("peT", [128, 32], BF16, pc)
                            hb = sb("hb", [128, 1], F32, pc)
                            ckk = [('ckcvT', t) for t in range(16)]
                            for X in range(2):
                                for half in range(2):
                                    S.dma(w1t[half * 64:(half + 1) * 64, :, :], cmp_w1[X][o].rearrange("l d e -> d l e"), writes=[('w1t', half)], q='pool')
                                    S.dma(peT[half * 64:(half + 1) * 64, :], cmp_pe[X][o].rearrange("l d -> d l"), writes=[('peT', half)], q='pool',
                                          allow_slow_non_contiguous=True)
                                S.dma(w2t[:], cmp_w2[X][o], writes=['w2t'], q='pool')
                                ps, pk = PS()
                                for li in range(32):
                                    mm(ps[:, 0:1], w1t[0:64, li, :], peT[0:64, li:li + 1], li == 0, li == 31, reads=[('w1t', 0), ('peT', 0)], writes=[pk])
                                S.op('dve', lambda: V.tensor_copy(out=hb[:], in_=ps[:, 0:1]), reads=[pk], writes=['hb'])
                                for kv in range(4):
                                    half = kv % 2
                                    hs = slice(half * 64, half * 64 + 64)
                                    X3 = ckcvT[hs, 2 * X + kv // 2, :].rearrange("p (c s) -> p c s", s=16)
                                    ps, pk = PS()
                                    for li in range(32):
                                        mm(ps[:, 0:127], w1t[hs, li, :], X3[:, li // 16:li // 16 + 127, li % 16], li == 0, li == 31,
                                           reads=[('w1t', half)] + ckk, writes=[pk])
                                    hg, hgk = s16.get()
                                    S.op('act', lambda: A.activation(out=hg[:, 0:127], in_=ps[:, 0:127], func=AF.Gelu_apprx_tanh, bias=hb[:, 0:1], scale=1.0),
                                         reads=[pk, 'hb'], writes=[hgk])
                                    ps2, pk2 = PS()
                                    if X == 0:
                                        mm(ps2[hs, 0:127], w2t[:, 0:64], hg[:, 0:127], True, True, reads=['w2t', hgk], writes=[pk2])
                                        S.op('dve', lambda: V.tensor_copy(out=kcT[hs, kv // 2, 0:127], in_=ps2[hs, 0:127]), reads=[pk2], writes=['kcT'])
                                    else:
                                        mm(ps2[0:127, 0:64], hg[:, 0:127], w2t[:, 0:64], True, True, reads=['w2t', hgk], writes=[pk2])
                                        S.op('dve', lambda: V.tensor_copy(out=vc[0:127, kv, :], in_=ps2[0:127, 0:64]), reads=[pk2], writes=['vc'])
                with phase() as pb_:
                    qT = sb("qT", [128, 8, TQ], BF16, pb_)
                    qrT = sb("qrT", [128, 8, TQ], BF16, pb_)
                    oT = htile[:, :, 0:TQ]
                    qtk = sb("qtk", [128, 1024], F32, pb_)
                    qrk = sb("qrk", [128, 1024], F32, pb_)
                    gT = sb("gT", [48, TQ], BF16, pb_)
                    gtk = sb("gtk", [128, 48], F32, pb_)
                    bandb = sb("bandb", [128, 4, TQ], BF16, pb_)
                    Emat = sb("Emat", [32, 16, 128], BF16, pb_)
                    ov32 = sb("ov32", [128, 32], F32, pb_)
                    cmpb = sb("cmpb", [128, TQ], BF16, pb_)
                    selbT = sb("selbT", [32, 4, TQ], BF16, pb_)
                    tA = sb("tA", [128, 2, 32], F32, pb_)
                    tB = sb("tB", [128, 2, 32], F32, pb_)
                    pgA = [sb("pgA%d" % i, [128, TQ], F32, pb_) for i in range(2)]
                    scr = sb("scr", [128, 32], F32, pb_)
                    selt = sb("selt", [128, 32], F32, pb_)
                    selb = sb("selb", [128, 32], F32, pb_)
                    m8 = sb("m8", [128, 8], F32, pb_)
                    acc = sb("acc", [128, TQ], F32, pb_)
                    pring = Ring(nc, pb_, "pr_", 6, [128, TQ], BF16)
                    S.dma(bandb[:], bandb_d, writes=['bandb'], q='pool')
                    S.dma(Emat[:], Emat_d, writes=['Emat'], q='pool')
                    S.dma(ov32[0:127, :], ov_d, writes=['ov32'])
                    band_idx = {-4: 0, -3: 1, 0: 2, 1: 3}

                    sets = {'acc': [0, 1, 2, 3], 'st': [4, 5, 6], 'misc': [7]}
                    seti = {k: 0 for k in sets}

                    def PSS(name):
                        lst = sets[name]
                        i = lst[seti[name] % len(lst)]
                        seti[name] += 1
                        return ps_t[i], ('ps', i), None

                    def softmax_block(head_rows, qsrc, m_, kv, KT, VB, vrows, kts, kind, qt, po, pok, psm, psmk, M_sum):
                        hs = head_rows
                        n = len(kts)
                        LOOK = 2
                        pend = []

                        def emit_pv(i, kt, kr, P, Pk):
                            if kind == 'cmp':
                                lv = vc[0:127, kv, :]
                                vrd = ['vc']
                            else:
                                lv = VB[:, kt, kv * 64:(kv + 1) * 64]
                                vrd = [('svb' if kind == 'sel' else 'wvb', kt)]
                            mm(po[hs, 0:TQ], lv, P[0:kr, :], i == 0, i == n - 1, reads=vrd + [Pk], writes=[pok])
                            if M_sum == 128:
                                mm(psm[:, 0:TQ], onesb[0:kr, :], P[0:kr, :], i == 0, i == n - 1, reads=['onesb', Pk], writes=[psmk])
                            else:
                                mm(psm[hs, 0:TQ], onesb[0:kr, 0:64], P[0:kr, :], i == 0, i == n - 1, reads=['onesb', Pk], writes=[psmk])

                        for i, kt in enumerate(kts):
                            ps_s, psk, _ = PSS('st')
                            if kind == 'cmp':
                                kr = 127
                                mm(ps_s[0:kr, 0:TQ], kcT[hs, m_ // 4, 0:127], qsrc[hs, m_, :], True, False, reads=['kcT', 'qT'], writes=[psk])
                                mm(ps_s[0:kr, 0:TQ], identb[0:127, 0:127], cmpb[0:127, :], False, True, reads=['identb', 'cmpb'], writes=[psk])
                            else:
                                kr = 128
                                d = kt - 2 * qt
                                extra = []
                                if kind == 'sel':
                                    extra.append((Emat[0:32, kt, :], selbT[0:32, kv, :], ['Emat', ('selbT', kv)]))
                                if d in band_idx:
                                    extra.append((identb[:], bandb[:, band_idx[d], :], ['identb', 'bandb']))
                                ktn = 'skT' if kind == 'sel' else 'wkT'
                                mm(ps_s[:, 0:TQ], KT[hs, m_ // 4, kt * 128:(kt + 1) * 128], qsrc[hs, m_, :], True, len(extra) == 0,
                                   reads=[(ktn, kt), 'qrT'], writes=[psk])
                                for ei, (l_, r_, rd) in enumerate(extra):
                                    mm(ps_s[:, 0:TQ], l_, r_, False, ei == len(extra) - 1, reads=rd, writes=[psk])
                            P, Pk = pring.get()
                            S.op('act', lambda: A.activation(out=P[0:kr, :], in_=ps_s[0:kr, 0:TQ], func=AF.Exp, scale=SC), reads=[psk], writes=[Pk])
                            pend.append((i, kt, kr, P, Pk))
                            if len(pend) > LOOK:
                                emit_pv(*pend.pop(0))
                        while pend:
                            emit_pv(*pend.pop(0))

                    def gate_bc(m_, br):
                        pg_, pgk, _ = PSS('misc')
                        for half in range(2):
                            h_ = (8 * (m_ // 4) + 4 * half + (m_ % 4))
                            row = h_ * 3 + br
                            mm(pg_[half * 64:(half + 1) * 64, 0:TQ], identb[0:48, row:row + 1].to_broadcast([48, 64]), gT[0:48, :], True, True,
                               reads=['identb', 'gT'], writes=[pgk], inc=(half == 1))
                        return pg_, pgk

                    for qt in range(NQ if OST >= 3 else 0):
                        t0 = qt * TQ
                        tt = qt // 2
                        if qt % 2 == 0:
                            norm_tile(l, 0, tt, htile, htk)
                        hoff = (qt % 2) * TQ
                        S.dma(cs2[:, 0:2, :], rope_cos[t0:t0 + TQ, :].rearrange("(s p) d -> p s d", p=128), writes=['cs2'])
                        S.dma(sn2[:, 0:2, :], rope_sin[t0:t0 + TQ, :].rearrange("(s p) d -> p s d", p=128), writes=['sn2'])
                        S.dma(tA[:], selA[t0:t0 + TQ, :].rearrange("(s p) d -> p s d", p=128), writes=['tA'])
                        S.dma(tB[:], selB[t0:t0 + TQ, :].rearrange("(s p) d -> p s d", p=128), writes=['tB'])
                        S.dma(cmpb[0:127, :], cmpb_d[:, t0:t0 + TQ], writes=['cmpb'], q='pool')
                        W0, W0k = w_next()
                        W1, W1k = w_next()
                        for sub in range(2):
                            hc = slice(hoff + sub * 128, hoff + (sub + 1) * 128)
                            for blk in range(2):
                                W, Wk = (W0, W0k) if blk == 0 else (W1, W1k)
                                ps, pk, _ = PSS('st')
                                for k in range(8):
                                    mm(ps[:], htile[:, k, hc], W[:, k, :], k == 0, k == 7, reads=Wk + htk, writes=[pk])
                                pin = ps[:].rearrange("p (h m d) -> p h m d", h=2, m=4)
                                qo = qtk[:, blk * 512:(blk + 1) * 512].rearrange("p (m h d) -> p h m d", m=4, h=2)
                                S.op('act', lambda: A.copy(out=qo, in_=pin), reads=[pk], writes=['qtk'])
                                t12, t12k = s32.get()
                                t3, t3k = s32.get()
                                p3 = ps[:].rearrange("p (h d) -> p h d", h=8)
                                t1 = t12[:].rearrange("p (h d) -> p h d", h=8)
                                t2 = t3[:].rearrange("p (h d) -> p h d", h=8)
                                S.op('dve', lambda: V.tensor_tensor(out=t1, in0=p3, in1=cs2[:, sub, :].unsqueeze(1).to_broadcast([128, 8, 64]), op=ALU.mult),
                                     reads=[pk, 'cs2'], writes=[t12k])
                                S.op('dve', lambda: V.tensor_tensor(out=t2[:, :, 0:32], in0=p3[:, :, 32:64],
                                                                     in1=sn2[:, sub, 0:32].unsqueeze(1).to_broadcast([128, 8, 32]), op=ALU.mult),
                                     reads=[pk, 'sn2'], writes=[t3k])
                                S.op('dve', lambda: V.tensor_tensor(out=t2[:, :, 32:64], in0=p3[:, :, 0:32],
                                                                     in1=sn2[:, sub, 32:64].unsqueeze(1).to_broadcast([128, 8, 32]), op=ALU.mult),
                                     reads=[pk, 'sn2'], writes=[t3k])
                                qro = qrk[:, blk * 512:(blk + 1) * 512].rearrange("p (m h d) -> p h m d", m=4, h=2)
                                S.op('dve', lambda: V.tensor_tensor(out=qro, in0=t12[:].rearrange("p (h m d) -> p h m d", h=2, m=4),
                                                                     in1=t3[:].rearrange("p (h m d) -> p h m d", h=2, m=4), op=ALU.add),
                                     reads=[t12k, t3k], writes=['qrk'])
                            for (src, srck, dstT, dk) in ((qtk, 'qtk', qT, 'qT'), (qrk, 'qrk', qrT, 'qrT')):
                                for hb_ in range(2):
                                    ps, pk, _ = PSS('st')
                                    for c in range(4):
                                        cc = hb_ * 4 + c
                                        tr(ps[:, c * 128:(c + 1) * 128], src[:, cc * 128:(cc + 1) * 128], ident32[:], reads=[srck, 'ident32'], writes=[pk], inc=(c == 3))
                                    S.op('act' if hb_ == 0 else 'dve',
                                         (lambda: A.copy(out=dstT[:, hb_ * 4:hb_ * 4 + 4, sub * 128:(sub + 1) * 128], in_=ps[:].rearrange("p (a b) -> p a b", a=4))) if hb_ == 0 else
                                         (lambda: V.tensor_copy(out=dstT[:, hb_ * 4:hb_ * 4 + 4, sub * 128:(sub + 1) * 128], in_=ps[:].rearrange("p (a b) -> p a b", a=4))),
                                         reads=[pk], writes=[dk])
                        Wg_, Wgk_ = w_next()
                        for sub in range(2):
                            hc = slice(hoff + sub * 128, hoff + (sub + 1) * 128)
                            ps, pk, _ = PSS('st')
                            for k in range(8):
                                mm(ps[:, 0:48], htile[:, k, hc], Wg_[:, k, :], k == 0, k == 7, reads=Wgk_ + htk, writes=[pk])
                            S.op('act', lambda: A.activation(out=gtk[:], in_=ps[:, 0:48], func=AF.Sigmoid), reads=[pk], writes=['gtk'])
                            ps, pk, _ = PSS('misc')
                            tr(ps[0:48, 0:128], gtk[:], ident32[:], reads=['gtk', 'ident32'], writes=[pk])
                            S.op('dve', lambda: V.tensor_copy(out=gT[:, sub * 128:(sub + 1) * 128], in_=ps[0:48, 0:128]), reads=[pk], writes=['gT'])
                        ok = htk
                        for mg in range(2 if OST >= 4 else 0):
                            for mi in range(4):
                                m_ = 4 * mg + mi
                                po, pok, _ = PSS('acc')
                                recs = []
                                for half in range(2):
                                    kv = 2 * mg + half
                                    hs = slice(half * 64, half * 64 + 64)
                                    psm, psmk, _ = PSS('acc')
                                    softmax_block(hs, qT, m_, kv, None, None, None, [0], 'cmp', qt, po, pok, psm, psmk, 128)
                                    rec, reck = s32.get()
                                    S.op('dve', lambda: V.tensor_scalar(out=rec[:, 0:TQ], in0=psm[:, 0:TQ], scalar1=1e-30, scalar2=None, op0=ALU.max), reads=[psmk], writes=[reck])
                                    S.op('dve', lambda: V.reciprocal(out=rec[:, 0:TQ], in_=rec[:, 0:TQ]), reads=[reck], writes=[reck])
                                    Plast = pring.t[(pring.i - 1) % 6]
                                    Plk = ("pr_", (pring.i - 1) % 6)
                                    pg = pgA[half]
                                    if mi == 0:
                                        S.op('dve', lambda: V.tensor_tensor(out=pg[0:127, :], in0=Plast[0:127, :], in1=rec[0:127, 0:TQ], op=ALU.mult),
                                             reads=[Plk, reck], writes=[('pg', half)])
                                    else:
                                        t_, tk_ = s32.get()
                                        S.op('dve', lambda: V.tensor_tensor(out=t_[0:127, 0:TQ], in0=Plast[0:127, :], in1=rec[0:127, 0:TQ], op=ALU.mult),
                                             reads=[Plk, reck], writes=[tk_])
                                        S.op('dve', lambda: V.tensor_tensor(out=pg[0:127, :], in0=pg[0:127, :], in1=t_[0:127, 0:TQ], op=ALU.add),
                                             reads=[tk_, ('pg', half)], writes=[('pg', half)])
                                    recs.append((rec, reck))
                                pg_, pgk = gate_bc(m_, 0)
                                for half in range(2):
                                    hs = slice(half * 64, half * 64 + 64)
                                    rec, reck = recs[half]
                                    S.op('dve', lambda: V.tensor_tensor(out=rec[hs, 0:TQ], in0=rec[hs, 0:TQ], in1=pg_[hs, 0:TQ], op=ALU.mult), reads=[reck, pgk], writes=[reck])
                                    S.op('dve', lambda: V.tensor_tensor(out=oT[hs, m_, :], in0=po[hs, 0:TQ], in1=rec[hs, 0:TQ], op=ALU.mult),
                                         reads=[pok, reck], writes=[ok[m_]])
                            for half in range(2):
                                kv = 2 * mg + half
                                pg = pgA[half]
                                for sub in range(2):
                                    ps, pk, pb = PSS('misc')
                                    mm(ps[:, 0:32], pg[0:127, sub * 128:(sub + 1) * 128], ov32[0:127, :], True, True, reads=[('pg', half), 'ov32'], writes=[pk])
                                    S.op('dve', lambda: V.tensor_tensor(out=scr[:], in0=ps[:, 0:32], in1=tA[:, sub, :], op=ALU.mult), reads=[pk, 'tA'], writes=['scr'])
                                    S.op('dve', lambda: V.tensor_tensor(out=scr[:], in0=scr[:], in1=tB[:, sub, :], op=ALU.add), reads=['scr', 'tB'], writes=['scr'])
                                    S.op('dve', lambda: V.max(out=m8[:], in_=scr[:]), reads=['scr'], writes=['m8'])
                                    S.op('dve', lambda: V.tensor_scalar(out=selt[:], in0=scr[:], scalar1=m8[:, 7:8], scalar2=None, op0=ALU.is_ge), reads=['scr', 'm8'], writes=['selt'])
                                    S.op('dve', lambda: V.tensor_scalar(out=selb[:], in0=selt[:], scalar1=-1.0, scalar2=30000.0, op0=ALU.add, op1=ALU.mult), reads=['selt'], writes=['selb'])
                                    ps, pk, _ = PSS('misc')
                                    tr(ps[0:32, 0:128], selb[:], ident32[:], reads=['selb', 'ident32'], writes=[pk])
                                    S.op('act', lambda: A.copy(out=selbT[:, kv, sub * 128:(sub + 1) * 128], in_=ps[0:32, 0:128]), reads=[pk], writes=[('selbT', kv)])
                        for m_ in range(8 if OST >= 5 else 0):
                            for br, kind in ((1, 'sel'), (2, 'win')):
                                po, pok, _ = PSS('acc')
                                psm, psmk, _ = PSS('acc')
                                if kind == 'sel':
                                    kts = list(range(0, 2 * qt + 2))
                                    KT, VB = skT, svb
                                else:
                                    kts = list(range(max(0, 2 * qt - 4), 2 * qt + 2))
                                    KT, VB = wkT, wvb
                                for half in range(2):
                                    kv = 2 * (m_ // 4) + half
                                    hs = slice(half * 64, half * 64 + 64)
                                    softmax_block(hs, qrT, m_, kv, KT, VB, None, kts, kind, qt, po, pok, psm, psmk, 64)
                                pg_, pgk = gate_bc(m_, br)
                                rec, reck = s32.get()
                                S.op('dve', lambda: V.tensor_scalar(out=rec[:, 0:TQ], in0=psm[:, 0:TQ], scalar1=1e-30, scalar2=None, op0=ALU.max), reads=[psmk], writes=[reck])
                                S.op('dve', lambda: V.reciprocal(out=rec[:, 0:TQ], in_=rec[:, 0:TQ]), reads=[reck], writes=[reck])
                                S.op('dve', lambda: V.tensor_tensor(out=rec[:, 0:TQ], in0=rec[:, 0:TQ], in1=pg_[:, 0:TQ], op=ALU.mult), reads=[reck, pgk], writes=[reck])
                                if kind == 'sel':
                                    S.op('dve', lambda: V.tensor_tensor(out=acc[:], in0=po[:, 0:TQ], in1=rec[:, 0:TQ], op=ALU.mult), reads=[pok, reck], writes=['acc'])
                                else:
                                    S.op('dve', lambda: V.tensor_tensor(out=rec[:, 0:TQ], in0=po[:, 0:TQ], in1=rec[:, 0:TQ], op=ALU.mult), reads=[pok, reck], writes=[reck])
                                    S.op('dve', lambda: V.tensor_tensor(out=acc[:], in0=acc[:], in1=rec[:, 0:TQ], op=ALU.add), reads=['acc', reck], writes=['acc'])
                            S.op('dve', lambda: V.tensor_tensor(out=oT[:, m_, :], in0=oT[:, m_, :], in1=acc[:], op=ALU.add), reads=['acc', ok[m_]], writes=[ok[m_]])
                        if OST < 5:
                            continue
                        Wo1, Wo1k = w_next()
                        Wo2, Wo2k = w_next()
                        for i in range(8):
                            ps, pk, _ = PSS('st')
                            for m_ in range(8):
                                Wo, Wok = (Wo1, Wo1k) if m_ < 4 else (Wo2, Wo2k)
                                mm(ps[:, 0:TQ], Wo[:, m_ % 4, i * 128:(i + 1) * 128], oT[:, m_, :], m_ == 0, m_ == 7, reads=Wok + [ok[m_]], writes=[pk])
                            S.op('dve', lambda: V.scalar_tensor_tensor(out=xT[:, i, t0:t0 + TQ], in0=ps[:, 0:TQ], scalar=m[:, g1 + i, 0:1],
                                                                        in1=xT[:, i, t0:t0 + TQ], op0=ALU.mult, op1=ALU.add),
                                 reads=[pk, ('mod', l % 2)] + xkeys[tt], writes=[xkeys[tt][i // 4]])
                if OST >= 6:
                  with phase() as sp_:
                    hflat32 = hflat.bitcast(F32)
                    stg = [hflat32[:, i * 4096:(i + 1) * 4096].rearrange("p (j f) -> p j f", j=16) for i in range(2)]
                    stgk = [[('stg', i, j) for j in range(16)] for i in range(2)]
                    S.op('dve', lambda: V.memset(hflat[:, 16384:16392], 0.0), writes=allh + arena + stgk[0] + stgk[1])
                    qsT = sb("qsT", [128, 8, NS], BF16, sp_)
                    qrsT = sb("qrsT", [128, 8, NS], BF16, sp_)
                    knT = sb("knT", [128, 4, NS], BF16, sp_)
                    vnb = sb("vnb", [NS, 512], BF16, sp_)
                    gs_t = sb("gs_t", [NS, 48], F32, sp_)
                    gperm = sb("gperm", [NS, 3, 16], F32, sp_)
                    pgall = sb("pgall", [128, 4, NS], F32, sp_)
                    oTb = [sb("oTb%d" % i, [128, 8, NS], F32, sp_) for i in range(3)]
                    kcs = sb("kcs", [128, 2, 128], BF16, sp_)
                    vcs = sb("vcs", [128, 4, 64], BF16, sp_)
                    selbTs = sb("selbTs", [33, 4, NS], BF16, sp_)
                    idx = sb("idx", [128, 256], I32, sp_)
                    idxf = sb("idxf", [128, 256], F32, sp_)
                    poff = sb("poff", [128, 2], F32, sp_)
                    nb16 = sb("nb16", [16, 16], BF16, sp_)
                    Emat_s = sb("Emat_s", [32, 16, 128], BF16, sp_)
                    ov33 = sb("ov33", [128, 33], F32, sp_)
                    sAs = sb("sAs", [NS, 33], F32, sp_)
                    sBs = sb("sBs", [NS, 33], F32, sp_)
                    XTs = sb("XTs", [128, 2, SEQ], BF16, sp_)
                    Pc = sb("Pc", [128, 144], BF16, sp_)
                    recs = sb("recs", [128, 8], F32, sp_)
                    pns = sb("pns", [128, 8], F32, sp_)
                    S.dma(nb16[:], nb16_d, writes=['nb16'], q='pool')
                    S.dma(Emat_s[:], Emat_d, writes=['Emat_s'], q='pool')
                    S.dma(ov33[0:127, :], ov33_d, writes=['ov33'])
                    S.dma(sAs[:], selA_s.to_broadcast([NS, 33]), writes=['sAs'])
                    S.dma(sBs[:], selB_s.to_broadcast([NS, 33]), writes=['sBs'])
                    S.dma(poff[:], poff_d, writes=['poff'])
                    S.dma(idx[:], page_table.rearrange("b j -> (b j)").unsqueeze(0).to_broadcast([128, 256]), writes=['idx'])
                    S.op('dve', lambda: V.tensor_copy(out=idxf[:], in_=idx[:]), reads=['idx'], writes=['idxf'])
                    S.op('dve', lambda: V.tensor_scalar(out=idxf[:], in0=idxf[:], scalar1=128.0, scalar2=poff[:, o:o + 1], op0=ALU.mult, op1=ALU.add),
                         reads=['idxf', 'poff'], writes=['idxf'])
                    S.op('dve', lambda: V.tensor_copy(out=idx[:], in_=idxf[:]), reads=['idxf'], writes=['idx'])

                    def gather(cache, b, dst, dkeys):
                        for j in range(16):
                            S.dma_ind(dst[:, j, :], cache, idx[:, b * 16 + j:b * 16 + j + 1], reads=['idx'], writes=[dkeys[j]])

                    with phase() as s1:
                        zs = sb("zs", [NS, 2608], F32, s1)
                        qr = sb("qr", [NS, 1024], F32, s1)
                        tm = sb("tm", [NS, 1024], F32, s1)
                        qp = tm
                        c2s = sb("c2s", [NS, 64], F32, s1)
                        s2s = sb("s2s", [NS, 64], F32, s1)
                        S.dma(c2s[:], rope_cos[SEQ:SEQ + 1, :].to_broadcast([NS, 64]), writes=['c2s'])
                        S.dma(s2s[:], rope_sin[SEQ:SEQ + 1, :].to_broadcast([NS, 64]), writes=['s2s'])
                        for (c0, n) in SBLK:
                            W, Wk = w_next()
                            ps, pk = PS()
                            for k in range(8):
                                mm(ps[0:NS, 0:n], hs_t[:, k, :], W[:, k, 0:n], k == 0, k == 7, reads=Wk + ['hs_t'], writes=[pk])
                            S.op('act', lambda: A.copy(out=zs[:, c0:c0 + n], in_=ps[0:NS, 0:n]), reads=[pk], writes=['zs'])

                        def rope_tok(dst, src, nh, key_w):
                            s3 = src.rearrange("p (h d) -> p h d", h=nh)
                            d3 = dst.rearrange("p (h d) -> p h d", h=nh)
                            t3 = tm[:, 0:nh * 64].rearrange("p (h d) -> p h d", h=nh)
                            S.op('dve', lambda: V.tensor_tensor(out=t3[:, :, 0:32], in0=s3[:, :, 32:64], in1=s2s[:, 0:32].unsqueeze(1).to_broadcast([NS, nh, 32]), op=ALU.mult),
                                 reads=['zs', 's2s'], writes=['tm'])
                            S.op('dve', lambda: V.tensor_tensor(out=t3[:, :, 32:64], in0=s3[:, :, 0:32], in1=s2s[:, 32:64].unsqueeze(1).to_broadcast([NS, nh, 32]), op=ALU.mult),
                                 reads=['zs', 's2s'], writes=['tm'])
                            S.op('dve', lambda: V.tensor_tensor(out=d3, in0=s3, in1=c2s[:].unsqueeze(1).to_broadcast([NS, nh, 64]), op=ALU.mult),
                                 reads=['zs', 'c2s'], writes=[key_w])
                            S.op('dve', lambda: V.tensor_tensor(out=d3, in0=d3, in1=t3, op=ALU.add), reads=[key_w, 'tm'], writes=[key_w])

                        rope_tok(qr[:], zs[:, 0:1024], 16, 'qr')
                        rope_tok(zs[:, 1536:1792], zs[:, 1536:1792], 4, 'zs')
                        rope_tok(zs[:, 2048:2304], zs[:, 2048:2304], 4, 'zs')
                        S.dma(nsa_s[0][o], zs[:, 1024:1280], reads=['zs'])
                        S.dma(nsa_s[1][o], zs[:, 1280:1536], reads=['zs'])
                        S.dma(nsa_s[2][o], zs[:, 1536:1792], reads=['zs'])
                        S.dma(nsa_s[3][o], zs[:, 1792:2048], reads=['zs'])
                        for X in range(2):
                            S.dma(win_s[X][o, :, 0:511, :], state_win[X][o, :, 1:512, :])
                            S.dma(win_s[X][o, :, 511, :], zs[:, 2048 + 256 * X:2304 + 256 * X], reads=['zs'])
                        S.op('act', lambda: A.activation(out=gs_t[:], in_=zs[:, 2560:2608], func=AF.Sigmoid), reads=['zs'], writes=['gs_t'])
                        for br in range(3):
                            gsrc = gs_t[:].rearrange("p (g h m c) -> p g h m c", g=2, h=2, m=4)[:, :, :, :, br]
                            for g_ in range(2):
                                S.op('dve', lambda: V.tensor_copy(out=gperm[:, br, g_ * 8:(g_ + 1) * 8].rearrange("p (m h) -> p h m", m=4, h=2), in_=gsrc[:, g_]),
                                     reads=['gs_t'], writes=['gperm'])
                        S.op('act', lambda: A.copy(out=vnb[:, 0:256], in_=zs[:, 1792:2048]), reads=['zs'], writes=['vnb'])
                        S.op('act', lambda: A.copy(out=vnb[:, 256:512], in_=zs[:, 2304:2560]), reads=['zs'], writes=['vnb'])
                        for (src, srck, dstT, dk) in ((zs, 'zs', qsT, 'qsT'), (qr, 'qr', qrsT, 'qrsT')):
                            for g_ in range(2):
                                S.op('dve', lambda: V.tensor_copy(out=qp[:, g_ * 512:(g_ + 1) * 512].rearrange("p (m h d) -> p h m d", m=4, h=2),
                                                                   in_=src[:, g_ * 512:(g_ + 1) * 512].rearrange("p (h m d) -> p h m d", h=2, m=4)),
                                     reads=[srck], writes=['tm'])
                            ps, pk = PS()
                            for c in range(8):
                                tr(ps[:, c * NS:(c + 1) * NS], qp[:, c * 128:(c + 1) * 128], ident32[0:NS, 0:NS], reads=['tm', 'ident32'], writes=[pk], inc=(c == 7))
                            S.op('act', lambda: A.copy(out=dstT[:], in_=ps[:, 0:8 * NS].rearrange("p (a b) -> p a b", a=8)), reads=[pk], writes=[dk])
                        ps, pk = PS()
                        for ci, c0 in enumerate((1536, 1664, 2048, 2176)):
                            tr(ps[:, ci * NS:(ci + 1) * NS], zs[:, c0:c0 + 128], ident32[0:NS, 0:NS], reads=['zs', 'ident32'], writes=[pk], inc=(ci == 3))
                        S.op('act', lambda: A.copy(out=knT[:], in_=ps[:, 0:4 * NS].rearrange("p (a b) -> p a b", a=4)), reads=[pk], writes=['knT'])

                    S.serial_compute = int(os.environ.get("K_SSER", 0))

                    def transposeX(st, stks, ntile, dstT, dkey):
                        for j0 in range(0, ntile, 2):
                            ps, pk = PS()
                            for jj in range(2):
                                for c in range(2):
                                    tr(ps[:, (jj * 2 + c) * 128:(jj * 2 + c + 1) * 128], st[:, j0 + jj, c * 128:(c + 1) * 128], ident32[:],
                                       reads=[stks[j0 + jj], 'ident32'], writes=[pk], inc=(jj == 1 and c == 1))
                            S.op('act' if (j0 // 2) % 2 == 0 else 'dve',
                                 (lambda: A.copy(out=dstT[:, :, j0 * 128:(j0 + 2) * 128].rearrange("p c (jj t) -> p c jj t", jj=2),
                                                 in_=ps[:].rearrange("p (jj c t) -> p c jj t", jj=2, c=2))) if (j0 // 2) % 2 == 0 else
                                 (lambda: V.tensor_copy(out=dstT[:, :, j0 * 128:(j0 + 2) * 128].rearrange("p c (jj t) -> p c jj t", jj=2),
                                                        in_=ps[:].rearrange("p (jj c t) -> p c jj t", jj=2, c=2))),
                                 reads=[pk], writes=[dkey])

                    with phase() as p1:
                      if SST >= 2:
                        w1ts = [sb("w1ts%d" % X, [128, 32, 128], BF16, p1) for X in range(2)]
                        w2ts = [sb("w2ts%d" % X, [128, 64], BF16, p1) for X in range(2)]
                        peTs = sb("peTs", [128, 32], BF16, p1)
                        hbs = [sb("hbs%d" % X, [128, 1], F32, p1) for X in range(2)]
                        for X in range(2):
                            for half in range(2):
                                S.dma(w1ts[X][half * 64:(half + 1) * 64, :, :], cmp_w1[X][o].rearrange("l d e -> d l e"), writes=[('w1ts', X, half)], q='pool')
                            S.dma(peTs[0:64, :], cmp_pe[X][o].rearrange("l d -> d l"), writes=['peTs'], q='pool', allow_slow_non_contiguous=True)
                            S.dma(w2ts[X][:], cmp_w2[X][o], writes=[('w2ts', X)], q='pool')
                            ps, pk = PS()
                            for li in range(32):
                                mm(ps[:, 0:1], w1ts[X][0:64, li, :], peTs[0:64, li:li + 1], li == 0, li == 31, reads=[('w1ts', X, 0), 'peTs'], writes=[pk])
                            S.op('dve', lambda: V.tensor_copy(out=hbs[X][:], in_=ps[:, 0:1]), reads=[pk], writes=[('hbs', X)])
                        for b in range(NS):
                            for X in range(2):
                                si = (2 * b + X) % 2
                                gather(caches[X], b, stg[si], stgk[si])
                                if P1 >= 2:
                                    transposeX(stg[si], stgk[si], 16, XTs, 'XTs')
                                for kv in range(4 if P1 >= 3 else 0):
                                    half = kv % 2
                                    hs = slice(half * 64, half * 64 + 64)
                                    X3 = XTs[hs, kv // 2, :].rearrange("p (c s) -> p c s", s=16)
                                    ps, pk = PS()
                                    for li in range(32):
                                        mm(ps[:, 0:127], w1ts[X][hs, li, :], X3[:, li // 16:li // 16 + 127, li % 16], li == 0, li == 31,
                                           reads=[('w1ts', X, half), 'XTs'], writes=[pk])
                                    hg, hgk = s16.get()
                                    S.op('act', lambda: A.activation(out=hg[:, 0:127], in_=ps[:, 0:127], func=AF.Gelu_apprx_tanh, bias=hbs[X][:, 0:1], scale=1.0),
                                         reads=[pk, ('hbs', X)], writes=[hgk])
                                    ps2, pk2 = PS()
                                    if X == 0:
                                        mm(ps2[hs, 0:127], w2ts[0][:, 0:64], hg[:, 0:127], True, True, reads=[('w2ts', 0), hgk], writes=[pk2])
                                        S.op('dve', lambda: V.tensor_copy(out=kcs[hs, kv // 2, 0:127], in_=ps2[hs, 0:127]), reads=[pk2], writes=['kcs'])
                                    else:
                                        mm(ps2[0:127, 0:64], hg[:, 0:127], w2ts[1][:, 0:64], True, True, reads=[('w2ts', 1), hgk], writes=[pk2])
                                        S.op('dve', lambda: V.tensor_copy(out=vcs[0:127, kv, :], in_=ps2[0:127, 0:64]), reads=[pk2], writes=['vcs'])
                            for mg in range(2 if P1 >= 4 else 0):
                                for half in range(2):
                                    hs = slice(half * 64, half * 64 + 64)
                                    ps_s, psk = PS()
                                    mm(ps_s[0:127, 0:4], kcs[hs, mg, 0:127], qsT[hs, 4 * mg:4 * mg + 4, b], True, True,
                                       reads=['kcs', 'qsT'], writes=[psk])
                                    S.op('act', lambda: A.activation(out=Pc[0:127, half * 4:half * 4 + 4], in_=ps_s[0:127, 0:4], func=AF.Exp, scale=SC), reads=[psk], writes=['Pc'])
                                ps_m, pmk = PS()
                                mm(ps_m[:, 0:8], onesb[0:127, :], Pc[0:127, 0:8], True, True, reads=['onesb', 'Pc'], writes=[pmk])
                                S.op('dve', lambda: V.reciprocal(out=recs[:], in_=ps_m[:, 0:8]), reads=[pmk], writes=['recs'])
                                S.op('dve', lambda: V.tensor_tensor(out=pns[0:127, :], in0=Pc[0:127, 0:8], in1=recs[0:127, :], op=ALU.mult), reads=['Pc', 'recs'], writes=['pns'])
                                S.op('dve', lambda: V.tensor_reduce(out=pgall[0:127, 2 * mg:2 * mg + 2, b], in_=pns[0:127, :].rearrange("p (h j) -> p h j", h=2),
                                                                     axis=AX.X, op=ALU.add), reads=['pns'], writes=['pgall'])
                                ps_o, pok = PS()
                                for half in range(2):
                                    hs = slice(half * 64, half * 64 + 64)
                                    mm(ps_o[hs, 0:4], vcs[0:127, 2 * mg + half, :], Pc[0:127, half * 4:half * 4 + 4], True, True,
                                       reads=['vcs', 'Pc'], writes=[pok], inc=(half == 1))
                                for half in range(2):
                                    hs = slice(half * 64, half * 64 + 64)
                                    S.op('dve', lambda: V.tensor_tensor(out=oTb[0][hs, 4 * mg:4 * mg + 4, b], in0=ps_o[hs, 0:4], in1=recs[hs, half * 4:half * 4 + 4], op=ALU.mult),
                                         reads=[pok, 'recs'], writes=['oTb0'])
                    scs = sb("scs", [NS, 33], F32, sp_)
                    for kv in range(4 if SST >= 3 else 0):
                        ps, pk = PS()
                        mm(ps[0:NS, 0:33], pgall[0:127, kv, :], ov33[0:127, :], True, True, reads=['pgall', 'ov33'], writes=[pk])
                        S.op('dve', lambda: V.tensor_tensor(out=scs[:], in0=ps[0:NS, 0:33], in1=sAs[:], op=ALU.mult), reads=[pk, 'sAs'], writes=['scs'])
                        S.op('dve', lambda: V.tensor_tensor(out=scs[:], in0=scs[:], in1=sBs[:], op=ALU.add), reads=['scs', 'sBs'], writes=['scs'])
                        S.op('dve', lambda: V.max(out=recs[0:NS, 0:8], in_=scs[:]), reads=['scs'], writes=['recs'])
                        S.op('dve', lambda: V.tensor_scalar(out=scs[:], in0=scs[:], scalar1=recs[0:NS, 7:8], scalar2=None, op0=ALU.is_ge), reads=['scs', 'recs'], writes=['scs'])
                        ps, pk = PS()
                        tr(ps[0:33, 0:NS], scs[:], ident32[0:NS, 0:NS], reads=['scs', 'ident32'], writes=[pk])
                        S.op('act', lambda: A.copy(out=selbTs[:, kv, :], in_=ps[0:33, 0:NS]), reads=[pk], writes=['selbTs'])
                    with phase() as p2:
                        svs = sb("svs", [128, 16, 256], BF16, p2)
                        svk = [('svs', j) for j in range(16)]
                        wst = sb("wst", [128, 4, 256], F32, p2)
                        wvs = sb("wvs", [128, 4, 256], BF16, p2)
                        wkTs = sb("wkTs", [128, 2, 512], BF16, p2)
                        for b in range(NS if SST >= 4 else 0):
                            si = b % 2
                            gather(caches[2], b, stg[si], stgk[si])
                            gather(caches[3], b, svs, svk)
                            S.dma(wst[:], state_win[0][o, b].rearrange("(t p) f -> p t f", p=128), writes=['wst'])
                            S.dma(wvs[:], state_win[1][o, b].rearrange("(t p) f -> p t f", p=128), writes=['wvs'], q='pool')
                            transposeX(stg[si], stgk[si], 16, XTs, 'XTs')
                            transposeX(wst, ['wst'] * 4, 4, wkTs, 'wkTs')
                            for mg in range(2):
                                for br, KTs, ktk, VBs, vks, nkt, knc, vnc in ((1, XTs, 'XTs', svs, svk, 16, 0, 0), (2, wkTs, 'wkTs', wvs, ['wvs'] * 4, 4, 2, 256)):
                                    c0 = nkt * 4
                                    for half in range(2):
                                        hs = slice(half * 64, half * 64 + 64)
                                        kv = 2 * mg + half
                                        pcb = half * 72
                                        ps_s, psk = PS()
                                        for kt in range(nkt):
                                            mm(ps_s[:, kt * 4:kt * 4 + 4], KTs[hs, mg, kt * 128:(kt + 1) * 128], qrsT[hs, 4 * mg:4 * mg + 4, b], True, True,
                                               reads=[ktk, 'qrsT'], writes=[psk], inc=False)
                                        mm(ps_s[0:NS, c0:c0 + 4], knT[hs, knc + mg, :], qrsT[hs, 4 * mg:4 * mg + 4, b], True, True, reads=['knT', 'qrsT'], writes=[psk])
                                        S.op('act', lambda: A.activation(out=Pc[:, pcb:pcb + c0], in_=ps_s[:, 0:c0], func=AF.Exp, scale=SC), reads=[psk], writes=['Pc'])
                                        S.op('act', lambda: A.activation(out=Pc[0:NS, pcb + c0:pcb + c0 + 4], in_=ps_s[0:NS, c0:c0 + 4], func=AF.Exp, scale=SC), reads=[psk], writes=['Pc'])
                                        S.op('dve', lambda: V.tensor_scalar(out=Pc[0:NS, pcb + c0:pcb + c0 + 4], in0=Pc[0:NS, pcb + c0:pcb + c0 + 4],
                                                                             scalar1=ident32[0:NS, b:b + 1], scalar2=None, op0=ALU.mult), reads=['Pc', 'ident32'], writes=['Pc'])
                                        if br == 1:
                                            ps_k, pkk = PS()
                                            for kt in range(nkt):
                                                mm(ps_k[:, kt:kt + 1], Emat_s[0:32, kt, :], selbTs[0:32, kv, b:b + 1], True, True, reads=['Emat_s', 'selbTs'], writes=[pkk], inc=(kt == nkt - 1))
                                            S.op('dve', lambda: V.tensor_tensor(out=Pc[:, pcb:pcb + c0].rearrange("p (k h) -> p k h", h=4),
                                                                                 in0=Pc[:, pcb:pcb + c0].rearrange("p (k h) -> p k h", h=4),
                                                                                 in1=ps_k[:, 0:nkt].unsqueeze(2).to_broadcast([128, nkt, 4]), op=ALU.mult),
                                                 reads=['Pc', pkk], writes=['Pc'])
                                    ps_o, pok = PS()
                                    ps_m, pmk = PS()
                                    for half in range(2):
                                        hs = slice(half * 64, half * 64 + 64)
                                        kv = 2 * mg + half
                                        pcb = half * 72
                                        for kt in range(nkt):
                                            cols = slice(pcb + kt * 4, pcb + kt * 4 + 4)
                                            mm(ps_o[hs, 0:4], VBs[:, kt, kv * 64:(kv + 1) * 64], Pc[:, cols], kt == 0, False, reads=[vks[kt], 'Pc'], writes=[pok])
                                            mm(ps_m[hs, 0:4], onesb[:, 0:64], Pc[:, cols], kt == 0, False, reads=['onesb', 'Pc'], writes=[pmk])
                                        cols = slice(pcb + c0, pcb + c0 + 4)
                                        mm(ps_o[hs, 0:4], vnb[0:NS, vnc + kv * 64:vnc + (kv + 1) * 64], Pc[0:NS, cols], False, True, reads=['vnb', 'Pc'], writes=[pok])
                                        mm(ps_m[hs, 0:4], onesb[0:NS, 0:64], Pc[0:NS, cols], False, True, reads=['onesb', 'Pc'], writes=[pmk])
                                    S.op('dve', lambda: V.reciprocal(out=recs[:, 0:4], in_=ps_m[:, 0:4]), reads=[pmk], writes=['recs'])
                                    S.op('dve', lambda: V.tensor_tensor(out=oTb[br][:, 4 * mg:4 * mg + 4, b], in0=ps_o[:, 0:4], in1=recs[:, 0:4], op=ALU.mult),
                                         reads=[pok, 'recs'], writes=['oTb%d' % br])
                    with phase() as p3:
                      if SST >= 5:
                        otk = sb("otk", [NS, 1024], F32, p3)
                        osum = sb("osum", [NS, 1024], F32, p3)
                        oTs = sb("oTs", [128, 8, NS], BF16, p3)
                        for br in range(3):
                            for half_ in range(2):
                                ps, pk = PS()
                                for c in range(4):
                                    cc = half_ * 4 + c
                                    tr(ps[0:NS, c * 128:(c + 1) * 128], oTb[br][:, cc, :], ident32[:], reads=['oTb%d' % br, 'ident32'], writes=[pk], inc=(c == 3))
                                S.op('act', lambda: A.copy(out=otk[:, half_ * 512:(half_ + 1) * 512], in_=ps[0:NS, :]), reads=[pk], writes=['otk'])
                            gb = gperm[:, br, :].unsqueeze(2).to_broadcast([NS, 16, 64])
                            o3 = otk[:].rearrange("p (h d) -> p h d", h=16)
                            if br == 0:
                                S.op('dve', lambda: V.tensor_tensor(out=osum[:].rearrange("p (h d) -> p h d", h=16), in0=o3, in1=gb, op=ALU.mult), reads=['otk', 'gperm'], writes=['osum'])
                            else:
                                S.op('dve', lambda: V.tensor_tensor(out=o3, in0=o3, in1=gb, op=ALU.mult), reads=['otk', 'gperm'], writes=['otk'])
                                S.op('dve', lambda: V.tensor_tensor(out=osum[:], in0=osum[:], in1=otk[:], op=ALU.add), reads=['otk', 'osum'], writes=['osum'])
                        ps, pk = PS()
                        for c in range(8):
                            tr(ps[:, c * NS:(c + 1) * NS], osum[:, c * 128:(c + 1) * 128], ident32[0:NS, 0:NS], reads=['osum', 'ident32'], writes=[pk], inc=(c == 7))
                        S.op('act', lambda: A.copy(out=oTs[:], in_=ps[:, 0:8 * NS].rearrange("p (a b) -> p a b", a=8)), reads=[pk], writes=['oTs'])
                        Wo1, Wo1k = w_next()
                        Wo2, Wo2k = w_next()
                        ps, pk = PS()
                        for i in range(8):
                            for m_ in range(8):
                                Wo, Wok = (Wo1, Wo1k) if m_ < 4 else (Wo2, Wo2k)
                                mm(ps[:, i * NS:(i + 1) * NS], Wo[:, m_ % 4, i * 128:(i + 1) * 128], oTs[:, m_, :], m_ == 0, m_ == 7, reads=Wok + ['oTs'], writes=[pk])
                        t, tk = s32.get()
                        tv = t[:, 0:8 * NS].rearrange("p (a b) -> p a b", a=8)
                        S.op('dve', lambda: V.tensor_tensor(out=tv, in0=ps[:, 0:8 * NS].rearrange("p (a b) -> p a b", a=8), in1=m[:, g1:g1 + 8, 1:17], op=ALU.mult),
                             reads=[pk, ('mod', l % 2)], writes=[tk])
                        S.op('dve', lambda: V.tensor_tensor(out=xsT[:], in0=xsT[:], in1=tv, op=ALU.add), reads=[tk, 'xs'], writes=['xs'])
                S.serial_compute = False
                S.op('dve', lambda: V.memset(hflat[:, 16384:16392], 0.0), writes=allh + arena + [('stg', i, j) for i in range(2) for j in range(16)])

        def final():
            with phase() as ph:
                yt = [sb("yt%d" % i, [128, D], F32, ph) for i in range(2)]
                yf = sb("yf", [128, 8, TT], F32, ph)
                for tt in range(NT):
                    cols = slice(tt * TT, (tt + 1) * TT)
                    ps, pk = PS()
                    for c in range(8):
                        sq, sqk = s16.get()
                        S.op('act', lambda: A.activation(out=sq[:], in_=xT[:, c, cols], func=AF.Square), reads=xkeys[tt], writes=[sqk])
                        mm(ps[:], onesb[:], sq[:], c == 0, c == 7, reads=[sqk, 'onesb'], writes=[pk], inc=True)
                    rs, rsk = rstd_t, 'rstd_t'
                    S.op('act', lambda: A.activation(out=rs[:], in_=ps[:], func=AF.Sqrt, bias=eps_t[:, 0:1], scale=1.0 / D), reads=[pk, 'eps'], writes=[rsk])
                    S.op('dve', lambda: V.reciprocal(out=rs[:], in_=rs[:]), reads=[rsk], writes=[rsk])
                    for c in range(8):
                        S.op('dve', lambda: V.scalar_tensor_tensor(out=yf[:, c, :], in0=xT[:, c, cols], scalar=pfin[:, c:c + 1], in1=rs[:],
                                                                    op0=ALU.mult, op1=ALU.mult), reads=xkeys[tt] + [rsk, 'pfin'], writes=[('yf', c)])
                    for sub in range(4):
                        y = yt[sub % 2]
                        yk = ('yt', sub % 2)
                        for half in range(2):
                            ps, pk = PS()
                            for q in range(4):
                                c = half * 4 + q
                                tr(ps[:, q * 128:(q + 1) * 128], yf[:, c, sub * 128:(sub + 1) * 128], ident32[:],
                                   reads=[('yf', c), 'ident32'], writes=[pk], inc=(q == 3))
                            if half == 0:
                                S.op('act', lambda: A.copy(out=y[:, 0:512], in_=ps[:]), reads=[pk], writes=[yk])
                            else:
                                S.op('dve', lambda: V.tensor_copy(out=y[:, 512:1024], in_=ps[:]), reads=[pk], writes=[yk])
                        r0 = tt * TT + sub * 128
                        S.dma(y_prompt[r0:r0 + 128, :], y[:], reads=[yk])
                ps, pk = PS()
                sq, sqk = s16.get()
                S.op('act', lambda: A.activation(out=sq[:, 0:8 * NS], in_=xsT[:].rearrange("p a b -> p (a b)"), func=AF.Square), reads=['xs'], writes=[sqk])
                for c in range(8):
                    mm(ps[:, 0:NS], onesb[:], sq[:, c * NS:(c + 1) * NS], c == 0, c == 7, reads=[sqk, 'onesb'], writes=[pk])
                rs, rsk = s32.get()
                S.op('act', lambda: A.activation(out=rs[:, 0:NS], in_=ps[:, 0:NS], func=AF.Sqrt, bias=eps_t[:, 0:1], scale=1.0 / D), reads=[pk, 'eps'], writes=[rsk])
                S.op('dve', lambda: V.reciprocal(out=rs[:, 0:NS], in_=rs[:, 0:NS]), reads=[rsk], writes=[rsk])
                t, tk = s32.get()
                tv = t[:, 0:8 * NS].rearrange("p (a b) -> p a b", a=8)
                S.op('dve', lambda: V.tensor_tensor(out=tv, in0=xsT[:], in1=rs[:, 0:NS].unsqueeze(1).to_broadcast([128, 8, NS]), op=ALU.mult), reads=['xs', rsk], writes=[tk])
                S.op('dve', lambda: V.tensor_tensor(out=tv, in0=tv, in1=pfin[:].unsqueeze(2).to_broadcast([128, 8, NS]), op=ALU.mult), reads=[tk, 'pfin'], writes=[tk])
                ps, pk = PS()
                ps2, pk2 = PS()
                for c in range(8):
                    pp = ps if c < 4 else ps2
                    tr(pp[0:NS, (c % 4) * 128:(c % 4 + 1) * 128], t[:, c * NS:(c + 1) * NS], ident32[:], reads=[tk, 'ident32'],
                       writes=[pk if c < 4 else pk2], inc=(c % 4 == 3))
                y = yt[0]
                S.op('act', lambda: A.copy(out=y[0:NS, 0:512], in_=ps[0:NS, :]), reads=[pk], writes=[('yt', 0)])
                S.op('dve', lambda: V.tensor_copy(out=y[0:NS, 512:1024], in_=ps2[0:NS, :]), reads=[pk2], writes=[('yt', 0)])
                S.dma(y_sample, y[0:NS, :], reads=[('yt', 0)])

        for l in range(n_layers):
            ada_finish(l)
            if l % 2 == 0:
                norm_pass(l, 0)
                even_layer(l)
            elif do_odd:
                odd_layer(l)
            norm_pass(l, 1)
            mlp(l)
        final()
        assert wstate['used'] == len(wplan), (wstate, len(wplan))
        S.finish()
        print("instr counts", S.n_ins, "waits", S.n_wait, flush=True)
    return nc


_OUT_NAMES = ["y_prompt", "y_sample", "conv_p", "conv_s", "chunkv_p", "chunkv_s"]


def kernel(**inputs):
    f = lambda k: np.ascontiguousarray(np.asarray(inputs[k]))
    n_layers = int(os.environ.get("K_LAYERS", DEPTH))
    nc = build_program(n_layers=n_layers)
    p_layer = np.zeros((DEPTH, 128, 64), np.float32)
    p_layer[:, :, 0:48] = f("ada_b").reshape(DEPTH, 48, 128).transpose(0, 2, 1)
    p_layer[:, :, 48:56] = f("norm_mix_g").reshape(DEPTH, 8, 128).transpose(0, 2, 1)
    p_layer[:, :, 56:64] = f("norm_ffn_g").reshape(DEPTH, 8, 128).transpose(0, 2, 1)
    p_even = np.zeros((2, 128, 136), np.float32)
    p_even[:, :, 0:4] = f("conv_b").reshape(2, 4, 128).transpose(0, 2, 1)
    p_even[:, :, 4:8] = f("conv_ln_g").reshape(2, 4, 128).transpose(0, 2, 1)
    p_even[:, :, 8:12] = f("conv_ln_b").reshape(2, 4, 128).transpose(0, 2, 1)
    p_even[:, :, 12:136] = f("conv_w").reshape(2, 31, 4, 128).transpose(0, 3, 1, 2).reshape(2, 128, 124)
    p_final = np.ascontiguousarray(f("final_norm_g").reshape(8, 128).T)
    sgu_wT = np.ascontiguousarray(f("sgu_w").transpose(0, 3, 1, 2))
    cst = np.zeros((128, 256), np.float32)
    cst[:, 0:128] = np.eye(128, dtype=np.float32)
    cst[:, 128:256] = np.triu(np.ones((128, 128), np.float32))
    half = 32
    inv = np.power(np.float32(10000.0), -np.arange(half, dtype=np.float32) * np.float32(2.0 / 64))
    pos = np.arange(SEQ + 1, dtype=np.float32)
    ang = pos[:, None] * inv[None, :]
    cosv, sinv = np.cos(ang).astype(np.float32), np.sin(ang).astype(np.float32)
    rope_cos = np.concatenate([cosv, cosv], 1)
    rope_sin = np.concatenate([-sinv, sinv], 1)
    tpos = np.arange(SEQ)[:, None]
    blk = np.arange(32)[None, :]
    cur = tpos // 64
    valid = blk * 64 <= tpos
    forced = (blk == 0) | (blk == cur) | (blk == cur - 1)
    selA = (valid & ~forced).astype(np.float32)
    selB = np.where(valid, np.where(forced, 1e4, 0.0), -1e30).astype(np.float32)
    ncmp = np.arange(127)[:, None]
    cmpb = np.where(16 * ncmp + 31 <= np.arange(SEQ)[None, :], 0.0, -30000.0).astype(np.float32)
    key = np.arange(128)[:, None, None]
    dd = np.array([-4, -3, 0, 1])[None, :, None]
    tq = np.arange(256)[None, None, :]
    diff = tq - key - 128 * dd
    bandb = np.where((diff >= 0) & (diff <= 512), 0.0, -30000.0).astype(np.float32)
    jj = np.arange(32)[:, None, None]
    ktt = np.arange(16)[None, :, None]
    kk = np.arange(128)[None, None, :]
    Emat = (jj == 2 * ktt + kk // 64).astype(np.float32)
    ci = np.arange(127)[:, None] * 16
    sj = np.arange(32)[None, :] * 64
    ov = ((ci < sj + 64) & (ci + 32 > sj)).astype(np.float32)
    ci33 = np.arange(127)[:, None] * 16
    sj33 = np.arange(33)[None, :] * 64
    ov33 = ((ci33 < sj33 + 64) & (ci33 + 32 > sj33)).astype(np.float32)
    j33 = np.arange(33)
    forced33 = (j33 == 0) | (j33 == 32) | (j33 == 31)
    selA_s = (~forced33).astype(np.float32)[None, :]
    selB_s = np.where(forced33, 1e4, 0.0).astype(np.float32)[None, :]
    nb16 = np.where(np.eye(16, dtype=bool), 0.0, -30000.0).astype(np.float32)
    NPOOL = int(os.environ.get("K_POOLPAGES", 2560))
    poff = (np.arange(2)[None, :] * (NPOOL * 128) + np.arange(128)[:, None]).astype(np.float32)
    shared = dict(ada_w=f("ada_w"), ffn_w1=f("ffn_w1"), ffn_w2=f("ffn_w2"), even_w_in=f("even_w_in"),
                  even_w_out=f("even_w_out"), p_layer=p_layer, p_even=p_even, p_final=p_final,
                  conv_w=f("conv_w"), conv_b=f("conv_b"), conv_ln_g=f("conv_ln_g"), conv_ln_b=f("conv_ln_b"),
                  sgu_ln_g=f("sgu_ln_g"), sgu_ln_b=f("sgu_ln_b"), sgu_wT=sgu_wT, sgu_b=f("sgu_b"), cst=cst,
                  odd_w_in=f("odd_w_in"), odd_w_out=f("odd_w_out"),
                  cmp_pe_k=f("cmp_pe_k"), cmp_pe_v=f("cmp_pe_v"), cmp_w1_k=f("cmp_w1_k"), cmp_w1_v=f("cmp_w1_v"),
                  cmp_w2_k=f("cmp_w2_k"), cmp_w2_v=f("cmp_w2_v"),
                  rope_cos=rope_cos, rope_sin=rope_sin, selA=selA, selB=selB, cmpb=cmpb, bandb=bandb, Emat=Emat, ov=ov,
                  ov33=ov33, selA_s=selA_s, selB_s=selB_s, nb16=nb16, poff=poff,
                  cache_cmp_k=np.ascontiguousarray(f("cache_cmp_k")[:, :NPOOL]).reshape(-1, 256), cache_cmp_v=np.ascontiguousarray(f("cache_cmp_v")[:, :NPOOL]).reshape(-1, 256),
                  cache_sel_k=np.ascontiguousarray(f("cache_sel_k")[:, :NPOOL]).reshape(-1, 256), cache_sel_v=np.ascontiguousarray(f("cache_sel_v")[:, :NPOOL]).reshape(-1, 256))
    xp, xs = f("x_prompt"), f("x_sample")
    cp, cs = f("c_prompt"), f("c_sample")
    stc = f("state_conv")
    swk, swv, ptab = f("state_win_k"), f("state_win_v"), f("page_table").astype(np.int32)
    in_maps = []
    for i in range(NCORES):
        sl = slice(i * NS, (i + 1) * NS)
        m = dict(shared)
        m["x_prompt"] = xp[i]
        m["x_sample"] = xs[sl, 0, :]
        m["c_all"] = np.concatenate([cp[i:i + 1], cs[sl]], axis=0)
        m["state_conv"] = np.ascontiguousarray(stc[:, sl])
        m["state_win_k"] = np.ascontiguousarray(swk[:, sl]).reshape(2, NS, 512, 256)
        m["state_win_v"] = np.ascontiguousarray(swv[:, sl]).reshape(2, NS, 512, 256)
        m["page_table"] = np.ascontiguousarray(ptab[sl] % NPOOL) if NPOOL != 2560 else np.ascontiguousarray(ptab[sl])
        in_maps.append(m)
    res = run_bass_kernel_spmd(nc, in_maps, core_ids=list(range(NCORES)))
    if os.environ.get("K_PRINT_TIME"):
        print("EXEC_TIME_NS", getattr(res, "exec_time_ns", None), flush=True)
    R = res.results
    g = lambda name: [np.asarray(r[name]) for r in R]
    y_prompt = np.stack(g("y_prompt"), 0)
    y_sample = np.concatenate(g("y_sample"), 0)[:, None, :]
    conv_p = np.stack(g("conv_p"), 1)
    conv_s = np.concatenate(g("conv_s"), 1)
    chunkv_p = np.stack(g("chunkv_p"), 1)
    chunkv_s = np.concatenate(g("chunkv_s"), 1)[:, :, None, :]
    z = lambda *s: np.zeros(s, np.float32)
    pk = lambda name, T_: np.stack(g(name), 1).reshape(2, NCORES, T_, 4, 64)
    outs = [y_prompt, y_sample, conv_p, conv_s, chunkv_p, chunkv_s,
            pk("cmp_k_p", SEQ), pk("cmp_v_p", SEQ), pk("sel_k_p", SEQ), pk("sel_v_p", SEQ),
            pk("win_k_p", 512), pk("win_v_p", 512),
            *[np.concatenate(g(n), 1).reshape(2, NCORES * NS, 1, 4, 64) for n in ("cmp_k_s", "cmp_v_s", "sel_k_s", "sel_v_s")],
            *[np.concatenate(g(n), 1).reshape(2, NCORES * NS, 512, 4, 64) for n in ("win_k_s", "win_v_s")]]
    return tuple(np.ascontiguousarray(o, dtype=np.float32) for o in outs)
```
